# Optimizing a Trainium2 kernel written in Bass

```python
import math
import jax, jax.numpy as jnp
from jax import lax
import numpy as np

D_MODEL = 1024
BATCH = 16
SEQ = 256
DEPTH = 4
DEC_BATCH = 8
DEC_SEQ = 4096
PAST_LEN = 256

GRID_W = 64
MIX_W = D_MODEL
N_MIXERS = 4
GROUP_W = MIX_W // N_MIXERS
SSM_CH = 16
SSM_GROUPS = GROUP_W // SSM_CH
SSM_STATE = 64
FNET_HEADS = 4
FNET_CH = GROUP_W // FNET_HEADS
POOL_WINDOWS = (2, 4, 8, 16)
POOL_CH = GROUP_W // len(POOL_WINDOWS)
SGU_HEADS = 4
SGU_CH = GROUP_W // SGU_HEADS
SGU_CHUNK = 128
D_FF = 2816
IN_COLS = 5 * GROUP_W
N_MOD = 9
ALPHA = (2 * DEPTH) ** 0.25
BETA = (8 * DEPTH) ** -0.25
LN_EPS = 1e-5

kernel_name = "hybrid_s5_fnet_pool_sgu_diffusion_step"


def layer_norm(x, g, b):
    x32 = x.astype(jnp.float32)
    mu = jnp.mean(x32, -1, keepdims=True)
    var = jnp.mean(jnp.square(x32 - mu), -1, keepdims=True)
    y = (x32 - mu) * lax.rsqrt(var + LN_EPS) * g.astype(jnp.float32) + b.astype(jnp.float32)
    return y.astype(x.dtype)


def swiglu(h, w_in, w_out):
    a, g = jnp.split(h @ w_in, 2, axis=-1)
    return (jax.nn.silu(g) * a) @ w_out


def sincos_2d(n_tokens):
    rows = n_tokens // GRID_W
    quarter = D_MODEL // 4
    omega = 1.0 / (10000.0 ** (jnp.arange(quarter, dtype=jnp.float32) / quarter))
    ang_r = jnp.arange(rows, dtype=jnp.float32)[:, None] * omega
    ang_c = jnp.arange(GRID_W, dtype=jnp.float32)[:, None] * omega
    emb_r = jnp.concatenate([jnp.sin(ang_r), jnp.cos(ang_r)], -1)
    emb_c = jnp.concatenate([jnp.sin(ang_c), jnp.cos(ang_c)], -1)
    half = D_MODEL // 2
    pos = jnp.concatenate([jnp.broadcast_to(emb_r[:, None], (rows, GRID_W, half)),
                           jnp.broadcast_to(emb_c[None], (rows, GRID_W, half))], -1)
    return pos.reshape(rows * GRID_W, D_MODEL)


def _ssm_combine(e1, e2):
    a1, b1 = e1
    a2, b2 = e2
    return a1 * a2, a2 * b1 + b2


def s5_mixer(u, lam_re, lam_im, log_dt, b_re, b_im, c_re, c_im, d_skip, glu_w, glu_b, h0_re, h0_im):
    f32 = jnp.float32
    nb, n, _ = u.shape
    u32 = u.astype(f32)
    ug = u32.reshape(nb, n, SSM_GROUPS, SSM_CH).astype(jnp.complex64)
    y = u32 * d_skip.astype(f32)
    fin_re, fin_im = [], []
    for dirn in range(2):
        lam = lax.complex(lam_re[dirn].astype(f32), lam_im[dirn].astype(f32))
        dt = jnp.exp(log_dt[dirn].astype(f32))[:, None]
        a_bar = jnp.exp(lam * dt)
        b_mat = lax.complex(b_re[dirn].astype(f32), b_im[dirn].astype(f32))
        b_bar = ((a_bar - 1.0) / lam)[..., None] * b_mat
        bu = jnp.einsum("blgh,gph->blgp", ug, b_bar)
        h0 = lax.complex(h0_re[:, dirn].astype(f32), h0_im[:, dirn].astype(f32))
        rev = dirn == 1
        first = n - 1 if rev else 0
        bu = bu.at[:, first].add(a_bar * h0)
        _, hs = lax.associative_scan(_ssm_combine, (jnp.broadcast_to(a_bar, bu.shape), bu),
                                     axis=1, reverse=rev)
        c_mat = lax.complex(c_re[dirn].astype(f32), c_im[dirn].astype(f32))
        y = y + jnp.einsum("blgp,ghp->blgh", hs, c_mat).real.reshape(nb, n, GROUP_W)
        fin = hs[:, 0] if rev else hs[:, -1]
        fin_re.append(jnp.real(fin))
        fin_im.append(jnp.imag(fin))
    y = jax.nn.gelu(y)
    y = y * jax.nn.sigmoid(y @ glu_w.astype(f32) + glu_b.astype(f32))
    return y, jnp.stack(fin_re, 1), jnp.stack(fin_im, 1)


def fnet_mixer(z, fnet_w):
    nb, n, _ = z.shape
    zh = z.astype(jnp.float32).reshape(nb, n, FNET_HEADS, FNET_CH)
    f = jnp.fft.fft2(zh, axes=(1, 3), norm="ortho").real
    return jnp.einsum("blgc,gcd->blgd", f, fnet_w.astype(jnp.float32)).reshape(nb, n, GROUP_W)


def pool_mixer(z, pool_w, pool_scale):
    f32 = jnp.float32
    nb, n, _ = z.shape
    ng = len(POOL_WINDOWS)
    z32 = z.astype(f32).reshape(nb, n, ng, POOL_CH)
    cs = jnp.concatenate([jnp.zeros((nb, 1, ng, POOL_CH), f32), jnp.cumsum(z32, axis=1)], axis=1)
    t = jnp.arange(n)
    pooled = []
    for g, w in enumerate(POOL_WINDOWS):
        lo = jnp.clip(t - w // 2, 0, n)
        hi = jnp.clip(t + w - w // 2, 0, n)
        win_sum = jnp.take(cs[:, :, g], hi, axis=1) - jnp.take(cs[:, :, g], lo, axis=1)
        pooled.append(win_sum / (hi - lo).astype(f32)[None, :, None])
    p = jnp.stack(pooled, 2) - z32
    y = jnp.einsum("blgc,gcd->blgd", p, pool_w.astype(f32)).reshape(nb, n, GROUP_W)
    return y * pool_scale.astype(f32)


def sgu_mixer(z, sgu_w, sgu_b):
    f32 = jnp.float32
    nb, n, _ = z.shape
    z = jax.nn.gelu(z.astype(f32))
    u, v = z[..., :GROUP_W], z[..., GROUP_W:]
    v = v.reshape(nb, n // SGU_CHUNK, SGU_CHUNK, SGU_HEADS, SGU_CH)
    mu = jnp.mean(v, -1, keepdims=True)
    var = jnp.mean(jnp.square(v - mu), -1, keepdims=True)
    v = (v - mu) * lax.rsqrt(var + LN_EPS)
    s = jnp.einsum("bnsgc,gts->bntgc", v, sgu_w.astype(f32)) + sgu_b.astype(f32).T[None, None, :, :, None]
    return u * s.reshape(nb, n, GROUP_W)


def token_mixers(h, lp, h0_re, h0_im):
    nb, n, _ = h.shape
    z = h @ lp["w_mix_in"]
    z_a = z[..., :GROUP_W]
    z_b = z[..., GROUP_W:2 * GROUP_W]
    z_c = z[..., 2 * GROUP_W:3 * GROUP_W]
    z_d = z[..., 3 * GROUP_W:]
    y_a, fin_re, fin_im = s5_mixer(z_a, lp["ssm_lam_re"], lp["ssm_lam_im"], lp["ssm_log_dt"],
                                   lp["ssm_b_re"], lp["ssm_b_im"], lp["ssm_c_re"], lp["ssm_c_im"],
                                   lp["ssm_d"], lp["ssm_glu_w"], lp["ssm_glu_b"], h0_re, h0_im)
    y_b = fnet_mixer(z_b, lp["fnet_w"])
    y_c = pool_mixer(z_c, lp["pool_w"], lp["pool_scale"])
    y_d = sgu_mixer(z_d, lp["sgu_w"], lp["sgu_b"])
    y = jnp.stack([y_a, y_b, y_c, y_d], axis=2)
    y = y * lax.rsqrt(jnp.mean(jnp.square(y), -1, keepdims=True) + LN_EPS)
    y = y.reshape(nb, n, MIX_W) * lp["mix_norm_g"].astype(jnp.float32)
    return y.astype(h.dtype) @ lp["w_mix_out"], fin_re, fin_im


def modulation(cond, w_ada, b_ada):
    return (jax.nn.silu(cond) @ w_ada + b_ada).reshape(cond.shape[0], N_MOD, D_MODEL)


def trunk_layer(x, mod, lp, h0_re, h0_im):
    m = [mod[:, k][:, None, :] for k in range(N_MOD)]
    h = x * (1.0 + m[1]) + m[0]
    x = layer_norm(ALPHA * x + 0.5 * m[2] * swiglu(h, lp["ffn_w_in"][0], lp["ffn_w_out"][0]),
                   lp["ln_g"][0], lp["ln_b"][0])
    h = x * (1.0 + m[4]) + m[3]
    y, fin_re, fin_im = token_mixers(h, lp, h0_re, h0_im)
    x = layer_norm(ALPHA * x + m[5] * y, lp["ln_g"][1], lp["ln_b"][1])
    h = x * (1.0 + m[7]) + m[6]
    x = layer_norm(ALPHA * x + 0.5 * m[8] * swiglu(h, lp["ffn_w_in"][1], lp["ffn_w_out"][1]),
                   lp["ln_g"][2], lp["ln_b"][2])
    return x, fin_re, fin_im


def setup_inputs(seed: int = 0) -> dict:
    key = jax.random.key(seed)
    ks = jax.random.split(key, 32)
    f32 = jnp.float32
    nrm = lambda k, shape, s: jax.random.normal(k, shape, f32) * s
    st_shape = (DEC_BATCH, DEPTH, 2, SSM_GROUPS, SSM_STATE)
    lam_im = jnp.broadcast_to(math.pi * jnp.arange(SSM_STATE, dtype=f32), (DEPTH, 2, SSM_GROUPS, SSM_STATE))
    return {
        "x_prompt": nrm(ks[0], (BATCH, SEQ, D_MODEL), 1.0),
        "x_sample": nrm(ks[1], (DEC_BATCH, DEC_SEQ, D_MODEL), 1.0),
        "c": nrm(ks[2], (DEC_BATCH, D_MODEL), 1.0),
        "state_s5_re": nrm(ks[3], st_shape, 0.1),
        "state_s5_im": nrm(ks[4], st_shape, 0.1),
        "c_ctx": nrm(ks[5], (D_MODEL,), 1.0),
        "w_ada": nrm(ks[6], (DEPTH, D_MODEL, N_MOD * D_MODEL), 0.5 * D_MODEL ** -0.5),
        "b_ada": nrm(ks[7], (DEPTH, N_MOD * D_MODEL), 0.01),
        "ffn_w_in": nrm(ks[8], (DEPTH, 2, D_MODEL, 2 * D_FF), D_MODEL ** -0.5),
        "ffn_w_out": nrm(ks[9], (DEPTH, 2, D_FF, D_MODEL), BETA * D_FF ** -0.5),
        "w_mix_in": nrm(ks[10], (DEPTH, D_MODEL, IN_COLS), D_MODEL ** -0.5),
        "w_mix_out": nrm(ks[11], (DEPTH, MIX_W, D_MODEL), BETA * MIX_W ** -0.5),
        "mix_norm_g": 1.0 + nrm(ks[12], (DEPTH, MIX_W), 0.01),
        "ssm_lam_re": -0.5 + nrm(ks[13], (DEPTH, 2, SSM_GROUPS, SSM_STATE), 0.01),
        "ssm_lam_im": lam_im + nrm(ks[14], (DEPTH, 2, SSM_GROUPS, SSM_STATE), 0.01),
        "ssm_log_dt": jax.random.uniform(ks[15], (DEPTH, 2, SSM_GROUPS), f32, math.log(1e-3), math.log(1e-1)),
        "ssm_b_re": nrm(ks[16], (DEPTH, 2, SSM_GROUPS, SSM_STATE, SSM_CH), (2 * SSM_CH) ** -0.5),
        "ssm_b_im": nrm(ks[17], (DEPTH, 2, SSM_GROUPS, SSM_STATE, SSM_CH), (2 * SSM_CH) ** -0.5),
        "ssm_c_re": nrm(ks[18], (DEPTH, 2, SSM_GROUPS, SSM_CH, SSM_STATE), (2 * SSM_STATE) ** -0.5),
        "ssm_c_im": nrm(ks[19], (DEPTH, 2, SSM_GROUPS, SSM_CH, SSM_STATE), (2 * SSM_STATE) ** -0.5),
        "ssm_d": nrm(ks[20], (DEPTH, GROUP_W), 1.0),
        "ssm_glu_w": nrm(ks[21], (DEPTH, GROUP_W, GROUP_W), GROUP_W ** -0.5),
        "ssm_glu_b": nrm(ks[22], (DEPTH, GROUP_W), 0.01),
        "fnet_w": nrm(ks[23], (DEPTH, FNET_HEADS, FNET_CH, FNET_CH), FNET_CH ** -0.5),
        "pool_w": nrm(ks[24], (DEPTH, len(POOL_WINDOWS), POOL_CH, POOL_CH), POOL_CH ** -0.5),
        "pool_scale": 1.0 + nrm(ks[25], (DEPTH, GROUP_W), 0.1),
        "sgu_w": nrm(ks[26], (DEPTH, SGU_HEADS, SGU_CHUNK, SGU_CHUNK), SGU_CHUNK ** -0.5),
        "sgu_b": 1.0 + nrm(ks[27], (DEPTH, SGU_HEADS, SGU_CHUNK), 0.01),
        "ln_g": 1.0 + nrm(ks[28], (DEPTH, 3, D_MODEL), 0.01),
        "ln_b": nrm(ks[29], (DEPTH, 3, D_MODEL), 0.01),
    }


def reference(x_prompt, x_sample, c, state_s5_re, state_s5_im, c_ctx, w_ada, b_ada, ffn_w_in, ffn_w_out,
              w_mix_in, w_mix_out, mix_norm_g, ssm_lam_re, ssm_lam_im, ssm_log_dt, ssm_b_re, ssm_b_im,
              ssm_c_re, ssm_c_im, ssm_d, ssm_glu_w, ssm_glu_b, fnet_w, pool_w, pool_scale, sgu_w, sgu_b,
              ln_g, ln_b):
    n_ctx_batch = x_prompt.shape[0]
    cond_ctx = jnp.broadcast_to(c_ctx, (n_ctx_batch, D_MODEL))
    zero_state = jnp.zeros((n_ctx_batch, 2, SSM_GROUPS, SSM_STATE), jnp.float32)
    xp = x_prompt
    xs = x_sample + sincos_2d(x_sample.shape[1]).astype(x_sample.dtype)[None]
    new_re, new_im = [], []
    for i in range(DEPTH):
        lp = {
            "ffn_w_in": ffn_w_in[i], "ffn_w_out": ffn_w_out[i],
            "w_mix_in": w_mix_in[i], "w_mix_out": w_mix_out[i], "mix_norm_g": mix_norm_g[i],
            "ssm_lam_re": ssm_lam_re[i], "ssm_lam_im": ssm_lam_im[i], "ssm_log_dt": ssm_log_dt[i],
            "ssm_b_re": ssm_b_re[i], "ssm_b_im": ssm_b_im[i], "ssm_c_re": ssm_c_re[i], "ssm_c_im": ssm_c_im[i],
            "ssm_d": ssm_d[i], "ssm_glu_w": ssm_glu_w[i], "ssm_glu_b": ssm_glu_b[i],
            "fnet_w": fnet_w[i], "pool_w": pool_w[i], "pool_scale": pool_scale[i],
            "sgu_w": sgu_w[i], "sgu_b": sgu_b[i], "ln_g": ln_g[i], "ln_b": ln_b[i],
        }
        xp, fin_re, fin_im = trunk_layer(xp, modulation(cond_ctx, w_ada[i], b_ada[i]), lp,
                                         zero_state, zero_state)
        new_re.append(fin_re)
        new_im.append(fin_im)
        xs, _, _ = trunk_layer(xs, modulation(c, w_ada[i], b_ada[i]), lp,
                               state_s5_re[:, i], state_s5_im[:, i])
    new_s5_re = jnp.stack(new_re, axis=1)
    new_s5_im = jnp.stack(new_im, axis=1)
    return (xp, xs, new_s5_re, new_s5_im)
```

```python
import math
from contextlib import ExitStack

import numpy as np
import ml_dtypes

import concourse.bass as bass
import concourse.mybir as mybir
from concourse.bass_utils import run_bass_kernel_spmd

F32 = mybir.dt.float32
F32R = mybir.dt.float32r
BF16 = mybir.dt.bfloat16
I32 = mybir.dt.int32
ALU = mybir.AluOpType
AF = mybir.ActivationFunctionType
AX = mybir.AxisListType

D = 1024
KC = 8
DFF = 2816
FC = 22
T = 512
NMOD = 9
ALPHA = 8.0 ** 0.25
EPS = 1e-5
EPS_LN = EPS / (ALPHA * ALPHA)
TWO_PI = 2.0 * math.pi


class TR:
    def __init__(self, nc, es):
        self.nc = nc
        self.eng = {"pe": nc.tensor, "act": nc.scalar, "dve": nc.vector, "pool": nc.gpsimd, "sp": nc.sync}
        self.semh = {}
        for e in ("pe", "act", "dve", "pool"):
            self.semh[e] = es.enter_context(nc.semaphore("c_" + e))
        self.cnt = {e: 0 for e in ("pe", "act", "dve", "pool")}
        self.waited = {e: {} for e in self.eng}
        self.last_w = {}
        self.readers = {}
        self.ndma = {"sp": 0, "pool": 0, "act": 0, "poolbg": 0}
        self.dma_k = {"sp": 8, "pool": 8, "act": 4, "poolbg": 16}
        self.dma_uses = {}
        for q, k in self.dma_k.items():
            for i in range(k):
                key = ("dma", q, i)
                self.semh[key] = es.enter_context(nc.semaphore(f"d_{q}{i}"))
                self.dma_uses[key] = 0
        self.nwaits = 0

    def _wait(self, e, sk, v):
        if e == "pe" and sk == "pe":
            return
        if self.waited[e].get(sk, 0) >= v:
            return
        self.eng[e].wait_ge(self.semh[sk], v)
        self.waited[e][sk] = v
        self.nwaits += 1

    def _deps(self, e, reads, writes):
        for r in reads:
            t = self.last_w.get(r)
            if t is not None:
                self._wait(e, t[0], t[1])
        for w in writes:
            t = self.last_w.get(w)
            if t is not None:
                self._wait(e, t[0], t[1])
            rd = self.readers.get(w)
            if rd:
                for sk, v in rd.items():
                    self._wait(e, sk, v)

    def _commit(self, tok, reads, writes):
        for r in reads:
            d = self.readers.setdefault(r, {})
            if d.get(tok[0], 0) < tok[1]:
                d[tok[0]] = tok[1]
        for w in writes:
            self.last_w[w] = tok
            self.readers[w] = {}

    def op(self, e, fn, reads=(), writes=()):
        self._deps(e, reads, writes)
        inst = fn(self.eng[e])
        self.cnt[e] += 1
        inst.then_inc(self.semh[e], 1)
        self._commit((e, self.cnt[e]), reads, writes)

    def dma(self, q, out, in_, r=(), w=(), slow=False, bg=False):
        reads, writes = r, w
        self._deps(q, reads, writes)
        qs = q + "bg" if bg else q
        n = self.ndma[qs]
        self.ndma[qs] += 1
        key = ("dma", qs, n % self.dma_k[qs])
        if self.dma_uses[key] > 0:
            self._wait(q, key, 16 * self.dma_uses[key])
        kw = {"allow_slow_non_contiguous": True} if slow else {}
        self.eng[q].dma_start(out=out, in_=in_, **kw).then_inc(self.semh[key], 16)
        self.dma_uses[key] += 1
        self._commit((key, 16 * self.dma_uses[key]), reads, writes)

    def barrier(self):
        toks = [(e, c) for e, c in self.cnt.items() if c > 0]
        toks += [(k, 16 * u) for k, u in self.dma_uses.items() if u > 0 and k[1] != "poolbg"]
        for e in self.eng:
            for sk, v in toks:
                if sk != e:
                    self._wait(e, sk, v)
        keep = {r: t for r, t in self.last_w.items() if isinstance(t[0], tuple) and t[0][1] == "poolbg"}
        self.last_w = keep
        self.readers = {}

    def finish(self):
        for k, u in self.dma_uses.items():
            if u > 0:
                self._wait("sp", k, 16 * u)
        for e, c in self.cnt.items():
            if c > 0:
                self._wait("sp", e, c)


def build(cfg):
    Ls, depth = cfg["Ls"], cfg["depth"]
    NPS, Lp = 2, 256
    NBS = Ls // T
    NB = NBS + 1
    Ltot = Ls + NPS * Lp
    NTI = Ltot // 128
    ZP = Ltot + 8 * 4
    nc = bass.Bass("TRN2", target_bir_lowering=False)

    def din(name, shape, dt=F32):
        return nc.dram_tensor(name, list(shape), dt, kind="ExternalInput").ap()

    def dout(name, shape, dt=F32):
        return nc.dram_tensor(name, list(shape), dt, kind="ExternalOutput").ap()

    I = {}
    I["xs"] = din("xs", [Ls, D])
    I["xp"] = din("xp", [NPS * Lp, D])
    I["pos"] = din("pos", [Ls, D])
    I["cvec"] = din("cvec", [2, D])
    I["st_re"] = din("st_re", [depth, 2, 16, 64])
    I["st_im"] = din("st_im", [depth, 2, 16, 64])
    wshapes = {
        "w_ada": [depth, D, NMOD * D], "b_ada": [depth, NMOD * D],
        "ffn_w_in": [depth, 2, D, 2 * DFF], "ffn_w_out": [depth, 2, DFF, D],
        "w_mix_in": [depth, D, 1280], "w_mix_out": [depth, D, D], "mix_norm_g": [depth, D],
        "ssm_lam_re": [depth, 2, 16, 64], "ssm_lam_im": [depth, 2, 16, 64], "ssm_log_dt": [depth, 2, 16],
        "ssm_b_re": [depth, 2, 16, 64, 16], "ssm_b_im": [depth, 2, 16, 64, 16],
        "ssm_c_re": [depth, 2, 16, 16, 64], "ssm_c_im": [depth, 2, 16, 16, 64],
        "ssm_d": [depth, 256], "ssm_glu_w": [depth, 256, 256], "ssm_glu_b": [depth, 256],
        "fnet_w": [depth, 4, 64, 64], "pool_w": [depth, 4, 64, 64], "pool_scale": [depth, 256],
        "sgu_w": [depth, 4, 128, 128], "sgu_b": [depth, 4, 128],
        "ln_g": [depth, 3, D], "ln_b": [depth, 3, D],
    }
    for k, s in wshapes.items():
        I[k] = din(k, s)
    NKB = Ls // T
    I["ident"] = din("ident", [128, 128])
    I["c64"] = din("c64", [128, 128])
    I["s64"] = din("s64", [128, 128])
    NT_ = Ls // 8 + 64
    I["iota"] = din("iota", [128, 2, NT_])
    I["dftS"] = din("dftS", [NKB, Ls // 128, 128, 2, T], BF16)
    I["dftP"] = din("dftP", [Lp // 128, 128, 2, Lp], BF16)
    O = {
        "ys": dout("ys", [Ls, D]), "yp": dout("yp", [NPS * Lp, D]),
        "nre": dout("nre", [NPS, depth, 2, 16, 64]), "nim": dout("nim", [NPS, depth, 2, 16, 64]),
    }

    scr_in = [[nc.dram_tensor(f"scr_in_{l}_{j}", [128, FC, KC * 256], BF16, kind="Internal").ap() for j in range(2)]
              for l in range(depth)]
    scr_out = [[nc.dram_tensor(f"scr_out_{l}_{j}", [128, KC, FC * 128], BF16, kind="Internal").ap() for j in range(2)]
               for l in range(depth)]

    SCRK = {}
    es = ExitStack()
    with es:
        tr = TR(nc, es)

        nsb = [0]

        def sb(name, shape, dt=F32, st=es):
            nsb[0] += 1
            return st.enter_context(nc.sbuf_tensor(f"s{nsb[0]}_{name}", list(shape), dt))

        PS = [es.enter_context(nc.psum_tensor(f"ps{i}", [128, 512], F32)) for i in range(8)]

        def dve(fn, r=(), w=()):
            tr.op("dve", fn, r, w)

        def act(fn, r=(), w=()):
            tr.op("act", fn, r, w)

        def pool(fn, r=(), w=()):
            tr.op("pool", fn, r, w)

        def pe(fn, r=(), w=()):
            tr.op("pe", fn, r, w)

        def mm(out, lhsT, rhs, start, stop, r, w):
            tr.op("pe", lambda e: e.matmul(out, lhsT=lhsT, rhs=rhs, start=start, stop=stop), r, w)

        X = sb("X", [128, KC, Ltot], BF16)
        ident = sb("ident", [128, 128])
        identb = sb("identb", [128, 128], BF16)
        onesD = sb("onesD", [128, 128], BF16)
        ones256 = sb("ones256", [128, 128], BF16)
        modT = sb("modT", [128, depth, NMOD, KC, 2])
        lng = sb("lng", [128, depth * 3, KC])
        lnb = sb("lnb", [128, depth * 3, KC])
        mng = sb("mng", [128, depth, KC])
        ssd = sb("ssd", [128, depth, 2])
        glb = sb("glb", [128, depth, 2])
        psc = sb("psc", [128, depth, 2])
        corrL = sb("corrL", [128, 2, 8])
        corrR = sb("corrR", [128, 2, 8])
        poolm = sb("poolm", [128, 2, 4])
        mskE = sb("mskE", [32, 1])
        mskO = sb("mskO", [32, 1])

        def xkey(c, b):
            return ("X", c, b)

        def xkeys(c, t0, n):
            return [("X", c, hb_) for hb_ in range(t0 // 256, (t0 + n + 255) // 256)]

        tr.dma("sp", ident[:], I["ident"], w=["ident"])
        act(lambda e: e.activation(out=identb[:], in_=ident[:], func=AF.Copy), ["ident"], ["identb"])
        dve(lambda e: e.memset(onesD[:], 1.0 / D), w=["onesD"])
        dve(lambda e: e.memset(ones256[:], 1.0 / 256), w=["ones256"])

        tr.dma("sp", lng[:], I["ln_g"].rearrange("l j (k p) -> p (l j) k", p=128), w=["lng"], slow=True)
        tr.dma("sp", lnb[:], I["ln_b"].rearrange("l j (k p) -> p (l j) k", p=128), w=["lnb"], slow=True)
        tr.dma("sp", mng[:], I["mix_norm_g"].rearrange("l (k p) -> p l k", p=128), w=["mng"], slow=True)
        tr.dma("sp", ssd[:], I["ssm_d"].rearrange("l (k p) -> p l k", p=128), w=["ssd"], slow=True)
        tr.dma("sp", glb[:], I["ssm_glu_b"].rearrange("l (k p) -> p l k", p=128), w=["glb"], slow=True)
        tr.dma("sp", psc[:], I["pool_scale"].rearrange("l (k p) -> p l k", p=128), w=["psc"], slow=True)
        dve(lambda e: e.memset(poolm[:], 0.0), w=["poolm"])
        dve(lambda e: e.memset(corrL[:], 1.0), w=["corrL"])
        dve(lambda e: e.memset(corrR[:], 1.0), w=["corrR"])
        for gi, wv in enumerate((2, 4, 8, 16)):
            ch, p0 = gi // 2, (gi % 2) * 64
            dve(lambda e, ch=ch, p0=p0, gi=gi, wv=wv: e.memset(poolm[p0:p0 + 64, ch, gi:gi + 1], 1.0 / wv),
                ["poolm"], ["poolm"])
            for t in range(8):
                cntL = t + wv // 2 - max(t - wv // 2, 0)
                if cntL != wv:
                    dve(lambda e, ch=ch, p0=p0, t=t, v=wv / cntL: e.memset(corrL[p0:p0 + 64, ch, t:t + 1], v),
                        ["corrL"], ["corrL"])
                dist = 8 - t
                cntR = min(wv // 2, dist) + wv // 2
                if cntR != wv:
                    dve(lambda e, ch=ch, p0=p0, t=t, v=wv / cntR: e.memset(corrR[p0:p0 + 64, ch, t:t + 1], v),
                        ["corrR"], ["corrR"])
        dve(lambda e: e.memset(mskE[:], 0.0), w=["mskE"])
        dve(lambda e: e.memset(mskO[:], 1.0), w=["mskO"])
        dve(lambda e: e.memset(mskE[0:16, :], 1.0), ["mskE"], ["mskE"])
        dve(lambda e: e.memset(mskO[0:16, :], 0.0), ["mskO"], ["mskO"])

        with ExitStack() as p0s:
            cvT = sb("cvT", [128, 2, KC], F32, p0s)
            scv = sb("scv", [128, 2, KC], F32, p0s)
            bada = sb("bada", [128, depth, NMOD * KC], F32, p0s)
            wad = [sb(f"wad{i}", [128, KC, 512], F32, p0s) for i in range(2)]
            xin = [sb(f"xin{i}", [128, D], F32, p0s) for i in range(2)]
            pin = [sb(f"pin{i}", [128, D], F32, p0s) for i in range(2)]
            stg = [sb(f"stg{i}", [128, 2816], F32, p0s) for i in range(2)]
            wall = [sb(f"wall{i}", [128, 11 * KC * 256], BF16, p0s) for i in range(1)]
            for cnd in range(2):
                tr.dma("sp", cvT[:, cnd, :], I["cvec"][cnd].rearrange("(k p) -> p k", p=128), w=["cvT"], slow=True)
            tr.dma("sp", bada[:], I["b_ada"].rearrange("l (m p) -> p l m", p=128), w=["bada"], slow=True)
            act(lambda e: e.activation(out=scv[:], in_=cvT[:], func=AF.Silu), ["cvT"], ["scv"])
            WA, WB, WC = [], [], []

            mrow = [sb(f"mrow{i}", [2, 512], F32, p0s) for i in range(2)]

            def mod_item(it, l, m, half):
                wt = wad[it % 2]
                wk = ("wad", it % 2)
                col0 = m * D + half * 512
                tr.dma("sp" if it % 2 == 0 else "act", wt[:],
                       I["w_ada"][l, :, col0:col0 + 512].rearrange("(k p) n -> p k n", p=128), w=[wk])
                ps = PS[it % 2]
                pk = ("ps", it % 2)
                for k in range(KC):
                    mm(ps[0:2, :], scv[:, :, k], wt[:, k, :], k == 0, k == KC - 1, [wk, "scv"], [pk])
                mr, mk = mrow[it % 2], ("mrow", it % 2)
                act(lambda e: e.activation(out=mr[:, :], in_=ps[0:2, :], func=AF.Copy), [pk], [mk])
                ps2 = PS[6 + it % 2]
                pk2 = ("ps", 6 + it % 2)
                for cc in range(4):
                    mm(ps2[:, cc * 2:cc * 2 + 2], mr[0:2, cc * 128:(cc + 1) * 128], ident[0:2, 0:2], True, True, [mk, "ident"], [pk2])
                for cnd in range(2):
                    dve(lambda e, cnd=cnd: e.tensor_tensor(
                        out=modT[:, l, m, half * 4:half * 4 + 4, cnd], in0=ps2[:, cnd:8:2],
                        in1=bada[:, l, m * KC + half * 4:m * KC + half * 4 + 4], op=ALU.add),
                        [pk2, "bada"], ["modT"])

            it = 0
            for l in range(depth):
                for m in range(NMOD):
                    for half in range(2):
                        WA.append(lambda it=it, l=l, m=m, half=half: mod_item(it, l, m, half))
                        it += 1

            def in_item(i):
                xt = xin[i % 2]
                xk = ("xin", i % 2)
                if i < Ls // 128:
                    tr.dma("sp", xt[:], I["xs"][i * 128:(i + 1) * 128, :], w=[xk])
                    tr.dma("act", pin[i % 2][:], I["pos"][i * 128:(i + 1) * 128, :], w=[("pin", i % 2)])
                    pool(lambda e, xt=xt, pt=pin[i % 2]: e.tensor_tensor(out=xt[:], in0=xt[:], in1=pt[:], op=ALU.add),
                         [xk, ("pin", i % 2)], [xk])
                else:
                    j = i - Ls // 128
                    tr.dma("sp", xt[:], I["xp"][j * 128:(j + 1) * 128, :], w=[xk])
                for hb in range(2):
                    ps = PS[2 + (2 * i + hb) % 4]
                    pk = ("ps", 2 + (2 * i + hb) % 4)
                    for c4 in range(4):
                        c = hb * 4 + c4
                        pe(lambda e, ps=ps, c4=c4, c=c, xt=xt: e.transpose(ps[:, c4 * 128:(c4 + 1) * 128],
                                                                          xt[:, c * 128:(c + 1) * 128], ident[:]),
                           [xk, "ident"], [pk])
                    wr = [k_ for c4 in range(4) for k_ in xkeys(hb * 4 + c4, i * 128, 128)]
                    if hb == 0:
                        act(lambda e, ps=ps, hb=hb, i=i: e.activation(
                            out=X[:, hb * 4:hb * 4 + 4, i * 128:(i + 1) * 128],
                            in_=ps[:].rearrange("p (c t) -> p c t", c=4), func=AF.Copy), [pk], wr)
                    else:
                        dve(lambda e, ps=ps, hb=hb, i=i: e.tensor_copy(
                            out=X[:, hb * 4:hb * 4 + 4, i * 128:(i + 1) * 128],
                            in_=ps[:].rearrange("p (c t) -> p c t", c=4)), [pk], wr)

            for i in range(NTI):
                WB.append(lambda i=i: in_item(i))

            cst_ = {"nst": 0, "ncast": 0}

            def cast(out, in_, r, w):
                k = cst_["ncast"] % 3
                cst_["ncast"] += 1
                if k == 0:
                    act(lambda e: e.activation(out=out, in_=in_, func=AF.Copy), r, w)
                elif k == 1:
                    dve(lambda e: e.tensor_copy(out=out, in_=in_), r, w)
                else:
                    pool(lambda e: e.tensor_copy(out=out, in_=in_), r, w)

            wi0 = I["ffn_w_in"][0, 0].rearrange("(k p) (h n) -> p k h n", p=128, h=2)
            wo0 = I["ffn_w_out"][0, 0].rearrange("(f p) n -> p f n", p=128)
            wl0, wk0 = wall[0], ("wall", 0)

            def cv_in(fh, k):
                sg_, sk = stg[cst_["nst"] % 2], ("stg", cst_["nst"] % 2)
                cst_["nst"] += 1
                wv = wl0[:].rearrange("p (f k n) -> p f k n", f=11, k=KC)
                tr.dma("sp" if cst_["nst"] % 2 else "act", sg_[:].rearrange("p (h n) -> p h n", h=2), wi0[:, k, :, fh * 1408:(fh + 1) * 1408], w=[sk])
                cast(wv[:, :, k, :].rearrange("p f (h c) -> p h f c", h=2), sg_[:].rearrange("p (h f c) -> p h f c", h=2, c=128), [sk], [wk0])

            def cv_in_store(fh):
                tr.dma("sp", scr_in[0][0][:, fh * 11:(fh + 1) * 11, :].rearrange("p f n -> p (f n)"), wl0[:], r=[wk0], w=[("scr_in", 0, 0, fh)])
                SCRK.setdefault(("in", 0, 0), []).append(("scr_in", 0, 0, fh))

            def cv_out(fg):
                sg_, sk = stg[cst_["nst"] % 2], ("stg", cst_["nst"] % 2)
                cst_["nst"] += 1
                wv = wl0[:].rearrange("p (c f n) -> p c f n", c=KC, f=FC)
                tr.dma("sp" if cst_["nst"] % 2 else "act", sg_[:, 0:2048].rearrange("p (f n) -> p f n", f=2), wo0[:, fg * 2:fg * 2 + 2, :], w=[sk])
                cast(wv[:, :, fg * 2:fg * 2 + 2, :], sg_[:, 0:2048].rearrange("p (f c n) -> p c f n", f=2, n=128), [sk], [wk0])

            def cv_out_store():
                tr.dma("sp", scr_out[0][0][:].rearrange("p c n -> p (c n)"), wl0[:], r=[wk0], w=[("scr_out", 0, 0, 0)])
                SCRK.setdefault(("out", 0, 0), []).append(("scr_out", 0, 0, 0))

            for fh in range(2):
                for k in range(KC):
                    WC.append(lambda fh=fh, k=k: cv_in(fh, k))
                WC.append(lambda fh=fh: cv_in_store(fh))
            for fg in range(11):
                WC.append(lambda fg=fg: cv_out(fg))
            WC.append(cv_out_store)
            nA = len(WA)
            for ia in range(nA):
                WA[ia]()
                tb = (ia + 1) * len(WB) // nA - ia * len(WB) // nA
                for _ in range(tb):
                    WB.pop(0)()
                tcv = (ia + 1) * 29 // nA - ia * 29 // nA
                for _ in range(tcv):
                    if WC:
                        WC.pop(0)()
            while WB:
                WB.pop(0)()
            while WC:
                WC.pop(0)()
            for l in range(depth):
                for j in range(3):
                    dve(lambda e, l=l, j=j: e.tensor_scalar_add(out=modT[:, l, 3 * j + 1], in0=modT[:, l, 3 * j + 1],
                                                                scalar1=1.0), ["modT"], ["modT"])
                    gsc = (1.0 if j == 1 else 0.5) / ALPHA
                    dve(lambda e, l=l, j=j, gsc=gsc: e.tensor_scalar_mul(out=modT[:, l, 3 * j + 2],
                                                                         in0=modT[:, l, 3 * j + 2], scalar1=gsc),
                        ["modT"], ["modT"])
        tr.barrier()

        def conv_bg_list(l, j):
            wi = I["ffn_w_in"][l, j].rearrange("(k p) (h f c) -> p k h f c", p=128, h=2, c=128)
            wo = I["ffn_w_out"][l, j].rearrange("(f p) (c n) -> p f c n", p=128, n=128)
            si = scr_in[l][j].rearrange("p f (k h c) -> p f k h c", k=KC, h=2)
            so = scr_out[l][j].rearrange("p c (f n) -> p c f n", n=128)
            lst = []
            for k in range(KC):
                for hh in range(2):
                    key = ("scr_in", l, j, k * 2 + hh)
                    SCRK.setdefault(("in", l, j), []).append(key)
                    lst.append(lambda k=k, hh=hh, key=key: tr.dma("pool", si[:, :, k, hh, :], wi[:, k, hh, :, :], w=[key], bg=True))
            for c in range(KC):
                key = ("scr_out", l, j, c)
                SCRK.setdefault(("out", l, j), []).append(key)
                lst.append(lambda c=c, key=key: tr.dma("pool", so[:, c, :, :], wo[:, :, c, :], w=[key], bg=True))
            return lst

        BG = []

        def bg_issue(n):
            for _ in range(n):
                if BG:
                    BG.pop(0)()

        def col(t, *idx):
            a = t
            sl = tuple([slice(None)] + list(idx[:-1]) + [slice(idx[-1], idx[-1] + 1)])
            return t[sl]

        class Epi:
            pass

        def make_epi(st, W_=T):
            ep = Epi()
            ep.rp = sb("rp", [128, KC, W_], F32, st)
            ep.rb = [sb(f"rb{i}", [128, W_], BF16, st) for i in range(2)]
            ep.r2b = [sb(f"r2b{i}", [128, W_], BF16, st) for i in range(2)]
            ep.mean = sb("mean_sb", [128, W_], F32, st)
            ep.m2 = sb("m2", [128, W_], F32, st)
            ep.rstd = sb("rstd", [128, W_], F32, st)
            ep.t1 = [sb(f"t1_{i}", [128, W_], F32, st) for i in range(2)]
            ep.pend = None
            return ep

        def epi_chunk(ep, c, psO, pk, t0, n, l, sj, cond):
            blk = slice(t0, t0 + n)
            b = t0 // T
            ep.n = n
            dve(lambda e: e.scalar_tensor_tensor(out=ep.rp[:, c, :], in0=psO[:, 0:n], scalar=modT[:, l, 3 * sj + 2, c, cond:cond + 1],
                                                 in1=X[:, c, blk], op0=ALU.mult, op1=ALU.add),
                [pk, "modT"] + xkeys(c, t0, n), [("rp", c)])
            s = c % 2
            act(lambda e: e.activation(out=ep.rb[s][:], in_=ep.rp[:, c, :], func=AF.Copy), [("rp", c)], [("rb", s)])
            act(lambda e: e.activation(out=ep.r2b[s][:], in_=ep.rp[:, c, :], func=AF.Square), [("rp", c)], [("r2b", s)])
            epi_flush(ep)
            ep.pend = (c, s)

        def epi_flush(ep):
            if ep.pend is None:
                return
            c, s = ep.pend
            mm(PS[6][:, 0:ep.n], onesD[:], ep.rb[s][:], c == 0, c == KC - 1, [("rb", s), "onesD"], [("ps", 6)])
            mm(PS[7][:, 0:ep.n], onesD[:], ep.r2b[s][:], c == 0, c == KC - 1, [("r2b", s), "onesD"], [("ps", 7)])
            ep.pend = None

        def epi_finish(ep, t0, n, l, sj, final_out=None):
            epi_flush(ep)
            blk = slice(t0, t0 + n)
            b = t0 // T
            act(lambda e: e.activation(out=ep.mean[:], in_=PS[6][:, 0:n], func=AF.Copy), [("ps", 6)], ["mean"])
            dve(lambda e: e.tensor_tensor(out=ep.m2[:], in0=ep.mean[:], in1=ep.mean[:], op=ALU.mult), ["mean"], ["m2"])
            dve(lambda e: e.tensor_tensor(out=ep.m2[:], in0=PS[7][:, 0:n], in1=ep.m2[:], op=ALU.subtract),
                [("ps", 7), "m2"], ["m2"])
            dve(lambda e: e.tensor_scalar(out=ep.m2[:], in0=ep.m2[:], scalar1=EPS_LN, scalar2=None, op0=ALU.add),
                ["m2"], ["m2"])
            act(lambda e: e.activation(out=ep.m2[:], in_=ep.m2[:], func=AF.Ln), ["m2"], ["m2"])
            act(lambda e: e.activation(out=ep.rstd[:], in_=ep.m2[:], func=AF.Exp, scale=-0.5), ["m2"], ["rstd"])
            for c in range(KC):
                s = c % 2
                pool(lambda e, c=c, s=s: e.tensor_tensor(out=ep.t1[s][:], in0=ep.rp[:, c, :], in1=ep.mean[:], op=ALU.subtract),
                     [("rp", c), "mean"], [("t1", s)])
                pool(lambda e, c=c, s=s: e.tensor_tensor(out=ep.t1[s][:], in0=ep.t1[s][:], in1=ep.rstd[:], op=ALU.mult),
                     [("t1", s), "rstd"], [("t1", s)])
                if final_out is None:
                    pool(lambda e, c=c, s=s: e.tensor_scalar(out=X[:, c, blk], in0=ep.t1[s][:], scalar1=lng[:, l * 3 + sj, c:c + 1],
                                                             scalar2=lnb[:, l * 3 + sj, c:c + 1], op0=ALU.mult, op1=ALU.add),
                         [("t1", s), "lng", "lnb"], xkeys(c, t0, n))
                else:
                    pool(lambda e, c=c, s=s: e.tensor_scalar(out=ep.rp[:, c, :], in0=ep.t1[s][:], scalar1=lng[:, l * 3 + sj, c:c + 1],
                                                             scalar2=lnb[:, l * 3 + sj, c:c + 1], op0=ALU.mult, op1=ALU.add),
                         [("t1", s), "lng", "lnb", ("rp", c)], [("rp", c)])

        def make_h(hbuf, t0, n, l, sj):
            b = t0 // T
            cond = 0 if t0 < Ls else 1
            blk = slice(t0, t0 + n)
            for c in range(KC):
                dve(lambda e, c=c: e.tensor_scalar(out=hbuf[:, c, :], in0=X[:, c, blk],
                                                   scalar1=modT[:, l, 3 * sj + 1, c, cond:cond + 1],
                                                   scalar2=modT[:, l, 3 * sj, c, cond:cond + 1],
                                                   op0=ALU.mult, op1=ALU.add),
                    ["modT"] + xkeys(c, t0, n), [("h", c)])

        def ffn(l, j, final):
            sj = 0 if j == 0 else 2
            with ExitStack() as st:
                ep = make_epi(st)
                hbufs = [sb(f"h{i}", [128, KC, T], BF16, st) for i in range(2)]
                hid = sb("hid", [128, FC, T], BF16, st)
                NWI, NWO = 3, 2
                win = [sb(f"win{i}", [128, 2, KC, 256], BF16, st) for i in range(NWI)]
                wout = [sb(f"wout{i}", [128, FC, 128], BF16, st) for i in range(NWO)]
                sg = [sb(f"sg{i}", [128, T], F32, st) for i in range(2)]
                xo = ep.rp if final else None
                ot = [sb(f"ot{i}", [128, D], F32, st) for i in range(2)] if final else None
                nwi = 0
                nwo = 0
                npa = 0

                def mk_h(b):
                    hh = hbufs[b % 2]
                    cond = 0 if b < NBS else 1
                    for c in range(KC):
                        dve(lambda e, c=c, hh=hh: e.tensor_scalar(out=hh[:, c, :], in0=X[:, c, b * T:(b + 1) * T],
                                                                scalar1=modT[:, l, 3 * sj + 1, c, cond:cond + 1],
                                                                scalar2=modT[:, l, 3 * sj, c, cond:cond + 1],
                                                                op0=ALU.mult, op1=ALU.add),
                            ["modT"] + xkeys(c, b * T, T), [("h", b % 2, c)])

                mk_h(0)
                for b in range(NB):
                    cond = 0 if b < NBS else 1
                    h = hbufs[b % 2]
                    for f in range(FC):
                        if f % 2 == 0:
                            s = nwi % NWI
                            nwi += 1
                            wk = ("win", s)
                            tr.dma("sp", win[s][:].rearrange("p f k n -> p (f k n)"),
                                   scr_in[l][j][:, f:f + 2, :].rearrange("p f n -> p (f n)"), r=SCRK[("in", l, j)], w=[wk])
                        wt = win[s][:, f % 2]
                        pa, pg = PS[(npa % 2) * 2], PS[(npa % 2) * 2 + 1]
                        ka, kg = ("ps", (npa % 2) * 2), ("ps", (npa % 2) * 2 + 1)
                        npa += 1
                        for k in range(KC):
                            mm(pa[:, :], wt[:, k, 0:128], h[:, k, :], k == 0, k == KC - 1, [wk, ("h", b % 2, k)], [ka])
                        for k in range(KC):
                            mm(pg[:, :], wt[:, k, 128:256], h[:, k, :], k == 0, k == KC - 1, [wk, ("h", b % 2, k)], [kg])
                        q = f % 2
                        act(lambda e, pg=pg, q=q: e.activation(out=sg[q][:], in_=pg[:, :], func=AF.Silu), [kg], [("sg", q)])
                        dve(lambda e, pa=pa, q=q, f=f: e.tensor_tensor(out=hid[:, f, :], in0=pa[:, :], in1=sg[q][:], op=ALU.mult),
                            [ka, ("sg", q)], [("hid", f)])
                    if b + 1 < NB:
                        mk_h(b + 1)
                    for c in range(KC):
                        s = nwo % NWO
                        nwo += 1
                        wk = ("wout", s)
                        tr.dma("sp", wout[s][:].rearrange("p f n -> p (f n)"), scr_out[l][j][:, c, :], r=SCRK[("out", l, j)], w=[wk])
                        po = PS[4 + c % 2]
                        pk = ("ps", 4 + c % 2)
                        for f in range(FC):
                            mm(po[:, :], wout[s][:, f, :], hid[:, f, :], f == 0, f == FC - 1, [wk, ("hid", f)], [pk])
                        epi_chunk(ep, c, po, pk, b * T, T, l, sj, cond)
                    epi_finish(ep, b * T, T, l, sj, xo)
                    if final:
                        for q in range(4):
                            o = ot[q % 2]
                            ok = ("ot", q % 2)
                            for hb in range(2):
                                ps = PS[hb]
                                pk = ("ps", hb)
                                for c4 in range(4):
                                    c = hb * 4 + c4
                                    pe(lambda e, ps=ps, c4=c4, c=c, q=q: e.transpose(
                                        ps[:, c4 * 128:(c4 + 1) * 128], xo[:, c, q * 128:(q + 1) * 128], ident[:]),
                                       [("rp", c), "ident"], [pk])
                                if hb == 0:
                                    act(lambda e, ps=ps, o=o: e.activation(out=o[:, 0:512], in_=ps[:, :], func=AF.Copy), [pk], [ok])
                                else:
                                    dve(lambda e, ps=ps, o=o: e.tensor_copy(out=o[:, 512:1024], in_=ps[:, :]), [pk], [ok])
                            t0 = b * T + q * 128
                            if t0 < Ls:
                                tr.dma("sp", O["ys"][t0:t0 + 128, :], o[:], r=[ok])
                            else:
                                tr.dma("sp", O["yp"][t0 - Ls:t0 - Ls + 128, :], o[:], r=[ok])
            tr.barrier()

        NBLK = Ls // 8
        SEQS = [(0, NBLK, True, -1), (Ls, Lp // 8, False, 0), (Ls + Lp, Lp // 8, False, 1)]
        HB = []
        _c = 0
        for (_o, _n, _s, _q) in SEQS:
            HB.append(_c)
            _c += _n + 1
        NCOL = _c
        POFF = [8, Ls + 16, Ls + 16 + Lp + 8]

        def bc(ap, shape, axis):
            return ap.unsqueeze(axis).to_broadcast(list(shape))

        def mixer(l):
            with ExitStack() as ms:
                B1 = sb("B1", [128, 2 * Ltot], BF16, ms)
                ya = sb("ya", [128, 2, Ltot], BF16, ms)
                zaV = B1[:].rearrange("p (c t) -> p c t", c=2)
                zbV = B1[:].rearrange("p (i n) -> p i n", n=256)
                wmi = I["w_mix_in"][l].rearrange("(k p) n -> p k n", p=128)
                with ExitStack() as st:
                    wa = sb("wa", [128, KC, 256], BF16, st)
                    h = sb("hmix", [128, KC, T], BF16, st)
                    tr.dma("pool", wa[:], wmi[:, :, 0:256], w=["wa"])
                    BG.extend(conv_bg_list(l, 1))
                    if l + 1 < depth:
                        BG.extend(conv_bg_list(l + 1, 0))
                    for b in range(NB):
                        bg_issue(2)
                        make_h(h, b * T, T, l, 1)
                        for ct in range(2):
                            ps, pk = PS[ct], ("ps", ct)
                            for k in range(KC):
                                mm(ps[:, :], wa[:, k, ct * 128:(ct + 1) * 128], h[:, k, :], k == 0, k == KC - 1,
                                   ["wa", ("h", k)], [pk])
                            if ct == 0:
                                act(lambda e, ps=ps, b=b, ct=ct: e.activation(out=zaV[:, ct, b * T:(b + 1) * T], in_=ps[:, :], func=AF.Copy),
                                    [pk], [("za", ct, b)])
                            else:
                                dve(lambda e, ps=ps, b=b, ct=ct: e.tensor_copy(out=zaV[:, ct, b * T:(b + 1) * T], in_=ps[:, :]),
                                    [pk], [("za", ct, b)])
                tr.barrier()
                s5(l, zaV, ya)
                tr.barrier()
                zc = sb("zc", [128, 2, ZP], BF16, ms)
                yb = sb("yb", [128, 2, Ltot], BF16, ms)
                fnet_pass(l, zbV, zc, yb)
                tr.barrier()
                pass3(l, B1, ya, yb, zc)
                bg_issue(len(BG))
            tr.barrier()

        def s5(l, zaV, ya):
            with ExitStack() as st:
                A = lambda name, shape, dt=F32: sb(name, shape, dt, st)
                iota = A("iota", [128, 2, NT_])
                tr.dma("sp", iota[:], I["iota"], w=["iota"])
                lre, lim, dtc = A("lre", [128, 16]), A("lim", [128, 16]), A("dtc", [128, 16])
                h0re, h0im = A("h0re", [128, 16]), A("h0im", [128, 16])
                for d in range(2):
                    sl = slice(d * 8, (d + 1) * 8)
                    tr.dma("sp", lre[:, sl], I["ssm_lam_re"][l, d].rearrange("(i q) p -> (q p) i", q=2), w=["lre"], slow=True)
                    tr.dma("sp", lim[:, sl], I["ssm_lam_im"][l, d].rearrange("(i q) p -> (q p) i", q=2), w=["lim"], slow=True)
                    tr.dma("sp", h0re[:, sl], I["st_re"][l, d].rearrange("(i q) p -> (q p) i", q=2), w=["h0re"], slow=True)
                    tr.dma("sp", h0im[:, sl], I["st_im"][l, d].rearrange("(i q) p -> (q p) i", q=2), w=["h0im"], slow=True)
                    for q in range(2):
                        tr.dma("sp", dtc[q * 64:(q + 1) * 64, sl],
                               I["ssm_log_dt"][l, d].rearrange("(i q) -> q i", q=2)[q:q + 1, :].to_broadcast([64, 8]),
                               w=["dtc"], slow=True)
                act(lambda e: e.activation(out=dtc[:], in_=dtc[:], func=AF.Exp), ["dtc"], ["dtc"])
                lrdt, ang1 = A("lrdt", [128, 16]), A("ang1", [128, 16])
                dve(lambda e: e.tensor_tensor(out=lrdt[:], in0=lre[:], in1=dtc[:], op=ALU.mult), ["lre", "dtc"], ["lrdt"])
                dve(lambda e: e.tensor_tensor(out=ang1[:], in0=lim[:], in1=dtc[:], op=ALU.mult), ["lim", "dtc"], ["ang1"])
                mag = A("mag", [128, 16, 9])
                for e_ in range(9):
                    act(lambda e, e_=e_: e.activation(out=mag[:, :, e_], in_=lrdt[:], func=AF.Exp, scale=float(e_)), ["lrdt"], ["mag"])
                ysn, ycs = A("ysn", [128, 16, 9]), A("ycs", [128, 16, 9])
                for e_ in range(9):
                    dve(lambda e, e_=e_: e.tensor_scalar_mul(out=ysn[:, :, e_], in0=ang1[:], scalar1=float(e_) / TWO_PI), ["ang1"], ["ysn"])
                dve(lambda e: e.tensor_scalar_add(out=ycs[:], in0=ysn[:], scalar1=0.25), ["ysn"], ["ycs"])
                ki, kf = A("ki", [128, 16 * 9], I32), A("kf", [128, 16 * 9])

                def frac(y, key):
                    yv = y[:].rearrange("p a b -> p (a b)")
                    dve(lambda e: e.tensor_copy(out=ki[:], in_=yv), [key], ["ki"])
                    dve(lambda e: e.tensor_copy(out=kf[:], in_=ki[:]), ["ki"], ["kf"])
                    dve(lambda e: e.tensor_tensor(out=yv, in0=yv, in1=kf[:], op=ALU.subtract), [key, "kf"], [key])
                    dve(lambda e: e.tensor_scalar(out=yv, in0=yv, scalar1=0.49999, scalar2=-0.49999, op0=ALU.min, op1=ALU.max), [key], [key])

                frac(ysn, "ysn")
                frac(ycs, "ycs")
                sn, cs = A("sn", [128, 16, 9]), A("cs", [128, 16, 9])
                act(lambda e: e.activation(out=sn[:], in_=ysn[:], func=AF.Sin, scale=TWO_PI), ["ysn"], ["sn"])
                act(lambda e: e.activation(out=cs[:], in_=ycs[:], func=AF.Sin, scale=TWO_PI), ["ycs"], ["cs"])
                pwr, pwi = A("pwr", [128, 16, 9]), A("pwi", [128, 16, 9])
                dve(lambda e: e.tensor_tensor(out=pwr[:], in0=mag[:], in1=cs[:], op=ALU.mult), ["mag", "cs"], ["pwr"])
                dve(lambda e: e.tensor_tensor(out=pwi[:], in0=mag[:], in1=sn[:], op=ALU.mult), ["mag", "sn"], ["pwi"])
                am1, den, t_a, t_b = A("am1", [128, 16]), A("den", [128, 16]), A("t_a", [128, 16]), A("t_b", [128, 16])
                cfr, cfi = A("cfr", [128, 16]), A("cfi", [128, 16])
                ar, ai = pwr[:, :, 1], pwi[:, :, 1]
                TT = lambda o, a, b_, op, r, w: dve(lambda e: e.tensor_tensor(out=o, in0=a, in1=b_, op=op), r, w)
                dve(lambda e: e.tensor_scalar_add(out=am1[:], in0=ar, scalar1=-1.0), ["pwr"], ["am1"])
                TT(den[:], lre[:], lre[:], ALU.mult, ["lre"], ["den"])
                TT(t_a[:], lim[:], lim[:], ALU.mult, ["lim"], ["t_a"])
                TT(den[:], den[:], t_a[:], ALU.add, ["den", "t_a"], ["den"])
                dve(lambda e: e.reciprocal(out=den[:], in_=den[:]), ["den"], ["den"])
                TT(t_a[:], am1[:], lre[:], ALU.mult, ["am1", "lre"], ["t_a"])
                TT(t_b[:], ai, lim[:], ALU.mult, ["pwi", "lim"], ["t_b"])
                TT(t_a[:], t_a[:], t_b[:], ALU.add, ["t_a", "t_b"], ["t_a"])
                TT(cfr[:], t_a[:], den[:], ALU.mult, ["t_a", "den"], ["cfr"])
                TT(t_a[:], ai, lre[:], ALU.mult, ["pwi", "lre"], ["t_a"])
                TT(t_b[:], am1[:], lim[:], ALU.mult, ["am1", "lim"], ["t_b"])
                TT(t_a[:], t_a[:], t_b[:], ALU.subtract, ["t_a", "t_b"], ["t_a"])
                TT(cfi[:], t_a[:], den[:], ALU.mult, ["t_a", "den"], ["cfi"])
                cbr, cbi, tmp9 = A("cbr", [128, 16, 8]), A("cbi", [128, 16, 8]), A("tmp9", [128, 16, 8])
                S8 = [128, 16, 8]
                TT(cbr[:], pwr[:, :, 0:8], bc(cfr[:], S8, 2), ALU.mult, ["pwr", "cfr"], ["cbr"])
                TT(tmp9[:], pwi[:, :, 0:8], bc(cfi[:], S8, 2), ALU.mult, ["pwi", "cfi"], ["tmp9"])
                TT(cbr[:], cbr[:], tmp9[:], ALU.subtract, ["cbr", "tmp9"], ["cbr"])
                TT(cbi[:], pwi[:, :, 0:8], bc(cfr[:], S8, 2), ALU.mult, ["pwi", "cfr"], ["cbi"])
                TT(tmp9[:], pwr[:, :, 0:8], bc(cfi[:], S8, 2), ALU.mult, ["pwr", "cfi"], ["tmp9"])
                TT(cbi[:], cbi[:], tmp9[:], ALU.add, ["cbi", "tmp9"], ["cbi"])
                g0r, g0i = A("g0r", [128, 16]), A("g0i", [128, 16])
                TT(g0r[:], h0re[:], cs[:, :, 8], ALU.mult, ["h0re", "cs"], ["g0r"])
                TT(t_a[:], h0im[:], sn[:, :, 8], ALU.mult, ["h0im", "sn"], ["t_a"])
                TT(g0r[:], g0r[:], t_a[:], ALU.subtract, ["g0r", "t_a"], ["g0r"])
                TT(g0i[:], h0re[:], sn[:, :, 8], ALU.mult, ["h0re", "sn"], ["g0i"])
                TT(t_a[:], h0im[:], cs[:, :, 8], ALU.mult, ["h0im", "cs"], ["t_a"])
                TT(g0i[:], g0i[:], t_a[:], ALU.add, ["g0i", "t_a"], ["g0i"])
                fin = A("fin", [128, 2, 16, 2])
                S3 = [128, 8, 64]
                npo = 0
                for ct in range(2):
                  with ExitStack() as cst:
                    Ac = lambda name, shape, dt=F32: sb(name, shape, dt, cst)
                    WinAll = Ac("WinAll", [128, 64, 128], BF16)
                    WoutAll = Ac("WoutAll", [128, 8, 8, 2, 64], BF16)
                    KK = Ac("KK", [128, 16, 128], BF16)
                    dve(lambda e: e.memset(KK[:], 0.0), [], ["KK"])
                    with ExitStack() as pst:
                        Ap = lambda name, shape, dt=F32: sb(name, shape, dt, pst)
                        Bn = [Ap(f"Bn{p}", [128, 8, 16]) for p in range(2)]
                        Cn = [Ap(f"Cn{p}", [32, 8, 64]) for p in range(2)]
                        for d in range(2):
                            sl = slice(d * 4, (d + 1) * 4)
                            gs = slice(ct * 8, ct * 8 + 8)
                            for p, nm in enumerate(("ssm_b_re", "ssm_b_im")):
                                tr.dma("sp", Bn[p][:, sl, :], I[nm][l, d, gs].rearrange("(i q) p c -> (q p) i c", q=2), w=[("Bn", p)], slow=True)
                            for p, nm in enumerate(("ssm_c_re", "ssm_c_im")):
                                tr.dma("sp", Cn[p][:, sl, :], I[nm][l, d, gs].rearrange("(i q) h p -> (q h) i p", q=2), w=[("Cn", p)], slow=True)
                        BBD = [Ap(f"BBD{p}", [128, 8, 64]) for p in range(2)]
                        Cpt = [Ap(f"Cpt{p}", [128, 8, 64]) for p in range(2)]
                        cbd = [Ap(f"cbd{i}", [32, 128]) for i in range(2)]
                        for p in range(2):
                            dve(lambda e, p=p: e.memset(BBD[p][:], 0.0), [], [("BBD", p)])
                            dve(lambda e, p=p: e.memset(Cpt[p][:], 0.0), [], [("Cpt", p)])
                            for ddl in range(8):
                                par = ddl % 2
                                c0 = par * 32
                                dve(lambda e, p=p, ddl=ddl, c0=c0: e.tensor_copy(out=BBD[p][0:64, ddl, c0:c0 + 16], in_=Bn[p][0:64, ddl, :]),
                                    [("Bn", p), ("BBD", p)], [("BBD", p)])
                                dve(lambda e, p=p, ddl=ddl, c0=c0: e.tensor_copy(out=BBD[p][64:128, ddl, c0 + 16:c0 + 32], in_=Bn[p][64:128, ddl, :]),
                                    [("Bn", p), ("BBD", p)], [("BBD", p)])
                            for ddl in range(8):
                                cb_, ck = cbd[ddl % 2], ("cbd", ddl % 2)
                                dve(lambda e, p=p, ddl=ddl, cb_=cb_: e.tensor_scalar_mul(out=cb_[:, 0:64], in0=Cn[p][:, ddl, :], scalar1=mskE[:, 0:1]),
                                    [("Cn", p), "mskE"], [ck])
                                dve(lambda e, p=p, ddl=ddl, cb_=cb_: e.tensor_scalar_mul(out=cb_[:, 64:128], in0=Cn[p][:, ddl, :], scalar1=mskO[:, 0:1]),
                                    [("Cn", p), "mskO", ck], [ck])
                                mm(PS[p][:, ddl * 32:(ddl + 1) * 32], cb_[:, :], ident[0:32, 0:32], True, True, [ck, "ident"], [("ps", p)])
                            for ddl in range(8):
                                c0 = (ddl % 2) * 32
                                fn = lambda e, p=p, ddl=ddl, c0=c0: e.activation(out=Cpt[p][:, ddl, c0:c0 + 32], in_=PS[p][:, ddl * 32:(ddl + 1) * 32],
                                                                               func=AF.Copy, scale=(1.0 if p == 0 else -1.0))
                                act(fn, [("ps", p), ("Cpt", p)], [("Cpt", p)])
                        VB = [Ap(f"VB{p}", [128, 4, 8, 64]) for p in range(2)]
                        vt = Ap("vt", [128, 8, 64])
                        vt2 = Ap("vt2", [128, 8, 64])
                        wo0, wo1 = Ap("wo0", [128, 8, 64]), Ap("wo1", [128, 8, 64])
                        TPp = lambda o, a, b_, op, r, w: pool(lambda e: e.tensor_tensor(out=o, in0=a, in1=b_, op=op), r, w)
                        nb_ = 0
                        for d in range(2):
                            for ip in range(4):
                                dd = d * 8 + ct * 4 + ip
                                ddl = d * 4 + ip
                                br, bi = bc(BBD[0][:, ddl, :], S3, 1), bc(BBD[1][:, ddl, :], S3, 1)
                                cr, ci = bc(cbr[:, dd, :], S3, 2), bc(cbi[:, dd, :], S3, 2)
                                rk = [("BBD", 0), ("BBD", 1), "cbr", "cbi"]
                                TT(VB[0][:, ip], br, cr, ALU.mult, rk, [("VB", 0, ip)])
                                TT(vt[:], bi, ci, ALU.mult, rk, ["vt"])
                                TT(VB[0][:, ip], VB[0][:, ip], vt[:], ALU.subtract, [("VB", 0, ip), "vt"], [("VB", 0, ip)])
                                TT(VB[1][:, ip], br, ci, ALU.mult, rk, [("VB", 1, ip)])
                                TT(vt[:], bi, cr, ALU.mult, rk, ["vt"])
                                TT(VB[1][:, ip], VB[1][:, ip], vt[:], ALU.add, [("VB", 1, ip), "vt"], [("VB", 1, ip)])
                                Cr, nCi = bc(Cpt[0][:, ddl, :], S3, 1), bc(Cpt[1][:, ddl, :], S3, 1)
                                pr, pi_ = bc(pwr[:, dd, 1:9], S3, 2), bc(pwi[:, dd, 1:9], S3, 2)
                                rk2 = [("Cpt", 0), ("Cpt", 1), "pwr", "pwi"]
                                TPp(wo0[:], Cr, pr, ALU.mult, rk2, ["wo0"])
                                TPp(vt2[:], nCi, pi_, ALU.mult, rk2, ["vt2"])
                                TPp(WoutAll[:, ddl, :, 0, :], wo0[:], vt2[:], ALU.add, ["wo0", "vt2"], [("Wout", ddl)])
                                TPp(wo1[:], nCi, pr, ALU.mult, rk2, ["wo1"])
                                TPp(vt2[:], Cr, pi_, ALU.mult, rk2, ["vt2"])
                                TPp(WoutAll[:, ddl, :, 1, :], wo1[:], vt2[:], ALU.subtract, ["wo1", "vt2"], [("Wout", ddl)])
                            for e_ in range(8):
                                ps, pk = PS[2 + nb_ % 4], ("ps", 2 + nb_ % 4)
                                nb_ += 1
                                for part in range(2):
                                    for ip in range(4):
                                        u, par = ip // 2, ip % 2
                                        rg = part * 2 + par
                                        pe(lambda e, ps=ps, rg=rg, ip=ip, u=u, part=part, e_=e_: e.matmul(
                                            ps[u * 64:(u + 1) * 64, rg * 128:(rg + 1) * 128], lhsT=VB[part][:, ip, e_, :],
                                            rhs=ident[:, :], start=True, stop=True), [("VB", part, ip), "ident"], [pk])
                                r0 = (d * 8 + e_) * 4
                                act(lambda e, ps=ps, r0=r0: e.activation(out=WinAll[:, r0:r0 + 4, :].rearrange("p a b -> p (a b)"), in_=ps[:, :], func=AF.Copy),
                                    [pk], ["WinAll"])
                            for i4 in range(2):
                                ps, pk = PS[2 + nb_ % 4], ("ps", 2 + nb_ % 4)
                                nb_ += 1
                                for ii in range(4):
                                    e_ = i4 * 4 + ii
                                    for u in range(2):
                                        n_t = 0
                                        for par in range(2):
                                            ip = u * 2 + par
                                            ddl = d * 4 + ip
                                            for part in range(2):
                                                n_t += 1
                                                pe(lambda e, ps=ps, ii=ii, u=u, ip=ip, part=part, ddl=ddl, e_=e_, n_t=n_t: e.matmul(
                                                    ps[u * 64:(u + 1) * 64, ii * 128 + u * 64: ii * 128 + (u + 1) * 64],
                                                    lhsT=VB[part][:, ip, e_, :], rhs=Cpt[part][:, ddl, :], start=(n_t == 1), stop=(n_t == 4)),
                                                   [("VB", part, ip), ("Cpt", part)], [pk])
                                for u in range(2):
                                    src = ps[u * 64:(u + 1) * 64, :].rearrange("p (a b) -> p a b", b=128)[:, :, u * 64:(u + 1) * 64]
                                    s0 = d * 8 + i4 * 4
                                    dve(lambda e, src=src, u=u, s0=s0: e.tensor_copy(
                                        out=KK[u * 64:(u + 1) * 64, s0:s0 + 4, u * 64:(u + 1) * 64], in_=src), [pk, "KK"], ["KK"])
                        dve(lambda e: e.tensor_tensor(out=KK[:, 0, :], in0=KK[:, 0, :], in1=KK[:, 8, :], op=ALU.add), ["KK"], ["KK"])
                        dve(lambda e, ct=ct: e.scalar_tensor_tensor(out=KK[:, 0, :], in0=identb[:], scalar=ssd[:, l, ct:ct + 1],
                                                                   in1=KK[:, 0, :], op0=ALU.mult, op1=ALU.add), ["KK", "identb", "ssd"], ["KK"])
                    tr.barrier()
                    with ExitStack() as mst:
                        Am = lambda name, shape, dt=F32: sb(name, shape, dt, mst)
                        Hs = Am("Hs", [128, 4, 2, 2, NCOL], BF16)
                        cosT, sinT = Am("cosT", [128, NT_]), Am("sinT", [128, NT_])
                        kiw = Am("kiw", [128, NT_], I32)
                        Sre, Sim = Am("Sre", [128, NT_]), Am("Sim", [128, NT_])
                        Gr, Gi = Am("Gr", [128, NT_]), Am("Gi", [128, NT_])
                        u1, u2 = Am("u1", [128, NT_]), Am("u2", [128, NT_])
                        u3, u4 = Am("u3", [128, NT_]), Am("u4", [128, NT_])
                        kiw2 = Am("kiw2", [128, NT_], I32)
                        TP = lambda o, a, b_, op, r, w: pool(lambda e: e.tensor_tensor(out=o, in0=a, in1=b_, op=op), r, w)
                        CR = [(0, NBLK), (NBLK, NBLK + 32), (NBLK + 32, NBLK + 64)]
                        for ip in range(4):
                            u, par = ip // 2, ip % 2
                            for d in range(2):
                                dd = d * 8 + ct * 4 + ip
                                for (tab, off, key, E_, yw_, ki_, kf_, sfx) in ((sinT, 0.0, "sinT", dve, u1, kiw, u2, ("u1", "kiw", "u2")),
                                                                               (cosT, 0.25, "cosT", pool, u3, kiw2, u4, ("u3", "kiw2", "u4"))):
                                    E_(lambda e, off=off, dd=dd, yw_=yw_, d=d: e.tensor_scalar(out=yw_[:], in0=iota[:, d, :], scalar1=ysn[:, dd, 8:9], scalar2=off,
                                                                                             op0=ALU.mult, op1=ALU.add), ["iota", "ysn"], [sfx[0]])
                                    E_(lambda e, yw_=yw_, ki_=ki_: e.tensor_copy(out=ki_[:], in_=yw_[:]), [sfx[0]], [sfx[1]])
                                    E_(lambda e, kf_=kf_, ki_=ki_: e.tensor_copy(out=kf_[:], in_=ki_[:]), [sfx[1]], [sfx[2]])
                                    E_(lambda e, yw_=yw_, kf_=kf_: e.tensor_tensor(out=yw_[:], in0=yw_[:], in1=kf_[:], op=ALU.subtract), [sfx[0], sfx[2]], [sfx[0]])
                                    E_(lambda e, yw_=yw_: e.tensor_scalar(out=yw_[:], in0=yw_[:], scalar1=0.49999, scalar2=-0.49999, op0=ALU.min, op1=ALU.max), [sfx[0]], [sfx[0]])
                                    act(lambda e, tab=tab, yw_=yw_: e.activation(out=tab[:], in_=yw_[:], func=AF.Sin, scale=TWO_PI), [sfx[0]], [key])
                                for si, (off, nb, is_s, qi) in enumerate(SEQS):
                                    zk = [("za", ct, bb) for bb in range(off // T, (off + 8 * nb - 1) // T + 1)]
                                    for part in range(2):
                                        if is_s:
                                            dst, dkey = PS[part][:, 0:nb], ("ps", part)
                                        else:
                                            dst, dkey = PS[2 + part][:, qi * 32:qi * 32 + 32], ("ps", 2 + part)
                                        for s in range(8):
                                            e_ = 7 - s if d == 0 else s
                                            rg = ((d * 8 + e_) * 2 + part) * 2 + par
                                            mm(dst, WinAll[u * 64:(u + 1) * 64, rg, :],
                                               zaV[u * 64:(u + 1) * 64, ct, off + s:off + 8 * nb:8], s == 0, s == 7, ["WinAll"] + zk, [dkey])
                                act(lambda e: e.activation(out=Sre[:, 0:NBLK], in_=PS[0][:, 0:NBLK], func=AF.Copy), [("ps", 0)], ["Sre"])
                                act(lambda e: e.activation(out=Sim[:, 0:NBLK], in_=PS[1][:, 0:NBLK], func=AF.Copy), [("ps", 1)], ["Sim"])
                                act(lambda e: e.activation(out=Sre[:, NBLK:NBLK + 64], in_=PS[2][:, 0:64], func=AF.Copy), [("ps", 2), "Sre"], ["Sre"])
                                act(lambda e: e.activation(out=Sim[:, NBLK:NBLK + 64], in_=PS[3][:, 0:64], func=AF.Copy), [("ps", 3), "Sim"], ["Sim"])
                                TT(u1[:], Sre[:], cosT[:], ALU.mult, ["Sre", "cosT"], ["u1"])
                                TT(u2[:], Sim[:], sinT[:], ALU.mult, ["Sim", "sinT"], ["u2"])
                                TT(Gr[:], u1[:], u2[:], ALU.add, ["u1", "u2"], ["Gr"])
                                TP(u3[:], Sim[:], cosT[:], ALU.mult, ["Sim", "cosT"], ["u3"])
                                TP(u4[:], Sre[:], sinT[:], ALU.mult, ["Sre", "sinT"], ["u4"])
                                TP(Gi[:], u3[:], u4[:], ALU.subtract, ["u3", "u4"], ["Gi"])
                                for si, (off, nb, is_s, qi) in enumerate(SEQS):
                                    lo, hi_ = CR[si]
                                    rv = slice(lo, hi_) if d == 0 else slice(hi_ - 1, (lo - 1) if lo > 0 else None, -1)
                                    r8 = mag[:, dd, 8:9].to_broadcast([128, nb])
                                    for (G, g0, gk) in ((Gr, g0r, "Gr"), (Gi, g0i, "Gi")):
                                        init = g0[:, dd:dd + 1] if is_s else 0.0
                                        dve(lambda e, G=G, init=init, r8=r8, rv=rv: e.tensor_tensor_scan(
                                            out=G[:, rv], data0=r8, data1=G[:, rv], initial=init, op0=ALU.mult, op1=ALU.add),
                                            [gk, "mag", "g0r", "g0i"], [gk])
                                hk = ("Hs", ip, d)
                                TT(u1[:], Gr[:], cosT[:], ALU.mult, ["Gr", "cosT"], ["u1"])
                                TT(u2[:], Gi[:], sinT[:], ALU.mult, ["Gi", "sinT"], ["u2"])
                                TP(u3[:], Gr[:], sinT[:], ALU.mult, ["Gr", "sinT"], ["u3"])
                                TP(u4[:], Gi[:], cosT[:], ALU.mult, ["Gi", "cosT"], ["u4"])
                                for si, (off, nb, is_s, qi) in enumerate(SEQS):
                                    base = HB[si]
                                    lo, hi_ = CR[si]
                                    c0 = base + 1 if d == 0 else base
                                    TT(Hs[:, ip, d, 0, c0:c0 + nb], u1[:, lo:hi_], u2[:, lo:hi_], ALU.subtract, ["u1", "u2"], [hk])
                                    TP(Hs[:, ip, d, 1, c0:c0 + nb], u3[:, lo:hi_], u4[:, lo:hi_], ALU.add, ["u3", "u4"], [hk])
                                    if not is_s:
                                        fc = hi_ - 1 if d == 0 else lo
                                        TT(fin[:, qi, dd, 0:1], u1[:, fc:fc + 1], u2[:, fc:fc + 1], ALU.subtract, ["u1", "u2"], ["fin"])
                                        TP(fin[:, qi, dd, 1:2], u3[:, fc:fc + 1], u4[:, fc:fc + 1], ALU.add, ["u3", "u4"], ["fin"])
                                    cb0 = base if d == 0 else base + nb
                                    if is_s:
                                        dve(lambda e, cb0=cb0, ip=ip, d=d, dd=dd: e.tensor_copy(out=Hs[:, ip, d, 0, cb0:cb0 + 1], in_=h0re[:, dd:dd + 1]), ["h0re"], [hk])
                                        dve(lambda e, cb0=cb0, ip=ip, d=d, dd=dd: e.tensor_copy(out=Hs[:, ip, d, 1, cb0:cb0 + 1], in_=h0im[:, dd:dd + 1]), ["h0im"], [hk])
                                    else:
                                        dve(lambda e, cb0=cb0, ip=ip, d=d: e.memset(Hs[:, ip, d, :, cb0:cb0 + 1], 0.0), [], [hk])
                        for si, (off, nb, is_s, qi) in enumerate(SEQS):
                            base = HB[si]
                            zk = [("za", ct, bb) for bb in range(off // T, (off + 8 * nb - 1) // T + 1)]
                            for s in range(8):
                                ps, pk = PS[2 + npo % 4], ("ps", 2 + npo % 4)
                                npo += 1
                                for s2 in range(8):
                                    mm(ps[:, 0:nb], KK[:, (s - s2) if s >= s2 else 8 + (s2 - s), :], zaV[:, ct, off + s2:off + 8 * nb:8], s2 == 0, False, ["KK"] + zk, [pk])
                                n_t = 0
                                for ip in range(4):
                                    u = ip // 2
                                    for d in range(2):
                                        ddl = d * 4 + ip
                                        e_ = s + 1 if d == 0 else 8 - s
                                        c0 = base if d == 0 else base + 1
                                        for part in range(2):
                                            n_t += 1
                                            mm(ps[u * 64:(u + 1) * 64, 0:nb], WoutAll[:, ddl, e_ - 1, part, :], Hs[:, ip, d, part, c0:c0 + nb],
                                               False, n_t in (8, 16), [("Wout", ddl), ("Hs", ip, d)], [pk])
                                act(lambda e, ps=ps, nb=nb, off=off, s=s, ct=ct: e.activation(
                                    out=ya[:, ct, off + s:off + 8 * nb:8], in_=ps[:, 0:nb], func=AF.Gelu_apprx_tanh),
                                    [pk], [("ya", ct, bb) for bb in range(off // T, (off + 8 * nb - 1) // T + 1)])
                    tr.barrier()
                for qi in range(2):
                    for d in range(2):
                        for part, nm in enumerate(("nre", "nim")):
                            tr.dma("sp", O[nm][qi, l, d].rearrange("(i q) p -> (q p) i", q=2), fin[:, qi, d * 8:(d + 1) * 8, part],
                                   r=["fin"], slow=True)

        def fnet_pass(l, zbV, zc, yb):
            wmi = I["w_mix_in"][l].rearrange("(k p) n -> p k n", p=128)
            NTs = Ls // 128
            with ExitStack() as st:
                A = lambda name, shape, dt=F32: sb(name, shape, dt, st)
                h = A("hmix2", [128, KC, T], BF16)
                wbc = A("wbc", [128, KC, 512], BF16)
                tr.dma("pool", wbc[:], wmi[:, :, 256:768], w=["wbc"])
                zck = [("zc", b) for b in range(NB)]
                dve(lambda e: e.memset(zc[:], 0.0), [], zck)
                for b in range(NB):
                    bg_issue(2)
                    make_h(h, b * T, T, l, 1)
                    hk = [("h", k) for k in range(KC)]
                    for q in range(4):
                        i = b * 4 + q
                        ps, pk = PS[q % 2], ("ps", q % 2)
                        for k in range(KC):
                            mm(ps[:, 0:256], h[:, k, q * 128:(q + 1) * 128], wbc[:, k, 0:256], k == 0, k == KC - 1, ["wbc", ("h", k)], [pk])
                        if q % 2 == 0:
                            act(lambda e, ps=ps, i=i: e.activation(out=zbV[:, i, :], in_=ps[:, 0:256], func=AF.Copy), [pk], [("zb", i)])
                        else:
                            dve(lambda e, ps=ps, i=i: e.tensor_copy(out=zbV[:, i, :], in_=ps[:, 0:256]), [pk], [("zb", i)])
                    for ct in range(2):
                        ps, pk = PS[2 + ct], ("ps", 2 + ct)
                        for k in range(KC):
                            mm(ps[:, :], wbc[:, k, 256 + ct * 128:256 + (ct + 1) * 128], h[:, k, :], k == 0, k == KC - 1, ["wbc", ("h", k)], [pk])
                        if b < NBS:
                            act(lambda e, ps=ps, b=b, ct=ct: e.activation(out=zc[:, ct, 8 + b * T:8 + (b + 1) * T], in_=ps[:, :], func=AF.Copy),
                                [pk], [("zc", b)])
                        else:
                            act(lambda e, ps=ps, ct=ct: e.activation(out=zc[:, ct, POFF[1]:POFF[1] + Lp], in_=ps[:, 0:Lp], func=AF.Copy), [pk], [("zc", b)])
                            dve(lambda e, ps=ps, ct=ct: e.tensor_copy(out=zc[:, ct, POFF[2]:POFF[2] + Lp], in_=ps[:, Lp:2 * Lp]), [pk], [("zc", b)])
                FW = A("FW", [128, 2, 128])
                c64t, s64t = A("c64t", [128, 128]), A("s64t", [128, 128])
                Wcs = A("Wcs", [128, 2, 3, 2, 128], BF16)
                dve(lambda e: e.memset(FW[:], 0.0), [], ["FW"])
                tr.dma("sp", c64t[:], I["c64"], w=["c64t"])
                tr.dma("sp", s64t[:], I["s64"], w=["s64t"])
                for g in range(4):
                    p0 = (g % 2) * 64
                    tr.dma("sp", FW[p0:p0 + 64, g // 2, p0:p0 + 64], I["fnet_w"][l, g], r=["FW"], w=["FW"])
                for ct in range(2):
                    mm(PS[4][:, ct * 128:(ct + 1) * 128], c64t[:, :], FW[:, ct, :], True, True, ["c64t", "FW"], [("ps", 4)])
                    mm(PS[5][:, ct * 128:(ct + 1) * 128], s64t[:, :], FW[:, ct, :], True, True, ["s64t", "FW"], [("ps", 5)])
                for Li, L_ in enumerate((Ls, Lp)):
                    nrm = 1.0 / math.sqrt(64.0 * L_)
                    act(lambda e, Li=Li, nrm=nrm: e.activation(out=Wcs[:, Li, 0].rearrange("p a b -> p (a b)"), in_=PS[4][:, 0:256], func=AF.Copy, scale=nrm),
                        [("ps", 4)], ["Wcs"])
                    act(lambda e, Li=Li, nrm=nrm: e.activation(out=Wcs[:, Li, 1].rearrange("p a b -> p (a b)"), in_=PS[5][:, 0:256], func=AF.Copy, scale=-nrm),
                        [("ps", 5)], ["Wcs"])
                    act(lambda e, Li=Li, nrm=nrm: e.activation(out=Wcs[:, Li, 2].rearrange("p a b -> p (a b)"), in_=PS[5][:, 0:256], func=AF.Copy, scale=nrm),
                        [("ps", 5)], ["Wcs"])
                dsl = [A(f"dft{i}", [128, 2, 2, T], BF16) for i in range(4)]
                NP_ = 256
                PWBD = A("PWBD", [128, 2, 128], BF16)
                Ein = A("Ein", [128, 2, NP_ + 16], BF16)
                hsv = A("hsv", [128, 2, 8], BF16)
                sA, sB = A("sA", [128, 2, NP_ + 16]), A("sB", [128, 2, NP_ + 16])
                pw = A("pw", [128, 2, NP_])
                pb = A("pb", [128, 2, NP_], BF16)
                et = A("et", [128, 2, 8])
                dve(lambda e: e.memset(PWBD[:], 0.0), [], ["PWBD"])
                for g in range(4):
                    p0 = (g % 2) * 64
                    tr.dma("pool", PWBD[p0:p0 + 64, g // 2, p0:p0 + 64], I["pool_w"][l, g], r=["PWBD"], w=["PWBD"])

                def pool_seg(t0):
                    n = NP_
                    b = t0 // T
                    if t0 < Ls:
                        col0, left, right = 8 + t0, t0 == 0, t0 + n == Ls
                    else:
                        col0, left, right = POFF[1 + (t0 - Ls) // Lp], True, True
                    zk = [("zc", bb) for bb in range(max(0, b - 1), min(NB, b + 2))]
                    if left:
                        pool(lambda e: e.tensor_copy(out=Ein[:, :, :], in_=zc[:, :, col0 - 8:col0 + n + 8]), zk, ["Ein"])
                    else:
                        pool(lambda e: e.tensor_copy(out=Ein[:, :, 0:8], in_=hsv[:, :, :]), ["hsv"], ["Ein"])
                        pool(lambda e: e.tensor_copy(out=Ein[:, :, 8:n + 16], in_=zc[:, :, col0:col0 + n + 8]), zk + ["Ein"], ["Ein"])
                    if not right:
                        pool(lambda e: e.tensor_copy(out=hsv[:, :, :], in_=zc[:, :, col0 + n - 8:col0 + n]), zk, ["hsv"])
                    E = Ein
                    dve(lambda e: e.tensor_tensor(out=sA[:, :, 1:n + 16], in0=E[:, :, 0:n + 15], in1=E[:, :, 1:n + 16], op=ALU.add), ["Ein"], ["sA"])
                    for c in range(2):
                        dve(lambda e, c=c: e.scalar_tensor_tensor(out=pw[:, c, :], in0=sA[:, c, 8:n + 8], scalar=poolm[:, c, 0:1], in1=E[:, c, 8:n + 8],
                                                                 op0=ALU.mult, op1=ALU.subtract), ["sA", "poolm", "Ein"], [("pw", c)])
                    dve(lambda e: e.tensor_tensor(out=sB[:, :, 2:n + 15], in0=sA[:, :, 1:n + 14], in1=sA[:, :, 3:n + 16], op=ALU.add), ["sA"], ["sB"])
                    dve(lambda e: e.scalar_tensor_tensor(out=pw[:, 0, :], in0=sB[:, 0, 8:n + 8], scalar=poolm[:, 0, 1:2], in1=pw[:, 0, :],
                                                         op0=ALU.mult, op1=ALU.add), ["sB", "poolm", ("pw", 0)], [("pw", 0)])
                    dve(lambda e: e.tensor_tensor(out=sA[:, 1, 4:n + 13], in0=sB[:, 1, 2:n + 11], in1=sB[:, 1, 6:n + 15], op=ALU.add), ["sB", "sA"], ["sA"])
                    dve(lambda e: e.scalar_tensor_tensor(out=pw[:, 1, :], in0=sA[:, 1, 8:n + 8], scalar=poolm[:, 1, 2:3], in1=pw[:, 1, :],
                                                         op0=ALU.mult, op1=ALU.add), ["sA", "poolm", ("pw", 1)], [("pw", 1)])
                    dve(lambda e: e.tensor_tensor(out=sB[:, 1, 8:n + 8], in0=sA[:, 1, 4:n + 4], in1=sA[:, 1, 12:n + 12], op=ALU.add), ["sA", "sB"], ["sB"])
                    dve(lambda e: e.scalar_tensor_tensor(out=pw[:, 1, :], in0=sB[:, 1, 8:n + 8], scalar=poolm[:, 1, 3:4], in1=pw[:, 1, :],
                                                         op0=ALU.mult, op1=ALU.add), ["sB", "poolm", ("pw", 1)], [("pw", 1)])
                    for (flag, cs_, corr, ck) in ((left, 0, corrL, "corrL"), (right, n - 8, corrR, "corrR")):
                        if not flag:
                            continue
                        zz = E[:, :, 8 + cs_:16 + cs_]
                        pk2 = [("pw", 0), ("pw", 1)]
                        dve(lambda e, cs_=cs_, zz=zz: e.tensor_tensor(out=et[:], in0=pw[:, :, cs_:cs_ + 8], in1=zz, op=ALU.add), pk2 + ["Ein"], ["et"])
                        dve(lambda e, corr=corr: e.tensor_tensor(out=et[:], in0=et[:], in1=corr[:], op=ALU.mult), ["et", ck], ["et"])
                        dve(lambda e, cs_=cs_, zz=zz: e.tensor_tensor(out=pw[:, :, cs_:cs_ + 8], in0=et[:], in1=zz, op=ALU.subtract), ["et", "Ein"] + pk2, pk2)
                    pool(lambda e: e.tensor_copy(out=pb[:, :, :], in_=pw[:, :, :]), [("pw", 0), ("pw", 1)], ["pb"])
                    for ct in range(2):
                        mm(PS[4 + ct][:, 0:n], PWBD[:, ct, :], pb[:, ct, :], True, True, ["PWBD", "pb"], [("ps", 4 + ct)])
                        pool_evac(ct, col0, n, b)

                def pool_evac(ct, col0, n, b):
                    dve(lambda e: e.tensor_scalar_mul(out=zc[:, ct, col0:col0 + n], in0=PS[4 + ct][:, 0:n], scalar1=psc[:, l, ct:ct + 1]),
                        [("ps", 4 + ct), "psc", "Ein", "hsv"], [("zc", b)])

                PSEGS = list(range(0, Ltot, NP_))
                dP = A("dftP", [128, 2, 2, Lp], BF16)
                Pb, Qb = A("Pb", [128, 2, T], BF16), A("Qb", [128, 2, T], BF16)
                tr.dma("sp", dP[:].rearrange("p j c k -> p j (c k)"), I["dftP"].rearrange("j p c k -> p j (c k)"), w=["dP"])
                nd = 0
                SYM = NKB >= 2 and NKB % 2 == 0
                NKH = NKB // 2 if SYM else NKB
                if SYM:
                    alt = A("alt", [128, 2], BF16)
                    tr.dma("sp", alt[:, 0:1], I["dftS"][NKB // 2, 0, :, 0, 0:1], w=["alt"], slow=True)
                    for i in range(NTs):
                        for ct in range(2):
                            mm(PS[ct][:, 0:1], zbV[:, i, ct * 128:(ct + 1) * 128], alt[:, 0:1], i == 0, i == NTs - 1, [("zb", i), "alt"], [("ps", ct)])
                    for ct in range(2):
                        act(lambda e, ct=ct: e.activation(out=Pb[:, ct, 0:1], in_=PS[ct][:, 0:1], func=AF.Copy), [("ps", ct)], [("Pb", ct)])
                        mm(PS[4 + ct][:, 0:1], Wcs[:, 0, 0, ct, :], Pb[:, ct, 0:1], True, True, ["Wcs", ("Pb", ct)], [("ps", 4 + ct)])
                        act(lambda e, ct=ct: e.activation(out=yb[:, ct, Ls // 2:Ls // 2 + 1], in_=PS[4 + ct][:, 0:1], func=AF.Copy), [("ps", 4 + ct)], [("yb", ct, NKB // 2)])

                for kb in list(range(NKH)) + [NKB, NKB + 1]:
                    if kb < NKB:
                        n, Li = T, 0
                        for ig in range(NTs // 2):
                            sl_, dk = dsl[nd % 4], ("dft", nd % 4)
                            nd += 1
                            tr.dma("sp", sl_[:].rearrange("p i c k -> p i (c k)"),
                                   I["dftS"][kb, ig * 2:(ig + 1) * 2].rearrange("i p c k -> p i (c k)"), w=[dk])
                            if ig % 4 == 3 and PSEGS:
                                pool_seg(PSEGS.pop(0))
                            for i4 in range(2):
                                i = ig * 2 + i4
                                for ct in range(2):
                                    mm(PS[ct][:, :], zbV[:, i, ct * 128:(ct + 1) * 128], sl_[:, i4, 0, :], i == 0, i == NTs - 1, [("zb", i), dk], [("ps", ct)])
                                    mm(PS[2 + ct][:, :], zbV[:, i, ct * 128:(ct + 1) * 128], sl_[:, i4, 1, :], i == 0, i == NTs - 1, [("zb", i), dk], [("ps", 2 + ct)])
                        c0 = kb * T
                        dkeys = lambda ct: [("yb", ct, kb)]
                    else:
                        qi = kb - NKB
                        n, Li = Lp, 1
                        for j in range(2):
                            i = NTs + 2 * qi + j
                            for ct in range(2):
                                mm(PS[ct][:, 0:n], zbV[:, i, ct * 128:(ct + 1) * 128], dP[:, j, 0, :], j == 0, j == 1, [("zb", i), "dP"], [("ps", ct)])
                                mm(PS[2 + ct][:, 0:n], zbV[:, i, ct * 128:(ct + 1) * 128], dP[:, j, 1, :], j == 0, j == 1, [("zb", i), "dP"], [("ps", 2 + ct)])
                        c0 = Ls + qi * Lp
                        dkeys = lambda ct: [("yb", ct, NBS)]
                        while qi == 1 and PSEGS:
                            pool_seg(PSEGS.pop(0))
                    for ct in range(2):
                        act(lambda e, ct=ct, n=n: e.activation(out=Pb[:, ct, 0:n], in_=PS[ct][:, 0:n], func=AF.Copy), [("ps", ct)], [("Pb", ct)])
                        dve(lambda e, ct=ct, n=n: e.tensor_copy(out=Qb[:, ct, 0:n], in_=PS[2 + ct][:, 0:n]), [("ps", 2 + ct)], [("Qb", ct)])
                    for ct in range(2):
                        mm(PS[4 + ct][:, 0:n], Wcs[:, Li, 0, ct, :], Pb[:, ct, 0:n], True, False, ["Wcs", ("Pb", ct)], [("ps", 4 + ct)])
                        mm(PS[4 + ct][:, 0:n], Wcs[:, Li, 1, ct, :], Qb[:, ct, 0:n], False, True, ["Wcs", ("Qb", ct)], [("ps", 4 + ct)])
                        if ct == 0:
                            act(lambda e, ct=ct, n=n, c0=c0: e.activation(out=yb[:, ct, c0:c0 + n], in_=PS[4 + ct][:, 0:n], func=AF.Copy), [("ps", 4 + ct)], dkeys(ct))
                        else:
                            dve(lambda e, ct=ct, n=n, c0=c0: e.tensor_copy(out=yb[:, ct, c0:c0 + n], in_=PS[4 + ct][:, 0:n]), [("ps", 4 + ct)], dkeys(ct))
                        if SYM and kb < NKB:
                            mm(PS[6 + ct][:, 0:n], Wcs[:, Li, 0, ct, :], Pb[:, ct, 0:n], True, False, ["Wcs", ("Pb", ct)], [("ps", 6 + ct)])
                            mm(PS[6 + ct][:, 0:n], Wcs[:, Li, 2, ct, :], Qb[:, ct, 0:n], False, True, ["Wcs", ("Qb", ct)], [("ps", 6 + ct)])
                            j0 = 1 if kb == 0 else 0
                            hi = Ls - T * kb - j0
                            dsl_ = slice(hi, Ls - T * kb - T, -1)
                            mk = [("yb", ct, NKB - 1 - kb)] + ([("yb", ct, NKB - kb)] if kb >= 1 else [])
                            if ct == 0:
                                dve(lambda e, ct=ct, j0=j0, dsl_=dsl_: e.tensor_copy(out=yb[:, ct, dsl_], in_=PS[6 + ct][:, j0:T]), [("ps", 6 + ct)], mk)
                            else:
                                act(lambda e, ct=ct, j0=j0, dsl_=dsl_: e.activation(out=yb[:, ct, dsl_], in_=PS[6 + ct][:, j0:T], func=AF.Copy), [("ps", 6 + ct)], mk)

        def pass3(l, B1, ya, yb, zc):
            TB = 256
            wmi = I["w_mix_in"][l].rearrange("(k p) n -> p k n", p=128)
            wmo = I["w_mix_out"][l].rearrange("(k p) n -> p k n", p=128)
            with ExitStack() as st:
                A = lambda name, shape, dt=F32: sb(name, shape, dt, st)
                cv = [0]

                def carve(nm, n):
                    if cv[0] + n <= 2 * Ltot:
                        ap = B1[:, cv[0]:cv[0] + n]
                        cv[0] += n
                        return ap
                    return A(nm, [128, n], BF16)[:]

                wd = carve("wd", KC * 512).rearrange("p (k n) -> p k n", n=512)
                h = carve("h3", KC * TB).rearrange("p (k n) -> p k n", n=TB)
                yn0 = carve("yn", KC * TB).rearrange("p (k n) -> p k n", n=TB)
                yn1 = A("yn1", [128, KC, TB], BF16)
                YN = [yn0, yn1[:]]
                vn = carve("vn", 512)
                sqb = carve("sqb", 2 * TB).rearrange("p (k n) -> p k n", n=TB)
                ep = make_epi(st, TB)
                glw = A("glw", [128, 2, 256], BF16)
                SWT = A("SWT", [128, 4, 128], BF16)
                biasT = A("biasT", [128, 2, TB])
                wmall = A("wmall", [128, KC, D], BF16)
                ym = [A(f"ym{i}", [128, 2, TB]) for i in range(2)]
                sig = A("sig", [128, 2, TB])
                rs = A("rs", [128, TB])
                ug = A("ug", [128, 2, TB])
                vg, vsq = A("vg", [128, 512]), A("vsq", [128, 512])
                sm1, sm2, mn_, msq = A("sm1", [128, 8]), A("sm2", [128, 8]), A("mn_", [128, 8]), A("msq", [128, 8])
                tr.dma("pool", wd, wmi[:, :, 768:1280], w=["wd"])
                for k in range(KC):
                    tr.dma("pool", wmall[:, k, :], wmo[:, k, :], w=[("wmall", k)])
                tr.dma("pool", glw[:], I["ssm_glu_w"][l].rearrange("(c p) n -> p c n", p=128), w=["glw"])
                for g in range(4):
                    p0 = (g % 2) * 64
                    for rep in range(TB // 128):
                        tr.dma("sp", biasT[p0:p0 + 64, g // 2, rep * 128:(rep + 1) * 128],
                               I["sgu_b"][l, g:g + 1, :].to_broadcast([64, 128]), w=["biasT"], slow=True)
                SWn = vg[:].rearrange("p (g s) -> p g s", g=4)
                tr.dma("sp", SWn, I["sgu_w"][l].rearrange("g t s -> t g s"), w=["vg"])
                for g in range(4):
                    pe(lambda e, g=g: e.transpose(PS[0][:, g * 128:(g + 1) * 128], SWn[:, g, :], ident[:]), ["vg", "ident"], [("ps", 0)])
                act(lambda e: e.activation(out=SWT[:].rearrange("p a b -> p (a b)"), in_=PS[0][:, :], func=AF.Copy), [("ps", 0)], ["SWT"])
                nwm = 0

                def rms(m, srcs, rkeys, slot):
                    yn = YN[slot]
                    for ct in range(2):
                        act(lambda e, ct=ct: e.activation(out=sqb[:, ct, :], in_=srcs[ct], func=AF.Square), rkeys, [("sqb", ct)])
                    mm(PS[3][:, 0:TB], ones256[:], sqb[:, 0, :], True, False, ["ones256", ("sqb", 0)], [("ps", 3)])
                    mm(PS[3][:, 0:TB], ones256[:], sqb[:, 1, :], False, True, ["ones256", ("sqb", 1)], [("ps", 3)])
                    dve(lambda e: e.tensor_scalar_add(out=rs[:], in0=PS[3][:, 0:TB], scalar1=EPS), [("ps", 3)], ["rs"])
                    act(lambda e: e.activation(out=rs[:], in_=rs[:], func=AF.Ln), ["rs"], ["rs"])
                    act(lambda e: e.activation(out=rs[:], in_=rs[:], func=AF.Exp, scale=-0.5), ["rs"], ["rs"])
                    for ct in range(2):
                        dve(lambda e, ct=ct: e.scalar_tensor_tensor(out=yn[:, 2 * m + ct, :], in0=srcs[ct], scalar=mng[:, l, 2 * m + ct:2 * m + ct + 1],
                                                                   in1=rs[:], op0=ALU.mult, op1=ALU.mult), rkeys + ["rs", "mng"], [("yn", slot, 2 * m + ct)])

                def p3_front(hbk):
                    t0 = hbk * TB
                    n = TB
                    b = t0 // T
                    cond = 0 if t0 < Ls else 1
                    slot = hbk % 2
                    tsl = slice(t0, t0 + n)
                    bg_issue(1)
                    make_h(h, t0, n, l, 1)
                    y0 = ym[0]
                    for dt_ in range(2):
                        for ct in range(2):
                            mm(PS[0][:, dt_ * n:(dt_ + 1) * n], glw[:, ct, dt_ * 128:(dt_ + 1) * 128], ya[:, ct, tsl], ct == 0, ct == 1, ["glw", ("ya", ct, b)], [("ps", 0)])
                    for ct in range(2):
                        for k in range(KC):
                            mm(PS[1][:, ct * n:(ct + 1) * n], wd[:, k, ct * 128:(ct + 1) * 128], h[:, k, :], k == 0, k == KC - 1, ["wd", ("h", k)], [("ps", 1)])
                    for qq in range(2):
                        for k in range(KC):
                            mm(PS[2][:, qq * 256:(qq + 1) * 256], h[:, k, qq * 128:(qq + 1) * 128], wd[:, k, 256:512], k == 0, k == KC - 1, ["wd", ("h", k)], [("ps", 2)])
                    for dt_ in range(2):
                        act(lambda e, dt_=dt_: e.activation(out=sig[:, dt_, :], in_=PS[0][:, dt_ * n:(dt_ + 1) * n], func=AF.Sigmoid, bias=glb[:, l, dt_:dt_ + 1], scale=1.0),
                            [("ps", 0), "glb"], [("sig", dt_)])
                    for ct in range(2):
                        act(lambda e, ct=ct: e.activation(out=ug[:, ct, :], in_=PS[1][:, ct * n:(ct + 1) * n], func=AF.Gelu_apprx_tanh), [("ps", 1)], [("ug", ct)])
                    act(lambda e: e.activation(out=vg[:], in_=PS[2][:, :], func=AF.Gelu_apprx_tanh), [("ps", 2)], ["vg"])
                    act(lambda e: e.activation(out=vsq[:], in_=vg[:], func=AF.Square), ["vg"], ["vsq"])
                    yield
                    for dt_ in range(2):
                        dve(lambda e, dt_=dt_: e.tensor_tensor(out=y0[:, dt_, :], in0=ya[:, dt_, tsl], in1=sig[:, dt_, :], op=ALU.mult),
                            [("ya", dt_, b), ("sig", dt_)], [("ym", 0)])
                    vg3, vsq3 = vg[:].rearrange("p (g c) -> p g c", c=64), vsq[:].rearrange("p (g c) -> p g c", c=64)
                    dve(lambda e: e.tensor_reduce(out=sm1[:], in_=vg3, axis=AX.X, op=ALU.add), ["vg"], ["sm1"])
                    dve(lambda e: e.tensor_reduce(out=sm2[:], in_=vsq3, axis=AX.X, op=ALU.add), ["vsq"], ["sm2"])
                    dve(lambda e: e.tensor_scalar_mul(out=mn_[:], in0=sm1[:], scalar1=1.0 / 64), ["sm1"], ["mn_"])
                    dve(lambda e: e.tensor_tensor(out=msq[:], in0=mn_[:], in1=mn_[:], op=ALU.mult), ["mn_"], ["msq"])
                    dve(lambda e: e.scalar_tensor_tensor(out=sm2[:], in0=sm2[:], scalar=1.0 / 64, in1=msq[:], op0=ALU.mult, op1=ALU.subtract), ["sm2", "msq"], ["sm2"])
                    dve(lambda e: e.tensor_scalar_add(out=sm2[:], in0=sm2[:], scalar1=EPS), ["sm2"], ["sm2"])
                    act(lambda e: e.activation(out=sm2[:], in_=sm2[:], func=AF.Ln), ["sm2"], ["sm2"])
                    act(lambda e: e.activation(out=sm2[:], in_=sm2[:], func=AF.Exp, scale=-0.5), ["sm2"], ["sm2"])
                    yield
                    rms(0, [y0[:, 0, :], y0[:, 1, :]], [("ym", 0)], slot)
                    yield
                    dve(lambda e: e.tensor_tensor(out=vg3, in0=vg3, in1=bc(mn_[:], [128, 8, 64], 2), op=ALU.subtract), ["vg", "mn_"], ["vg"])
                    dve(lambda e: e.tensor_tensor(out=vn.rearrange("p (g c) -> p g c", c=64), in0=vg3, in1=bc(sm2[:], [128, 8, 64], 2), op=ALU.mult),
                        ["vg", "sm2"], ["vn"])
                    for qq in range(2):
                        for g in range(4):
                            p0 = (g % 2) * 64
                            mm(PS[0][p0:p0 + 64, (g // 2) * n + qq * 128:(g // 2) * n + (qq + 1) * 128], vn[:, qq * 256 + g * 64:qq * 256 + (g + 1) * 64], SWT[:, g, :],
                               True, True, ["vn", "SWT"], [("ps", 0)])
                    yield
                    rms(1, [yb[:, 0, tsl], yb[:, 1, tsl]], [("yb", 0, b), ("yb", 1, b)], slot)
                    yield
                    if t0 < Ls:
                        col0 = 8 + t0
                    else:
                        col0 = POFF[1 + (t0 - Ls) // Lp]
                    rms(2, [zc[:, 0, col0:col0 + n], zc[:, 1, col0:col0 + n]], [("zc", b)], slot)
                    yield
                    y3 = ym[1]
                    for ct in range(2):
                        dve(lambda e, ct=ct: e.tensor_tensor(out=y3[:, ct, :], in0=PS[0][:, ct * n:(ct + 1) * n], in1=biasT[:, ct, :], op=ALU.add),
                            [("ps", 0), "biasT"], [("ym", 1)])
                        dve(lambda e, ct=ct: e.tensor_tensor(out=y3[:, ct, :], in0=y3[:, ct, :], in1=ug[:, ct, :], op=ALU.mult),
                            [("ym", 1), ("ug", ct)], [("ym", 1)])
                    rms(3, [y3[:, 0, :], y3[:, 1, :]], [("ym", 1)], slot)

                def p3_back(hbk):
                    t0 = hbk * TB
                    n = TB
                    b = t0 // T
                    cond = 0 if t0 < Ls else 1
                    slot = hbk % 2
                    for c in range(KC):
                        po, pk = PS[4 + c % 2], ("ps", 4 + c % 2)
                        for ic in range(KC):
                            mm(po[:, 0:n], wmall[:, ic, c * 128:(c + 1) * 128], YN[slot][:, ic, :], ic == 0, ic == KC - 1, [("wmall", ic), ("yn", slot, ic)], [pk])
                        epi_chunk(ep, c, po, pk, t0, n, l, 1, cond)
                        yield
                    epi_finish(ep, t0, n, l, 1)

                NHB = Ltot // TB
                for _ in p3_front(0):
                    pass
                for hbk in range(NHB):
                    gb = p3_back(hbk)
                    gf = p3_front(hbk + 1) if hbk + 1 < NHB else iter(())
                    fa, ba = True, True
                    while fa or ba:
                        if fa:
                            try:
                                next(gf)
                            except StopIteration:
                                fa = False
                        if ba:
                            try:
                                next(gb)
                            except StopIteration:
                                ba = False

        MIX = cfg.get("mixer", True)
        for l in range(depth):
            ffn(l, 0, False)
            if MIX:
                mixer(l)
            ffn(l, 1, l == depth - 1)
        tr.finish()
    return nc


def _consts(Ls):
    bf = ml_dtypes.bfloat16
    c = {}
    c["ident"] = np.eye(128, dtype=np.float32)
    k = np.arange(64)
    ang = 2 * np.pi * np.outer(k, k) / 64.0
    c64 = np.zeros((128, 128), np.float32)
    s64 = np.zeros((128, 128), np.float32)
    for g in range(2):
        c64[g * 64:(g + 1) * 64, g * 64:(g + 1) * 64] = np.cos(ang)
        s64[g * 64:(g + 1) * 64, g * 64:(g + 1) * 64] = np.sin(ang)
    c["c64"], c["s64"] = c64, s64
    nbk = Ls // 8
    fwd = np.concatenate([np.arange(nbk), np.arange(32), np.arange(32)]).astype(np.float32)
    rev = np.concatenate([np.arange(nbk)[::-1], np.arange(32)[::-1], np.arange(32)[::-1]]).astype(np.float32)
    c["iota"] = np.ascontiguousarray(np.broadcast_to(np.stack([fwd, rev])[None], (128, 2, nbk + 64))).astype(np.float32)
    quarter = D // 4
    omega = (1.0 / (10000.0 ** (np.arange(quarter, dtype=np.float32) / np.float32(quarter)))).astype(np.float32)
    rows = Ls // 64
    ang_r = (np.arange(rows, dtype=np.float32)[:, None] * omega).astype(np.float32)
    ang_c = (np.arange(64, dtype=np.float32)[:, None] * omega).astype(np.float32)
    emb_r = np.concatenate([np.sin(ang_r), np.cos(ang_r)], -1)
    emb_c = np.concatenate([np.sin(ang_c), np.cos(ang_c)], -1)
    pos = np.concatenate([np.broadcast_to(emb_r[:, None], (rows, 64, D // 2)),
                          np.broadcast_to(emb_c[None], (rows, 64, D // 2))], -1)
    c["pos"] = np.ascontiguousarray(pos.reshape(rows * 64, D).astype(np.float32))

    def dft(L, kblk):
        t = np.arange(L, dtype=np.int64)
        m = np.outer(t, t) % L
        a = 2 * np.pi * m.astype(np.float64) / L
        cs = np.stack([np.cos(a), np.sin(a)], 1)
        nk = L // kblk
        out = cs.reshape(L // 128, 128, 2, nk, kblk).transpose(3, 0, 1, 2, 4)
        return np.ascontiguousarray(out.astype(np.float32).astype(bf))

    c["dftS"] = dft(Ls, T)
    c["dftP"] = dft(256, 256)[0]
    return c


_WKEYS = ["w_ada", "b_ada", "ffn_w_in", "ffn_w_out", "w_mix_in", "w_mix_out", "mix_norm_g", "ssm_lam_re",
          "ssm_lam_im", "ssm_log_dt", "ssm_b_re", "ssm_b_im", "ssm_c_re", "ssm_c_im", "ssm_d", "ssm_glu_w",
          "ssm_glu_b", "fnet_w", "pool_w", "pool_scale", "sgu_w", "sgu_b", "ln_g", "ln_b"]


def run(inputs, cfg, ncores):
    Ls, depth = cfg["Ls"], cfg["depth"]
    nc = build(cfg)
    cst = _consts(Ls)
    f = lambda a: np.ascontiguousarray(np.asarray(a, dtype=np.float32))
    W = {k: f(inputs[k]) for k in _WKEYS}
    in_maps = []
    for c in range(ncores):
        m = dict(W)
        m.update(cst)
        m["xs"] = f(inputs["x_sample"][c])
        m["xp"] = f(np.asarray(inputs["x_prompt"])[2 * c:2 * c + 2].reshape(512, D))
        m["cvec"] = f(np.stack([np.asarray(inputs["c"])[c], np.asarray(inputs["c_ctx"])]))
        m["st_re"] = f(np.asarray(inputs["state_s5_re"])[c])
        m["st_im"] = f(np.asarray(inputs["state_s5_im"])[c])
        in_maps.append(m)
    res = run_bass_kernel_spmd(nc, in_maps, core_ids=list(range(ncores)))
    R = res.results
    ys = np.stack([np.asarray(r["ys"], np.float32) for r in R])
    yp = np.concatenate([np.asarray(r["yp"], np.float32).reshape(2, 256, D) for r in R])
    nre = np.concatenate([np.asarray(r["nre"], np.float32) for r in R])
    nim = np.concatenate([np.asarray(r["nim"], np.float32) for r in R])
    return yp, ys, nre, nim


def kernel(**inputs):
    return run(inputs, {"Ls": 4096, "depth": 4}, 8)
```

```python
import math
from contextlib import ExitStack

import numpy as np
import ml_dtypes

import concourse.bass as bass
import concourse.mybir as mybir
from concourse.bass_utils import run_bass_kernel_spmd

F32 = mybir.dt.float32
F32R = mybir.dt.float32r
BF16 = mybir.dt.bfloat16
I32 = mybir.dt.int32
ALU = mybir.AluOpType
AF = mybir.ActivationFunctionType
AX = mybir.AxisListType

D = 1024
KC = 8
DFF = 2816
FC = 22
T = 512
NMOD = 9
ALPHA = 8.0 ** 0.25
EPS = 1e-5
EPS_LN = EPS / (ALPHA * ALPHA)
TWO_PI = 2.0 * math.pi


class TR:
    def __init__(self, nc, es):
        self.nc = nc
        self.eng = {"pe": nc.tensor, "act": nc.scalar, "dve": nc.vector, "pool": nc.gpsimd, "sp": nc.sync}
        self.semh = {}
        for e in ("pe", "act", "dve", "pool"):
            self.semh[e] = es.enter_context(nc.semaphore("c_" + e))
        self.cnt = {e: 0 for e in ("pe", "act", "dve", "pool")}
        self.waited = {e: {} for e in self.eng}
        self.last_w = {}
        self.readers = {}
        self.ndma = {"sp": 0, "pool": 0, "act": 0, "poolbg": 0}
        self.dma_k = {"sp": 8, "pool": 8, "act": 4, "poolbg": 16}
        self.dma_uses = {}
        for q, k in self.dma_k.items():
            for i in range(k):
                key = ("dma", q, i)
                self.semh[key] = es.enter_context(nc.semaphore(f"d_{q}{i}"))
                self.dma_uses[key] = 0
        self.nwaits = 0

    def _wait(self, e, sk, v):
        if e == "pe" and sk == "pe":
            return
        if self.waited[e].get(sk, 0) >= v:
            return
        self.eng[e].wait_ge(self.semh[sk], v)
        self.waited[e][sk] = v
        self.nwaits += 1

    def _deps(self, e, reads, writes):
        for r in reads:
            t = self.last_w.get(r)
            if t is not None:
                self._wait(e, t[0], t[1])
        for w in writes:
            t = self.last_w.get(w)
            if t is not None:
                self._wait(e, t[0], t[1])
            rd = self.readers.get(w)
            if rd:
                for sk, v in rd.items():
                    self._wait(e, sk, v)

    def _commit(self, tok, reads, writes):
        for r in reads:
            d = self.readers.setdefault(r, {})
            if d.get(tok[0], 0) < tok[1]:
                d[tok[0]] = tok[1]
        for w in writes:
            self.last_w[w] = tok
            self.readers[w] = {}

    def op(self, e, fn, reads=(), writes=()):
        self._deps(e, reads, writes)
        inst = fn(self.eng[e])
        self.cnt[e] += 1
        inst.then_inc(self.semh[e], 1)
        self._commit((e, self.cnt[e]), reads, writes)

    def dma(self, q, out, in_, r=(), w=(), slow=False, bg=False):
        reads, writes = r, w
        self._deps(q, reads, writes)
        qs = q + "bg" if bg else q
        n = self.ndma[qs]
        self.ndma[qs] += 1
        key = ("dma", qs, n % self.dma_k[qs])
        if self.dma_uses[key] > 0:
            self._wait(q, key, 16 * self.dma_uses[key])
        kw = {"allow_slow_non_contiguous": True} if slow else {}
        self.eng[q].dma_start(out=out, in_=in_, **kw).then_inc(self.semh[key], 16)
        self.dma_uses[key] += 1
        self._commit((key, 16 * self.dma_uses[key]), reads, writes)

    def barrier(self):
        toks = [(e, c) for e, c in self.cnt.items() if c > 0]
        toks += [(k, 16 * u) for k, u in self.dma_uses.items() if u > 0 and k[1] != "poolbg"]
        for e in self.eng:
            for sk, v in toks:
                if sk != e:
                    self._wait(e, sk, v)
        keep = {r: t for r, t in self.last_w.items() if isinstance(t[0], tuple) and t[0][1] == "poolbg"}
        self.last_w = keep
        self.readers = {}

    def finish(self):
        for k, u in self.dma_uses.items():
            if u > 0:
                self._wait("sp", k, 16 * u)
        for e, c in self.cnt.items():
            if c > 0:
                self._wait("sp", e, c)


def build(cfg):
    Ls, depth = cfg["Ls"], cfg["depth"]
    NPS, Lp = 2, 256
    NBS = Ls // T
    NB = NBS + 1
    Ltot = Ls + NPS * Lp
    NTI = Ltot // 128
    ZP = Ltot + 8 * 4
    nc = bass.Bass("TRN2", target_bir_lowering=False)

    def din(name, shape, dt=F32):
        return nc.dram_tensor(name, list(shape), dt, kind="ExternalInput").ap()

    def dout(name, shape, dt=F32):
        return nc.dram_tensor(name, list(shape), dt, kind="ExternalOutput").ap()

    I = {}
    I["xs"] = din("xs", [Ls, D])
    I["xp"] = din("xp", [NPS * Lp, D])
    I["pos"] = din("pos", [Ls, D])
    I["cvec"] = din("cvec", [2, D])
    I["st_re"] = din("st_re", [depth, 2, 16, 64])
    I["st_im"] = din("st_im", [depth, 2, 16, 64])
    wshapes = {
        "w_ada": [depth, D, NMOD * D], "b_ada": [depth, NMOD * D],
        "ffn_w_in": [depth, 2, D, 2 * DFF], "ffn_w_out": [depth, 2, DFF, D],
        "w_mix_in": [depth, D, 1280], "w_mix_out": [depth, D, D], "mix_norm_g": [depth, D],
        "ssm_lam_re": [depth, 2, 16, 64], "ssm_lam_im": [depth, 2, 16, 64], "ssm_log_dt": [depth, 2, 16],
        "ssm_b_re": [depth, 2, 16, 64, 16], "ssm_b_im": [depth, 2, 16, 64, 16],
        "ssm_c_re": [depth, 2, 16, 16, 64], "ssm_c_im": [depth, 2, 16, 16, 64],
        "ssm_d": [depth, 256], "ssm_glu_w": [depth, 256, 256], "ssm_glu_b": [depth, 256],
        "fnet_w": [depth, 4, 64, 64], "pool_w": [depth, 4, 64, 64], "pool_scale": [depth, 256],
        "sgu_w": [depth, 4, 128, 128], "sgu_b": [depth, 4, 128],
        "ln_g": [depth, 3, D], "ln_b": [depth, 3, D],
    }
    for k, s in wshapes.items():
        I[k] = din(k, s)
    NKB = Ls // T
    I["ident"] = din("ident", [128, 128])
    I["c64"] = din("c64", [128, 128])
    I["s64"] = din("s64", [128, 128])
    NT_ = Ls // 8 + 64
    I["iota"] = din("iota", [128, 2, NT_])
    I["dftS"] = din("dftS", [NKB, Ls // 128, 128, 2, T], BF16)
    I["dftP"] = din("dftP", [Lp // 128, 128, 2, Lp], BF16)
    O = {
        "ys": dout("ys", [Ls, D]), "yp": dout("yp", [NPS * Lp, D]),
        "nre": dout("nre", [NPS, depth, 2, 16, 64]), "nim": dout("nim", [NPS, depth, 2, 16, 64]),
    }

    scr_in = [[nc.dram_tensor(f"scr_in_{l}_{j}", [128, FC, KC * 256], BF16, kind="Internal").ap() for j in range(2)]
              for l in range(depth)]
    scr_out = [[nc.dram_tensor(f"scr_out_{l}_{j}", [128, KC, FC * 128], BF16, kind="Internal").ap() for j in range(2)]
               for l in range(depth)]

    SCRK = {}
    es = ExitStack()
    with es:
        tr = TR(nc, es)

        nsb = [0]

        def sb(name, shape, dt=F32, st=es):
            nsb[0] += 1
            return st.enter_context(nc.sbuf_tensor(f"s{nsb[0]}_{name}", list(shape), dt))

        PS = [es.enter_context(nc.psum_tensor(f"ps{i}", [128, 512], F32)) for i in range(8)]

        def dve(fn, r=(), w=()):
            tr.op("dve", fn, r, w)

        def act(fn, r=(), w=()):
            tr.op("act", fn, r, w)

        def pool(fn, r=(), w=()):
            tr.op("pool", fn, r, w)

        def pe(fn, r=(), w=()):
            tr.op("pe", fn, r, w)

        def mm(out, lhsT, rhs, start, stop, r, w):
            tr.op("pe", lambda e: e.matmul(out, lhsT=lhsT, rhs=rhs, start=start, stop=stop), r, w)

        X = sb("X", [128, KC, Ltot], BF16)
        ident = sb("ident", [128, 128])
        identb = sb("identb", [128, 128], BF16)
        onesD = sb("onesD", [128, 128], BF16)
        ones256 = sb("ones256", [128, 128], BF16)
        modT = sb("modT", [128, depth, NMOD, KC, 2])
        lng = sb("lng", [128, depth * 3, KC])
        lnb = sb("lnb", [128, depth * 3, KC])
        mng = sb("mng", [128, depth, KC])
        ssd = sb("ssd", [128, depth, 2])
        glb = sb("glb", [128, depth, 2])
        psc = sb("psc", [128, depth, 2])
        corrL = sb("corrL", [128, 2, 8])
        corrR = sb("corrR", [128, 2, 8])
        poolm = sb("poolm", [128, 2, 4])
        mskE = sb("mskE", [32, 1])
        mskO = sb("mskO", [32, 1])

        def xkey(c, b):
            return ("X", c, b)

        def xkeys(c, t0, n):
            return [("X", c, hb_) for hb_ in range(t0 // 256, (t0 + n + 255) // 256)]

        tr.dma("sp", ident[:], I["ident"], w=["ident"])
        act(lambda e: e.activation(out=identb[:], in_=ident[:], func=AF.Copy), ["ident"], ["identb"])
        dve(lambda e: e.memset(onesD[:], 1.0 / D), w=["onesD"])
        dve(lambda e: e.memset(ones256[:], 1.0 / 256), w=["ones256"])

        tr.dma("sp", lng[:], I["ln_g"].rearrange("l j (k p) -> p (l j) k", p=128), w=["lng"], slow=True)
        tr.dma("sp", lnb[:], I["ln_b"].rearrange("l j (k p) -> p (l j) k", p=128), w=["lnb"], slow=True)
        tr.dma("sp", mng[:], I["mix_norm_g"].rearrange("l (k p) -> p l k", p=128), w=["mng"], slow=True)
        tr.dma("sp", ssd[:], I["ssm_d"].rearrange("l (k p) -> p l k", p=128), w=["ssd"], slow=True)
        tr.dma("sp", glb[:], I["ssm_glu_b"].rearrange("l (k p) -> p l k", p=128), w=["glb"], slow=True)
        tr.dma("sp", psc[:], I["pool_scale"].rearrange("l (k p) -> p l k", p=128), w=["psc"], slow=True)
        dve(lambda e: e.memset(poolm[:], 0.0), w=["poolm"])
        dve(lambda e: e.memset(corrL[:], 1.0), w=["corrL"])
        dve(lambda e: e.memset(corrR[:], 1.0), w=["corrR"])
        for gi, wv in enumerate((2, 4, 8, 16)):
            ch, p0 = gi // 2, (gi % 2) * 64
            dve(lambda e, ch=ch, p0=p0, gi=gi, wv=wv: e.memset(poolm[p0:p0 + 64, ch, gi:gi + 1], 1.0 / wv),
                ["poolm"], ["poolm"])
            for t in range(8):
                cntL = t + wv // 2 - max(t - wv // 2, 0)
                if cntL != wv:
                    dve(lambda e, ch=ch, p0=p0, t=t, v=wv / cntL: e.memset(corrL[p0:p0 + 64, ch, t:t + 1], v),
                        ["corrL"], ["corrL"])
                dist = 8 - t
                cntR = min(wv // 2, dist) + wv // 2
                if cntR != wv:
                    dve(lambda e, ch=ch, p0=p0, t=t, v=wv / cntR: e.memset(corrR[p0:p0 + 64, ch, t:t + 1], v),
                        ["corrR"], ["corrR"])
        dve(lambda e: e.memset(mskE[:], 0.0), w=["mskE"])
        dve(lambda e: e.memset(mskO[:], 1.0), w=["mskO"])
        dve(lambda e: e.memset(mskE[0:16, :], 1.0), ["mskE"], ["mskE"])
        dve(lambda e: e.memset(mskO[0:16, :], 0.0), ["mskO"], ["mskO"])

        with ExitStack() as p0s:
            cvT = sb("cvT", [128, 2, KC], F32, p0s)
            scv = sb("scv", [128, 2, KC], F32, p0s)
            bada = sb("bada", [128, depth, NMOD * KC], F32, p0s)
            wad = [sb(f"wad{i}", [128, KC, 512], F32, p0s) for i in range(2)]
            xin = [sb(f"xin{i}", [128, D], F32, p0s) for i in range(2)]
            pin = [sb(f"pin{i}", [128, D], F32, p0s) for i in range(2)]
            stg = [sb(f"stg{i}", [128, 2816], F32, p0s) for i in range(2)]
            wall = [sb(f"wall{i}", [128, 11 * KC * 256], BF16, p0s) for i in range(1)]
            for cnd in range(2):
                tr.dma("sp", cvT[:, cnd, :], I["cvec"][cnd].rearrange("(k p) -> p k", p=128), w=["cvT"], slow=True)
            tr.dma("sp", bada[:], I["b_ada"].rearrange("l (m p) -> p l m", p=128), w=["bada"], slow=True)
            act(lambda e: e.activation(out=scv[:], in_=cvT[:], func=AF.Silu), ["cvT"], ["scv"])
            WA, WB, WC = [], [], []

            wtb = sb("wtb", [128, KC, 512], BF16, p0s)
            scvb = sb("scvb", [128, 2, KC], BF16, p0s)
            dve(lambda e: e.tensor_copy(out=scvb[:], in_=scv[:]), ["scv"], ["scvb"])

            def mod_item(it, l, m, half):
                wt = wad[it % 2]
                wk = ("wad", it % 2)
                col0 = m * D + half * 512
                tr.dma("sp" if it % 2 == 0 else "act", wt[:],
                       I["w_ada"][l, :, col0:col0 + 512].rearrange("(k p) n -> p k n", p=128), w=[wk])
                if it % 2 == 0:
                    pool(lambda e: e.tensor_copy(out=wtb[:], in_=wt[:]), [wk], ["wtb"])
                else:
                    act(lambda e: e.activation(out=wtb[:], in_=wt[:], func=AF.Copy), [wk], ["wtb"])
                ps = PS[it % 2]
                pk = ("ps", it % 2)
                for cc in range(4):
                    for k in range(KC):
                        mm(ps[:, cc * 2:cc * 2 + 2], wtb[:, k, cc * 128:(cc + 1) * 128], scvb[:, :, k],
                           k == 0, k == KC - 1, ["wtb", "scvb"], [pk])
                for cnd in range(2):
                    dve(lambda e, cnd=cnd: e.tensor_tensor(
                        out=modT[:, l, m, half * 4:half * 4 + 4, cnd], in0=ps[:, cnd:8:2],
                        in1=bada[:, l, m * KC + half * 4:m * KC + half * 4 + 4], op=ALU.add),
                        [pk, "bada"], ["modT"])

            it = 0
            for l in range(depth):
                for m in range(NMOD):
                    for half in range(2):
                        WA.append(lambda it=it, l=l, m=m, half=half: mod_item(it, l, m, half))
                        it += 1

            def in_item(i):
                xt = xin[i % 2]
                xk = ("xin", i % 2)
                if i < Ls // 128:
                    tr.dma("sp", xt[:], I["xs"][i * 128:(i + 1) * 128, :], w=[xk])
                    tr.dma("act", pin[i % 2][:], I["pos"][i * 128:(i + 1) * 128, :], w=[("pin", i % 2)])
                    pool(lambda e, xt=xt, pt=pin[i % 2]: e.tensor_tensor(out=xt[:], in0=xt[:], in1=pt[:], op=ALU.add),
                         [xk, ("pin", i % 2)], [xk])
                else:
                    j = i - Ls // 128
                    tr.dma("sp", xt[:], I["xp"][j * 128:(j + 1) * 128, :], w=[xk])
                for hb in range(2):
                    ps = PS[2 + (2 * i + hb) % 4]
                    pk = ("ps", 2 + (2 * i + hb) % 4)
                    for c4 in range(4):
                        c = hb * 4 + c4
                        pe(lambda e, ps=ps, c4=c4, c=c, xt=xt: e.transpose(ps[:, c4 * 128:(c4 + 1) * 128],
                                                                          xt[:, c * 128:(c + 1) * 128], ident[:]),
                           [xk, "ident"], [pk])
                    wr = [k_ for c4 in range(4) for k_ in xkeys(hb * 4 + c4, i * 128, 128)]
                    if hb == 0:
                        act(lambda e, ps=ps, hb=hb, i=i: e.activation(
                            out=X[:, hb * 4:hb * 4 + 4, i * 128:(i + 1) * 128],
                            in_=ps[:].rearrange("p (c t) -> p c t", c=4), func=AF.Copy), [pk], wr)
                    else:
                        dve(lambda e, ps=ps, hb=hb, i=i: e.tensor_copy(
                            out=X[:, hb * 4:hb * 4 + 4, i * 128:(i + 1) * 128],
                            in_=ps[:].rearrange("p (c t) -> p c t", c=4)), [pk], wr)

            for i in range(NTI):
                WB.append(lambda i=i: in_item(i))

            cst_ = {"nst": 0, "ncast": 0}

            def cast(out, in_, r, w):
                k = cst_["ncast"] % 3
                cst_["ncast"] += 1
                if k == 0:
                    act(lambda e: e.activation(out=out, in_=in_, func=AF.Copy), r, w)
                elif k == 1:
                    dve(lambda e: e.tensor_copy(out=out, in_=in_), r, w)
                else:
                    pool(lambda e: e.tensor_copy(out=out, in_=in_), r, w)

            wi0 = I["ffn_w_in"][0, 0].rearrange("(k p) (h n) -> p k h n", p=128, h=2)
            wo0 = I["ffn_w_out"][0, 0].rearrange("(f p) n -> p f n", p=128)
            wl0, wk0 = wall[0], ("wall", 0)

            def cv_in(fh, k):
                sg_, sk = stg[cst_["nst"] % 2], ("stg", cst_["nst"] % 2)
                cst_["nst"] += 1
                wv = wl0[:].rearrange("p (f k n) -> p f k n", f=11, k=KC)
                tr.dma("sp" if cst_["nst"] % 2 else "act", sg_[:].rearrange("p (h n) -> p h n", h=2), wi0[:, k, :, fh * 1408:(fh + 1) * 1408], w=[sk])
                cast(wv[:, :, k, :].rearrange("p f (h c) -> p h f c", h=2), sg_[:].rearrange("p (h f c) -> p h f c", h=2, c=128), [sk], [wk0])

            def cv_in_store(fh):
                tr.dma("sp", scr_in[0][0][:, fh * 11:(fh + 1) * 11, :].rearrange("p f n -> p (f n)"), wl0[:], r=[wk0], w=[("scr_in", 0, 0, fh)])
                SCRK.setdefault(("in", 0, 0), []).append(("scr_in", 0, 0, fh))

            def cv_out(fg):
                sg_, sk = stg[cst_["nst"] % 2], ("stg", cst_["nst"] % 2)
                cst_["nst"] += 1
                wv = wl0[:].rearrange("p (c f n) -> p c f n", c=KC, f=FC)
                tr.dma("sp" if cst_["nst"] % 2 else "act", sg_[:, 0:2048].rearrange("p (f n) -> p f n", f=2), wo0[:, fg * 2:fg * 2 + 2, :], w=[sk])
                cast(wv[:, :, fg * 2:fg * 2 + 2, :], sg_[:, 0:2048].rearrange("p (f c n) -> p c f n", f=2, n=128), [sk], [wk0])

            def cv_out_store():
                tr.dma("sp", scr_out[0][0][:].rearrange("p c n -> p (c n)"), wl0[:], r=[wk0], w=[("scr_out", 0, 0, 0)])
                SCRK.setdefault(("out", 0, 0), []).append(("scr_out", 0, 0, 0))

            for fh in range(2):
                for k in range(KC):
                    WC.append(lambda fh=fh, k=k: cv_in(fh, k))
                WC.append(lambda fh=fh: cv_in_store(fh))
            for fg in range(11):
                WC.append(lambda fg=fg: cv_out(fg))
            WC.append(cv_out_store)
            nA = len(WA)
            for ia in range(nA):
                WA[ia]()
                tb = (ia + 1) * len(WB) // nA - ia * len(WB) // nA
                for _ in range(tb):
                    WB.pop(0)()
                tcv = (ia + 1) * 29 // nA - ia * 29 // nA
                for _ in range(tcv):
                    if WC:
                        WC.pop(0)()
            while WB:
                WB.pop(0)()
            while WC:
                WC.pop(0)()
            for l in range(depth):
                for j in range(3):
                    dve(lambda e, l=l, j=j: e.tensor_scalar_add(out=modT[:, l, 3 * j + 1], in0=modT[:, l, 3 * j + 1],
                                                                scalar1=1.0), ["modT"], ["modT"])
                    gsc = (1.0 if j == 1 else 0.5) / ALPHA
                    dve(lambda e, l=l, j=j, gsc=gsc: e.tensor_scalar_mul(out=modT[:, l, 3 * j + 2],
                                                                         in0=modT[:, l, 3 * j + 2], scalar1=gsc),
                        ["modT"], ["modT"])
        tr.barrier()

        def conv_bg_list(l, j):
            wi = I["ffn_w_in"][l, j].rearrange("(k p) (h f c) -> p k h f c", p=128, h=2, c=128)
            wo = I["ffn_w_out"][l, j].rearrange("(f p) (c n) -> p f c n", p=128, n=128)
            si = scr_in[l][j].rearrange("p f (k h c) -> p f k h c", k=KC, h=2)
            so = scr_out[l][j].rearrange("p c (f n) -> p c f n", n=128)
            lst = []
            for k in range(KC):
                for hh in range(2):
                    key = ("scr_in", l, j, k * 2 + hh)
                    SCRK.setdefault(("in", l, j), []).append(key)
                    lst.append(lambda k=k, hh=hh, key=key: tr.dma("pool", si[:, :, k, hh, :], wi[:, k, hh, :, :], w=[key], bg=True))
            for c in range(KC):
                key = ("scr_out", l, j, c)
                SCRK.setdefault(("out", l, j), []).append(key)
                lst.append(lambda c=c, key=key: tr.dma("pool", so[:, c, :, :], wo[:, :, c, :], w=[key], bg=True))
            return lst

        BG = []

        def bg_issue(n):
            for _ in range(n):
                if BG:
                    BG.pop(0)()

        def col(t, *idx):
            a = t
            sl = tuple([slice(None)] + list(idx[:-1]) + [slice(idx[-1], idx[-1] + 1)])
            return t[sl]

        class Epi:
            pass

        def make_epi(st, W_=T):
            ep = Epi()
            ep.rp = sb("rp", [128, KC, W_], F32, st)
            ep.rb = [sb(f"rb{i}", [128, W_], BF16, st) for i in range(2)]
            ep.r2b = [sb(f"r2b{i}", [128, W_], BF16, st) for i in range(2)]
            ep.mean = sb("mean_sb", [128, W_], F32, st)
            ep.m2 = sb("m2", [128, W_], F32, st)
            ep.rstd = sb("rstd", [128, W_], F32, st)
            ep.t1 = [sb(f"t1_{i}", [128, W_], F32, st) for i in range(2)]
            ep.pend = None
            return ep

        def epi_chunk(ep, c, psO, pk, t0, n, l, sj, cond):
            blk = slice(t0, t0 + n)
            b = t0 // T
            ep.n = n
            dve(lambda e: e.scalar_tensor_tensor(out=ep.rp[:, c, :], in0=psO[:, 0:n], scalar=modT[:, l, 3 * sj + 2, c, cond:cond + 1],
                                                 in1=X[:, c, blk], op0=ALU.mult, op1=ALU.add),
                [pk, "modT"] + xkeys(c, t0, n), [("rp", c)])
            s = c % 2
            act(lambda e: e.activation(out=ep.rb[s][:], in_=ep.rp[:, c, :], func=AF.Copy), [("rp", c)], [("rb", s)])
            act(lambda e: e.activation(out=ep.r2b[s][:], in_=ep.rp[:, c, :], func=AF.Square), [("rp", c)], [("r2b", s)])
            epi_flush(ep)
            ep.pend = (c, s)

        def epi_flush(ep):
            if ep.pend is None:
                return
            c, s = ep.pend
            mm(PS[6][:, 0:ep.n], onesD[:], ep.rb[s][:], c == 0, c == KC - 1, [("rb", s), "onesD"], [("ps", 6)])
            mm(PS[7][:, 0:ep.n], onesD[:], ep.r2b[s][:], c == 0, c == KC - 1, [("r2b", s), "onesD"], [("ps", 7)])
            ep.pend = None

        def epi_finish(ep, t0, n, l, sj, final_out=None):
            epi_flush(ep)
            blk = slice(t0, t0 + n)
            b = t0 // T
            act(lambda e: e.activation(out=ep.mean[:], in_=PS[6][:, 0:n], func=AF.Copy), [("ps", 6)], ["mean"])
            dve(lambda e: e.tensor_tensor(out=ep.m2[:], in0=ep.mean[:], in1=ep.mean[:], op=ALU.mult), ["mean"], ["m2"])
            dve(lambda e: e.tensor_tensor(out=ep.m2[:], in0=PS[7][:, 0:n], in1=ep.m2[:], op=ALU.subtract),
                [("ps", 7), "m2"], ["m2"])
            dve(lambda e: e.tensor_scalar(out=ep.m2[:], in0=ep.m2[:], scalar1=EPS_LN, scalar2=None, op0=ALU.add),
                ["m2"], ["m2"])
            act(lambda e: e.activation(out=ep.m2[:], in_=ep.m2[:], func=AF.Ln), ["m2"], ["m2"])
            act(lambda e: e.activation(out=ep.rstd[:], in_=ep.m2[:], func=AF.Exp, scale=-0.5), ["m2"], ["rstd"])
            for c in range(KC):
                s = c % 2
                pool(lambda e, c=c, s=s: e.tensor_tensor(out=ep.t1[s][:], in0=ep.rp[:, c, :], in1=ep.mean[:], op=ALU.subtract),
                     [("rp", c), "mean"], [("t1", s)])
                pool(lambda e, c=c, s=s: e.tensor_tensor(out=ep.t1[s][:], in0=ep.t1[s][:], in1=ep.rstd[:], op=ALU.mult),
                     [("t1", s), "rstd"], [("t1", s)])
                if final_out is None:
                    pool(lambda e, c=c, s=s: e.tensor_scalar(out=X[:, c, blk], in0=ep.t1[s][:], scalar1=lng[:, l * 3 + sj, c:c + 1],
                                                             scalar2=lnb[:, l * 3 + sj, c:c + 1], op0=ALU.mult, op1=ALU.add),
                         [("t1", s), "lng", "lnb"], xkeys(c, t0, n))
                else:
                    pool(lambda e, c=c, s=s: e.tensor_scalar(out=ep.rp[:, c, :], in0=ep.t1[s][:], scalar1=lng[:, l * 3 + sj, c:c + 1],
                                                             scalar2=lnb[:, l * 3 + sj, c:c + 1], op0=ALU.mult, op1=ALU.add),
                         [("t1", s), "lng", "lnb", ("rp", c)], [("rp", c)])

        def make_h(hbuf, t0, n, l, sj):
            b = t0 // T
            cond = 0 if t0 < Ls else 1
            blk = slice(t0, t0 + n)
            for c in range(KC):
                dve(lambda e, c=c: e.tensor_scalar(out=hbuf[:, c, :], in0=X[:, c, blk],
                                                   scalar1=modT[:, l, 3 * sj + 1, c, cond:cond + 1],
                                                   scalar2=modT[:, l, 3 * sj, c, cond:cond + 1],
                                                   op0=ALU.mult, op1=ALU.add),
                    ["modT"] + xkeys(c, t0, n), [("h", c)])

        def ffn(l, j, final):
            sj = 0 if j == 0 else 2
            with ExitStack() as st:
                ep = make_epi(st)
                hbufs = [sb(f"h{i}", [128, KC, T], BF16, st) for i in range(2)]
                hid = sb("hid", [128, FC, T], BF16, st)
                NWI, NWO = 3, 2
                win = [sb(f"win{i}", [128, 2, KC, 256], BF16, st) for i in range(NWI)]
                wout = [sb(f"wout{i}", [128, FC, 128], BF16, st) for i in range(NWO)]
                sg = [sb(f"sg{i}", [128, T], F32, st) for i in range(2)]
                xo = ep.rp if final else None
                ot = [sb(f"ot{i}", [128, D], F32, st) for i in range(2)] if final else None
                nwi = 0
                nwo = 0
                npa = 0

                def mk_h(b):
                    hh = hbufs[b % 2]
                    cond = 0 if b < NBS else 1
                    for c in range(KC):
                        dve(lambda e, c=c, hh=hh: e.tensor_scalar(out=hh[:, c, :], in0=X[:, c, b * T:(b + 1) * T],
                                                                scalar1=modT[:, l, 3 * sj + 1, c, cond:cond + 1],
                                                                scalar2=modT[:, l, 3 * sj, c, cond:cond + 1],
                                                                op0=ALU.mult, op1=ALU.add),
                            ["modT"] + xkeys(c, b * T, T), [("h", b % 2, c)])

                mk_h(0)
                for b in range(NB):
                    cond = 0 if b < NBS else 1
                    h = hbufs[b % 2]
                    for f in range(FC):
                        if f % 2 == 0:
                            s = nwi % NWI
                            nwi += 1
                            wk = ("win", s)
                            tr.dma("sp", win[s][:].rearrange("p f k n -> p (f k n)"),
                                   scr_in[l][j][:, f:f + 2, :].rearrange("p f n -> p (f n)"), r=SCRK[("in", l, j)], w=[wk])
                        wt = win[s][:, f % 2]
                        pa, pg = PS[(npa % 2) * 2], PS[(npa % 2) * 2 + 1]
                        ka, kg = ("ps", (npa % 2) * 2), ("ps", (npa % 2) * 2 + 1)
                        npa += 1
                        for k in range(KC):
                            mm(pa[:, :], wt[:, k, 0:128], h[:, k, :], k == 0, k == KC - 1, [wk, ("h", b % 2, k)], [ka])
                        for k in range(KC):
                            mm(pg[:, :], wt[:, k, 128:256], h[:, k, :], k == 0, k == KC - 1, [wk, ("h", b % 2, k)], [kg])
                        q = f % 2
                        act(lambda e, pg=pg, q=q: e.activation(out=sg[q][:], in_=pg[:, :], func=AF.Silu), [kg], [("sg", q)])
                        dve(lambda e, pa=pa, q=q, f=f: e.tensor_tensor(out=hid[:, f, :], in0=pa[:, :], in1=sg[q][:], op=ALU.mult),
                            [ka, ("sg", q)], [("hid", f)])
                    if b + 1 < NB:
                        mk_h(b + 1)
                    for c in range(KC):
                        s = nwo % NWO
                        nwo += 1
                        wk = ("wout", s)
                        tr.dma("sp", wout[s][:].rearrange("p f n -> p (f n)"), scr_out[l][j][:, c, :], r=SCRK[("out", l, j)], w=[wk])
                        po = PS[4 + c % 2]
                        pk = ("ps", 4 + c % 2)
                        for f in range(FC):
                            mm(po[:, :], wout[s][:, f, :], hid[:, f, :], f == 0, f == FC - 1, [wk, ("hid", f)], [pk])
                        epi_chunk(ep, c, po, pk, b * T, T, l, sj, cond)
                    epi_finish(ep, b * T, T, l, sj, xo)
                    if final:
                        for q in range(4):
                            o = ot[q % 2]
                            ok = ("ot", q % 2)
                            for hb in range(2):
                                ps = PS[hb]
                                pk = ("ps", hb)
                                for c4 in range(4):
                                    c = hb * 4 + c4
                                    pe(lambda e, ps=ps, c4=c4, c=c, q=q: e.transpose(
                                        ps[:, c4 * 128:(c4 + 1) * 128], xo[:, c, q * 128:(q + 1) * 128], ident[:]),
                                       [("rp", c), "ident"], [pk])
                                if hb == 0:
                                    act(lambda e, ps=ps, o=o: e.activation(out=o[:, 0:512], in_=ps[:, :], func=AF.Copy), [pk], [ok])
                                else:
                                    dve(lambda e, ps=ps, o=o: e.tensor_copy(out=o[:, 512:1024], in_=ps[:, :]), [pk], [ok])
                            t0 = b * T + q * 128
                            if t0 < Ls:
                                tr.dma("sp", O["ys"][t0:t0 + 128, :], o[:], r=[ok])
                            else:
                                tr.dma("sp", O["yp"][t0 - Ls:t0 - Ls + 128, :], o[:], r=[ok])
            tr.barrier()

        NBLK = Ls // 8
        SEQS = [(0, NBLK, True, -1), (Ls, Lp // 8, False, 0), (Ls + Lp, Lp // 8, False, 1)]
        HB = []
        _c = 0
        for (_o, _n, _s, _q) in SEQS:
            HB.append(_c)
            _c += _n + 1
        NCOL = _c
        POFF = [8, Ls + 16, Ls + 16 + Lp + 8]

        def bc(ap, shape, axis):
            return ap.unsqueeze(axis).to_broadcast(list(shape))

        def mixer(l):
            with ExitStack() as ms:
                B1 = sb("B1", [128, 2 * Ltot], BF16, ms)
                ya = sb("ya", [128, 2, Ltot], BF16, ms)
                zaV = B1[:].rearrange("p (c t) -> p c t", c=2)
                zbV = B1[:].rearrange("p (i n) -> p i n", n=256)
                wmi = I["w_mix_in"][l].rearrange("(k p) n -> p k n", p=128)
                with ExitStack() as st:
                    wa = sb("wa", [128, KC, 256], BF16, st)
                    h = sb("hmix", [128, KC, T], BF16, st)
                    tr.dma("pool", wa[:], wmi[:, :, 0:256], w=["wa"])
                    BG.extend(conv_bg_list(l, 1))
                    if l + 1 < depth:
                        BG.extend(conv_bg_list(l + 1, 0))
                    for b in range(NB):
                        bg_issue(2)
                        make_h(h, b * T, T, l, 1)
                        for ct in range(2):
                            ps, pk = PS[ct], ("ps", ct)
                            for k in range(KC):
                                mm(ps[:, :], wa[:, k, ct * 128:(ct + 1) * 128], h[:, k, :], k == 0, k == KC - 1,
                                   ["wa", ("h", k)], [pk])
                            if ct == 0:
                                act(lambda e, ps=ps, b=b, ct=ct: e.activation(out=zaV[:, ct, b * T:(b + 1) * T], in_=ps[:, :], func=AF.Copy),
                                    [pk], [("za", ct, b)])
                            else:
                                dve(lambda e, ps=ps, b=b, ct=ct: e.tensor_copy(out=zaV[:, ct, b * T:(b + 1) * T], in_=ps[:, :]),
                                    [pk], [("za", ct, b)])
                tr.barrier()
                s5(l, zaV, ya)
                tr.barrier()
                zc = sb("zc", [128, 2, ZP], BF16, ms)
                yb = sb("yb", [128, 2, Ltot], BF16, ms)
                fnet_pass(l, zbV, zc, yb)
                tr.barrier()
                pass3(l, B1, ya, yb, zc)
                bg_issue(len(BG))
            tr.barrier()

        def s5(l, zaV, ya):
            with ExitStack() as st:
                A = lambda name, shape, dt=F32: sb(name, shape, dt, st)
                iota = A("iota", [128, 2, NT_])
                tr.dma("sp", iota[:], I["iota"], w=["iota"])
                lre, lim, dtc = A("lre", [128, 16]), A("lim", [128, 16]), A("dtc", [128, 16])
                h0re, h0im = A("h0re", [128, 16]), A("h0im", [128, 16])
                for d in range(2):
                    sl = slice(d * 8, (d + 1) * 8)
                    tr.dma("sp", lre[:, sl], I["ssm_lam_re"][l, d].rearrange("(i q) p -> (q p) i", q=2), w=["lre"], slow=True)
                    tr.dma("sp", lim[:, sl], I["ssm_lam_im"][l, d].rearrange("(i q) p -> (q p) i", q=2), w=["lim"], slow=True)
                    tr.dma("sp", h0re[:, sl], I["st_re"][l, d].rearrange("(i q) p -> (q p) i", q=2), w=["h0re"], slow=True)
                    tr.dma("sp", h0im[:, sl], I["st_im"][l, d].rearrange("(i q) p -> (q p) i", q=2), w=["h0im"], slow=True)
                    for q in range(2):
                        tr.dma("sp", dtc[q * 64:(q + 1) * 64, sl],
                               I["ssm_log_dt"][l, d].rearrange("(i q) -> q i", q=2)[q:q + 1, :].to_broadcast([64, 8]),
                               w=["dtc"], slow=True)
                act(lambda e: e.activation(out=dtc[:], in_=dtc[:], func=AF.Exp), ["dtc"], ["dtc"])
                lrdt, ang1 = A("lrdt", [128, 16]), A("ang1", [128, 16])
                dve(lambda e: e.tensor_tensor(out=lrdt[:], in0=lre[:], in1=dtc[:], op=ALU.mult), ["lre", "dtc"], ["lrdt"])
                dve(lambda e: e.tensor_tensor(out=ang1[:], in0=lim[:], in1=dtc[:], op=ALU.mult), ["lim", "dtc"], ["ang1"])
                mag = A("mag", [128, 16, 9])
                for e_ in range(9):
                    act(lambda e, e_=e_: e.activation(out=mag[:, :, e_], in_=lrdt[:], func=AF.Exp, scale=float(e_)), ["lrdt"], ["mag"])
                ysn, ycs = A("ysn", [128, 16, 9]), A("ycs", [128, 16, 9])
                for e_ in range(9):
                    dve(lambda e, e_=e_: e.tensor_scalar_mul(out=ysn[:, :, e_], in0=ang1[:], scalar1=float(e_) / TWO_PI), ["ang1"], ["ysn"])
                dve(lambda e: e.tensor_scalar_add(out=ycs[:], in0=ysn[:], scalar1=0.25), ["ysn"], ["ycs"])
                ki, kf = A("ki", [128, 16 * 9], I32), A("kf", [128, 16 * 9])

                def frac(y, key):
                    yv = y[:].rearrange("p a b -> p (a b)")
                    dve(lambda e: e.tensor_copy(out=ki[:], in_=yv), [key], ["ki"])
                    dve(lambda e: e.tensor_copy(out=kf[:], in_=ki[:]), ["ki"], ["kf"])
                    dve(lambda e: e.tensor_tensor(out=yv, in0=yv, in1=kf[:], op=ALU.subtract), [key, "kf"], [key])
                    dve(lambda e: e.tensor_scalar(out=yv, in0=yv, scalar1=0.49999, scalar2=-0.49999, op0=ALU.min, op1=ALU.max), [key], [key])

                frac(ysn, "ysn")
                frac(ycs, "ycs")
                sn, cs = A("sn", [128, 16, 9]), A("cs", [128, 16, 9])
                act(lambda e: e.activation(out=sn[:], in_=ysn[:], func=AF.Sin, scale=TWO_PI), ["ysn"], ["sn"])
                act(lambda e: e.activation(out=cs[:], in_=ycs[:], func=AF.Sin, scale=TWO_PI), ["ycs"], ["cs"])
                pwr, pwi = A("pwr", [128, 16, 9]), A("pwi", [128, 16, 9])
                dve(lambda e: e.tensor_tensor(out=pwr[:], in0=mag[:], in1=cs[:], op=ALU.mult), ["mag", "cs"], ["pwr"])
                dve(lambda e: e.tensor_tensor(out=pwi[:], in0=mag[:], in1=sn[:], op=ALU.mult), ["mag", "sn"], ["pwi"])
                am1, den, t_a, t_b = A("am1", [128, 16]), A("den", [128, 16]), A("t_a", [128, 16]), A("t_b", [128, 16])
                cfr, cfi = A("cfr", [128, 16]), A("cfi", [128, 16])
                ar, ai = pwr[:, :, 1], pwi[:, :, 1]
                TT = lambda o, a, b_, op, r, w: dve(lambda e: e.tensor_tensor(out=o, in0=a, in1=b_, op=op), r, w)
                dve(lambda e: e.tensor_scalar_add(out=am1[:], in0=ar, scalar1=-1.0), ["pwr"], ["am1"])
                TT(den[:], lre[:], lre[:], ALU.mult, ["lre"], ["den"])
                TT(t_a[:], lim[:], lim[:], ALU.mult, ["lim"], ["t_a"])
                TT(den[:], den[:], t_a[:], ALU.add, ["den", "t_a"], ["den"])
                dve(lambda e: e.reciprocal(out=den[:], in_=den[:]), ["den"], ["den"])
                TT(t_a[:], am1[:], lre[:], ALU.mult, ["am1", "lre"], ["t_a"])
                TT(t_b[:], ai, lim[:], ALU.mult, ["pwi", "lim"], ["t_b"])
                TT(t_a[:], t_a[:], t_b[:], ALU.add, ["t_a", "t_b"], ["t_a"])
                TT(cfr[:], t_a[:], den[:], ALU.mult, ["t_a", "den"], ["cfr"])
                TT(t_a[:], ai, lre[:], ALU.mult, ["pwi", "lre"], ["t_a"])
                TT(t_b[:], am1[:], lim[:], ALU.mult, ["am1", "lim"], ["t_b"])
                TT(t_a[:], t_a[:], t_b[:], ALU.subtract, ["t_a", "t_b"], ["t_a"])
                TT(cfi[:], t_a[:], den[:], ALU.mult, ["t_a", "den"], ["cfi"])
                cbr, cbi, tmp9 = A("cbr", [128, 16, 8]), A("cbi", [128, 16, 8]), A("tmp9", [128, 16, 8])
                S8 = [128, 16, 8]
                TT(cbr[:], pwr[:, :, 0:8], bc(cfr[:], S8, 2), ALU.mult, ["pwr", "cfr"], ["cbr"])
                TT(tmp9[:], pwi[:, :, 0:8], bc(cfi[:], S8, 2), ALU.mult, ["pwi", "cfi"], ["tmp9"])
                TT(cbr[:], cbr[:], tmp9[:], ALU.subtract, ["cbr", "tmp9"], ["cbr"])
                TT(cbi[:], pwi[:, :, 0:8], bc(cfr[:], S8, 2), ALU.mult, ["pwi", "cfr"], ["cbi"])
                TT(tmp9[:], pwr[:, :, 0:8], bc(cfi[:], S8, 2), ALU.mult, ["pwr", "cfi"], ["tmp9"])
                TT(cbi[:], cbi[:], tmp9[:], ALU.add, ["cbi", "tmp9"], ["cbi"])
                g0r, g0i = A("g0r", [128, 16]), A("g0i", [128, 16])
                TT(g0r[:], h0re[:], cs[:, :, 8], ALU.mult, ["h0re", "cs"], ["g0r"])
                TT(t_a[:], h0im[:], sn[:, :, 8], ALU.mult, ["h0im", "sn"], ["t_a"])
                TT(g0r[:], g0r[:], t_a[:], ALU.subtract, ["g0r", "t_a"], ["g0r"])
                TT(g0i[:], h0re[:], sn[:, :, 8], ALU.mult, ["h0re", "sn"], ["g0i"])
                TT(t_a[:], h0im[:], cs[:, :, 8], ALU.mult, ["h0im", "cs"], ["t_a"])
                TT(g0i[:], g0i[:], t_a[:], ALU.add, ["g0i", "t_a"], ["g0i"])
                fin = A("fin", [128, 2, 16, 2])
                S3 = [128, 8, 64]
                npo = 0
                for ct in range(2):
                  with ExitStack() as cst:
                    Ac = lambda name, shape, dt=F32: sb(name, shape, dt, cst)
                    WinAll = Ac("WinAll", [128, 64, 128], BF16)
                    WoutAll = Ac("WoutAll", [128, 8, 8, 2, 64], BF16)
                    KK = Ac("KK", [128, 16, 128], BF16)
                    dve(lambda e: e.memset(KK[:], 0.0), [], ["KK"])
                    with ExitStack() as pst:
                        Ap = lambda name, shape, dt=F32: sb(name, shape, dt, pst)
                        Bn = [Ap(f"Bn{p}", [128, 8, 16]) for p in range(2)]
                        Cn = [Ap(f"Cn{p}", [32, 8, 64]) for p in range(2)]
                        for d in range(2):
                            sl = slice(d * 4, (d + 1) * 4)
                            gs = slice(ct * 8, ct * 8 + 8)
                            for p, nm in enumerate(("ssm_b_re", "ssm_b_im")):
                                tr.dma("sp", Bn[p][:, sl, :], I[nm][l, d, gs].rearrange("(i q) p c -> (q p) i c", q=2), w=[("Bn", p)], slow=True)
                            for p, nm in enumerate(("ssm_c_re", "ssm_c_im")):
                                tr.dma("sp", Cn[p][:, sl, :], I[nm][l, d, gs].rearrange("(i q) h p -> (q h) i p", q=2), w=[("Cn", p)], slow=True)
                        BBD = [Ap(f"BBD{p}", [128, 8, 64]) for p in range(2)]
                        Cpt = [Ap(f"Cpt{p}", [128, 8, 64]) for p in range(2)]
                        cbd = [Ap(f"cbd{i}", [32, 128]) for i in range(2)]
                        for p in range(2):
                            dve(lambda e, p=p: e.memset(BBD[p][:], 0.0), [], [("BBD", p)])
                            dve(lambda e, p=p: e.memset(Cpt[p][:], 0.0), [], [("Cpt", p)])
                            for ddl in range(8):
                                par = ddl % 2
                                c0 = par * 32
                                dve(lambda e, p=p, ddl=ddl, c0=c0: e.tensor_copy(out=BBD[p][0:64, ddl, c0:c0 + 16], in_=Bn[p][0:64, ddl, :]),
                                    [("Bn", p), ("BBD", p)], [("BBD", p)])
                                dve(lambda e, p=p, ddl=ddl, c0=c0: e.tensor_copy(out=BBD[p][64:128, ddl, c0 + 16:c0 + 32], in_=Bn[p][64:128, ddl, :]),
                                    [("Bn", p), ("BBD", p)], [("BBD", p)])
                            for ddl in range(8):
                                cb_, ck = cbd[ddl % 2], ("cbd", ddl % 2)
                                dve(lambda e, p=p, ddl=ddl, cb_=cb_: e.tensor_scalar_mul(out=cb_[:, 0:64], in0=Cn[p][:, ddl, :], scalar1=mskE[:, 0:1]),
                                    [("Cn", p), "mskE"], [ck])
                                dve(lambda e, p=p, ddl=ddl, cb_=cb_: e.tensor_scalar_mul(out=cb_[:, 64:128], in0=Cn[p][:, ddl, :], scalar1=mskO[:, 0:1]),
                                    [("Cn", p), "mskO", ck], [ck])
                                mm(PS[p][:, ddl * 32:(ddl + 1) * 32], cb_[:, :], ident[0:32, 0:32], True, True, [ck, "ident"], [("ps", p)])
                            for ddl in range(8):
                                c0 = (ddl % 2) * 32
                                fn = lambda e, p=p, ddl=ddl, c0=c0: e.activation(out=Cpt[p][:, ddl, c0:c0 + 32], in_=PS[p][:, ddl * 32:(ddl + 1) * 32],
                                                                               func=AF.Copy, scale=(1.0 if p == 0 else -1.0))
                                act(fn, [("ps", p), ("Cpt", p)], [("Cpt", p)])
                        VB = [Ap(f"VB{p}", [128, 4, 8, 64]) for p in range(2)]
                        vt = Ap("vt", [128, 8, 64])
                        vt2 = Ap("vt2", [128, 8, 64])
                        wo0, wo1 = Ap("wo0", [128, 8, 64]), Ap("wo1", [128, 8, 64])
                        TPp = lambda o, a, b_, op, r, w: pool(lambda e: e.tensor_tensor(out=o, in0=a, in1=b_, op=op), r, w)
                        nb_ = 0
                        for d in range(2):
                            for ip in range(4):
                                dd = d * 8 + ct * 4 + ip
                                ddl = d * 4 + ip
                                br, bi = bc(BBD[0][:, ddl, :], S3, 1), bc(BBD[1][:, ddl, :], S3, 1)
                                cr, ci = bc(cbr[:, dd, :], S3, 2), bc(cbi[:, dd, :], S3, 2)
                                rk = [("BBD", 0), ("BBD", 1), "cbr", "cbi"]
                                TT(VB[0][:, ip], br, cr, ALU.mult, rk, [("VB", 0, ip)])
                                TT(vt[:], bi, ci, ALU.mult, rk, ["vt"])
                                TT(VB[0][:, ip], VB[0][:, ip], vt[:], ALU.subtract, [("VB", 0, ip), "vt"], [("VB", 0, ip)])
                                TT(VB[1][:, ip], br, ci, ALU.mult, rk, [("VB", 1, ip)])
                                TT(vt[:], bi, cr, ALU.mult, rk, ["vt"])
                                TT(VB[1][:, ip], VB[1][:, ip], vt[:], ALU.add, [("VB", 1, ip), "vt"], [("VB", 1, ip)])
                                Cr, nCi = bc(Cpt[0][:, ddl, :], S3, 1), bc(Cpt[1][:, ddl, :], S3, 1)
                                pr, pi_ = bc(pwr[:, dd, 1:9], S3, 2), bc(pwi[:, dd, 1:9], S3, 2)
                                rk2 = [("Cpt", 0), ("Cpt", 1), "pwr", "pwi"]
                                TPp(wo0[:], Cr, pr, ALU.mult, rk2, ["wo0"])
                                TPp(vt2[:], nCi, pi_, ALU.mult, rk2, ["vt2"])
                                TPp(WoutAll[:, ddl, :, 0, :], wo0[:], vt2[:], ALU.add, ["wo0", "vt2"], [("Wout", ddl)])
                                TPp(wo1[:], nCi, pr, ALU.mult, rk2, ["wo1"])
                                TPp(vt2[:], Cr, pi_, ALU.mult, rk2, ["vt2"])
                                TPp(WoutAll[:, ddl, :, 1, :], wo1[:], vt2[:], ALU.subtract, ["wo1", "vt2"], [("Wout", ddl)])
                            for e_ in range(8):
                                ps, pk = PS[2 + nb_ % 4], ("ps", 2 + nb_ % 4)
                                nb_ += 1
                                for part in range(2):
                                    for ip in range(4):
                                        u, par = ip // 2, ip % 2
                                        rg = part * 2 + par
                                        pe(lambda e, ps=ps, rg=rg, ip=ip, u=u, part=part, e_=e_: e.matmul(
                                            ps[u * 64:(u + 1) * 64, rg * 128:(rg + 1) * 128], lhsT=VB[part][:, ip, e_, :],
                                            rhs=ident[:, :], start=True, stop=True), [("VB", part, ip), "ident"], [pk])
                                r0 = (d * 8 + e_) * 4
                                act(lambda e, ps=ps, r0=r0: e.activation(out=WinAll[:, r0:r0 + 4, :].rearrange("p a b -> p (a b)"), in_=ps[:, :], func=AF.Copy),
                                    [pk], ["WinAll"])
                            for i4 in range(2):
                                ps, pk = PS[2 + nb_ % 4], ("ps", 2 + nb_ % 4)
                                nb_ += 1
                                for ii in range(4):
                                    e_ = i4 * 4 + ii
                                    for u in range(2):
                                        n_t = 0
                                        for par in range(2):
                                            ip = u * 2 + par
                                            ddl = d * 4 + ip
                                            for part in range(2):
                                                n_t += 1
                                                pe(lambda e, ps=ps, ii=ii, u=u, ip=ip, part=part, ddl=ddl, e_=e_, n_t=n_t: e.matmul(
                                                    ps[u * 64:(u + 1) * 64, ii * 128 + u * 64: ii * 128 + (u + 1) * 64],
                                                    lhsT=VB[part][:, ip, e_, :], rhs=Cpt[part][:, ddl, :], start=(n_t == 1), stop=(n_t == 4)),
                                                   [("VB", part, ip), ("Cpt", part)], [pk])
                                for u in range(2):
                                    src = ps[u * 64:(u + 1) * 64, :].rearrange("p (a b) -> p a b", b=128)[:, :, u * 64:(u + 1) * 64]
                                    s0 = d * 8 + i4 * 4
                                    dve(lambda e, src=src, u=u, s0=s0: e.tensor_copy(
                                        out=KK[u * 64:(u + 1) * 64, s0:s0 + 4, u * 64:(u + 1) * 64], in_=src), [pk, "KK"], ["KK"])
                        dve(lambda e: e.tensor_tensor(out=KK[:, 0, :], in0=KK[:, 0, :], in1=KK[:, 8, :], op=ALU.add), ["KK"], ["KK"])
                        dve(lambda e, ct=ct: e.scalar_tensor_tensor(out=KK[:, 0, :], in0=identb[:], scalar=ssd[:, l, ct:ct + 1],
                                                                   in1=KK[:, 0, :], op0=ALU.mult, op1=ALU.add), ["KK", "identb", "ssd"], ["KK"])
                    tr.barrier()
                    with ExitStack() as mst:
                        Am = lambda name, shape, dt=F32: sb(name, shape, dt, mst)
                        Hs = Am("Hs", [128, 4, 2, 2, NCOL], BF16)
                        cosT, sinT = Am("cosT", [128, NT_]), Am("sinT", [128, NT_])
                        kiw = Am("kiw", [128, NT_], I32)
                        Sre, Sim = Am("Sre", [128, NT_]), Am("Sim", [128, NT_])
                        Gr, Gi = Am("Gr", [128, NT_]), Am("Gi", [128, NT_])
                        u1, u2 = Am("u1", [128, NT_]), Am("u2", [128, NT_])
                        u3, u4 = Am("u3", [128, NT_]), Am("u4", [128, NT_])
                        kiw2 = Am("kiw2", [128, NT_], I32)
                        TP = lambda o, a, b_, op, r, w: pool(lambda e: e.tensor_tensor(out=o, in0=a, in1=b_, op=op), r, w)
                        CR = [(0, NBLK), (NBLK, NBLK + 32), (NBLK + 32, NBLK + 64)]
                        for ip in range(4):
                            u, par = ip // 2, ip % 2
                            for d in range(2):
                                dd = d * 8 + ct * 4 + ip
                                for (tab, off, key, E_, yw_, ki_, kf_, sfx) in ((sinT, 0.0, "sinT", dve, u1, kiw, u2, ("u1", "kiw", "u2")),
                                                                               (cosT, 0.25, "cosT", pool, u3, kiw2, u4, ("u3", "kiw2", "u4"))):
                                    E_(lambda e, off=off, dd=dd, yw_=yw_, d=d: e.tensor_scalar(out=yw_[:], in0=iota[:, d, :], scalar1=ysn[:, dd, 8:9], scalar2=off,
                                                                                             op0=ALU.mult, op1=ALU.add), ["iota", "ysn"], [sfx[0]])
                                    E_(lambda e, yw_=yw_, ki_=ki_: e.tensor_copy(out=ki_[:], in_=yw_[:]), [sfx[0]], [sfx[1]])
                                    E_(lambda e, kf_=kf_, ki_=ki_: e.tensor_copy(out=kf_[:], in_=ki_[:]), [sfx[1]], [sfx[2]])
                                    E_(lambda e, yw_=yw_, kf_=kf_: e.tensor_tensor(out=yw_[:], in0=yw_[:], in1=kf_[:], op=ALU.subtract), [sfx[0], sfx[2]], [sfx[0]])
                                    E_(lambda e, yw_=yw_: e.tensor_scalar(out=yw_[:], in0=yw_[:], scalar1=0.49999, scalar2=-0.49999, op0=ALU.min, op1=ALU.max), [sfx[0]], [sfx[0]])
                                    act(lambda e, tab=tab, yw_=yw_: e.activation(out=tab[:], in_=yw_[:], func=AF.Sin, scale=TWO_PI), [sfx[0]], [key])
                                for si, (off, nb, is_s, qi) in enumerate(SEQS):
                                    zk = [("za", ct, bb) for bb in range(off // T, (off + 8 * nb - 1) // T + 1)]
                                    for part in range(2):
                                        if is_s:
                                            dst, dkey = PS[part][:, 0:nb], ("ps", part)
                                        else:
                                            dst, dkey = PS[2 + part][:, qi * 32:qi * 32 + 32], ("ps", 2 + part)
                                        for s in range(8):
                                            e_ = 7 - s if d == 0 else s
                                            rg = ((d * 8 + e_) * 2 + part) * 2 + par
                                            mm(dst, WinAll[u * 64:(u + 1) * 64, rg, :],
                                               zaV[u * 64:(u + 1) * 64, ct, off + s:off + 8 * nb:8], s == 0, s == 7, ["WinAll"] + zk, [dkey])
                                act(lambda e: e.activation(out=Sre[:, 0:NBLK], in_=PS[0][:, 0:NBLK], func=AF.Copy), [("ps", 0)], ["Sre"])
                                act(lambda e: e.activation(out=Sim[:, 0:NBLK], in_=PS[1][:, 0:NBLK], func=AF.Copy), [("ps", 1)], ["Sim"])
                                act(lambda e: e.activation(out=Sre[:, NBLK:NBLK + 64], in_=PS[2][:, 0:64], func=AF.Copy), [("ps", 2), "Sre"], ["Sre"])
                                act(lambda e: e.activation(out=Sim[:, NBLK:NBLK + 64], in_=PS[3][:, 0:64], func=AF.Copy), [("ps", 3), "Sim"], ["Sim"])
                                TT(u1[:], Sre[:], cosT[:], ALU.mult, ["Sre", "cosT"], ["u1"])
                                TT(u2[:], Sim[:], sinT[:], ALU.mult, ["Sim", "sinT"], ["u2"])
                                TT(Gr[:], u1[:], u2[:], ALU.add, ["u1", "u2"], ["Gr"])
                                TP(u3[:], Sim[:], cosT[:], ALU.mult, ["Sim", "cosT"], ["u3"])
                                TP(u4[:], Sre[:], sinT[:], ALU.mult, ["Sre", "sinT"], ["u4"])
                                TP(Gi[:], u3[:], u4[:], ALU.subtract, ["u3", "u4"], ["Gi"])
                                for si, (off, nb, is_s, qi) in enumerate(SEQS):
                                    lo, hi_ = CR[si]
                                    rv = slice(lo, hi_) if d == 0 else slice(hi_ - 1, (lo - 1) if lo > 0 else None, -1)
                                    r8 = mag[:, dd, 8:9].to_broadcast([128, nb])
                                    for (G, g0, gk) in ((Gr, g0r, "Gr"), (Gi, g0i, "Gi")):
                                        init = g0[:, dd:dd + 1] if is_s else 0.0
                                        dve(lambda e, G=G, init=init, r8=r8, rv=rv: e.tensor_tensor_scan(
                                            out=G[:, rv], data0=r8, data1=G[:, rv], initial=init, op0=ALU.mult, op1=ALU.add),
                                            [gk, "mag", "g0r", "g0i"], [gk])
                                hk = ("Hs", ip, d)
                                TT(u1[:], Gr[:], cosT[:], ALU.mult, ["Gr", "cosT"], ["u1"])
                                TT(u2[:], Gi[:], sinT[:], ALU.mult, ["Gi", "sinT"], ["u2"])
                                TP(u3[:], Gr[:], sinT[:], ALU.mult, ["Gr", "sinT"], ["u3"])
                                TP(u4[:], Gi[:], cosT[:], ALU.mult, ["Gi", "cosT"], ["u4"])
                                for si, (off, nb, is_s, qi) in enumerate(SEQS):
                                    base = HB[si]
                                    lo, hi_ = CR[si]
                                    c0 = base + 1 if d == 0 else base
                                    TT(Hs[:, ip, d, 0, c0:c0 + nb], u1[:, lo:hi_], u2[:, lo:hi_], ALU.subtract, ["u1", "u2"], [hk])
                                    TP(Hs[:, ip, d, 1, c0:c0 + nb], u3[:, lo:hi_], u4[:, lo:hi_], ALU.add, ["u3", "u4"], [hk])
                                    if not is_s:
                                        fc = hi_ - 1 if d == 0 else lo
                                        TT(fin[:, qi, dd, 0:1], u1[:, fc:fc + 1], u2[:, fc:fc + 1], ALU.subtract, ["u1", "u2"], ["fin"])
                                        TP(fin[:, qi, dd, 1:2], u3[:, fc:fc + 1], u4[:, fc:fc + 1], ALU.add, ["u3", "u4"], ["fin"])
                                    cb0 = base if d == 0 else base + nb
                                    if is_s:
                                        dve(lambda e, cb0=cb0, ip=ip, d=d, dd=dd: e.tensor_copy(out=Hs[:, ip, d, 0, cb0:cb0 + 1], in_=h0re[:, dd:dd + 1]), ["h0re"], [hk])
                                        dve(lambda e, cb0=cb0, ip=ip, d=d, dd=dd: e.tensor_copy(out=Hs[:, ip, d, 1, cb0:cb0 + 1], in_=h0im[:, dd:dd + 1]), ["h0im"], [hk])
                                    else:
                                        dve(lambda e, cb0=cb0, ip=ip, d=d: e.memset(Hs[:, ip, d, :, cb0:cb0 + 1], 0.0), [], [hk])
                        for si, (off, nb, is_s, qi) in enumerate(SEQS):
                            base = HB[si]
                            zk = [("za", ct, bb) for bb in range(off // T, (off + 8 * nb - 1) // T + 1)]
                            for s in range(8):
                                ps, pk = PS[2 + npo % 4], ("ps", 2 + npo % 4)
                                npo += 1
                                for s2 in range(8):
                                    mm(ps[:, 0:nb], KK[:, (s - s2) if s >= s2 else 8 + (s2 - s), :], zaV[:, ct, off + s2:off + 8 * nb:8], s2 == 0, False, ["KK"] + zk, [pk])
                                n_t = 0
                                for ip in range(4):
                                    u = ip // 2
                                    for d in range(2):
                                        ddl = d * 4 + ip
                                        e_ = s + 1 if d == 0 else 8 - s
                                        c0 = base if d == 0 else base + 1
                                        for part in range(2):
                                            n_t += 1
                                            mm(ps[u * 64:(u + 1) * 64, 0:nb], WoutAll[:, ddl, e_ - 1, part, :], Hs[:, ip, d, part, c0:c0 + nb],
                                               False, n_t in (8, 16), [("Wout", ddl), ("Hs", ip, d)], [pk])
                                act(lambda e, ps=ps, nb=nb, off=off, s=s, ct=ct: e.activation(
                                    out=ya[:, ct, off + s:off + 8 * nb:8], in_=ps[:, 0:nb], func=AF.Gelu_apprx_tanh),
                                    [pk], [("ya", ct, bb) for bb in range(off // T, (off + 8 * nb - 1) // T + 1)])
                    tr.barrier()
                for qi in range(2):
                    for d in range(2):
                        for part, nm in enumerate(("nre", "nim")):
                            tr.dma("sp", O[nm][qi, l, d].rearrange("(i q) p -> (q p) i", q=2), fin[:, qi, d * 8:(d + 1) * 8, part],
                                   r=["fin"], slow=True)

        def fnet_pass(l, zbV, zc, yb):
            wmi = I["w_mix_in"][l].rearrange("(k p) n -> p k n", p=128)
            NTs = Ls // 128
            with ExitStack() as st:
                A = lambda name, shape, dt=F32: sb(name, shape, dt, st)
                h = A("hmix2", [128, KC, T], BF16)
                wbc = A("wbc", [128, KC, 512], BF16)
                tr.dma("pool", wbc[:], wmi[:, :, 256:768], w=["wbc"])
                zck = [("zc", b) for b in range(NB)]
                dve(lambda e: e.memset(zc[:], 0.0), [], zck)
                for b in range(NB):
                    bg_issue(2)
                    make_h(h, b * T, T, l, 1)
                    hk = [("h", k) for k in range(KC)]
                    for q in range(4):
                        i = b * 4 + q
                        ps, pk = PS[q % 2], ("ps", q % 2)
                        for k in range(KC):
                            mm(ps[:, 0:256], h[:, k, q * 128:(q + 1) * 128], wbc[:, k, 0:256], k == 0, k == KC - 1, ["wbc", ("h", k)], [pk])
                        if q % 2 == 0:
                            act(lambda e, ps=ps, i=i: e.activation(out=zbV[:, i, :], in_=ps[:, 0:256], func=AF.Copy), [pk], [("zb", i)])
                        else:
                            dve(lambda e, ps=ps, i=i: e.tensor_copy(out=zbV[:, i, :], in_=ps[:, 0:256]), [pk], [("zb", i)])
                    for ct in range(2):
                        ps, pk = PS[2 + ct], ("ps", 2 + ct)
                        for k in range(KC):
                            mm(ps[:, :], wbc[:, k, 256 + ct * 128:256 + (ct + 1) * 128], h[:, k, :], k == 0, k == KC - 1, ["wbc", ("h", k)], [pk])
                        if b < NBS:
                            act(lambda e, ps=ps, b=b, ct=ct: e.activation(out=zc[:, ct, 8 + b * T:8 + (b + 1) * T], in_=ps[:, :], func=AF.Copy),
                                [pk], [("zc", b)])
                        else:
                            act(lambda e, ps=ps, ct=ct: e.activation(out=zc[:, ct, POFF[1]:POFF[1] + Lp], in_=ps[:, 0:Lp], func=AF.Copy), [pk], [("zc", b)])
                            dve(lambda e, ps=ps, ct=ct: e.tensor_copy(out=zc[:, ct, POFF[2]:POFF[2] + Lp], in_=ps[:, Lp:2 * Lp]), [pk], [("zc", b)])
                FW = A("FW", [128, 2, 128])
                c64t, s64t = A("c64t", [128, 128]), A("s64t", [128, 128])
                Wcs = A("Wcs", [128, 2, 3, 2, 128], BF16)
                dve(lambda e: e.memset(FW[:], 0.0), [], ["FW"])
                tr.dma("sp", c64t[:], I["c64"], w=["c64t"])
                tr.dma("sp", s64t[:], I["s64"], w=["s64t"])
                for g in range(4):
                    p0 = (g % 2) * 64
                    tr.dma("sp", FW[p0:p0 + 64, g // 2, p0:p0 + 64], I["fnet_w"][l, g], r=["FW"], w=["FW"])
                for ct in range(2):
                    mm(PS[4][:, ct * 128:(ct + 1) * 128], c64t[:, :], FW[:, ct, :], True, True, ["c64t", "FW"], [("ps", 4)])
                    mm(PS[5][:, ct * 128:(ct + 1) * 128], s64t[:, :], FW[:, ct, :], True, True, ["s64t", "FW"], [("ps", 5)])
                for Li, L_ in enumerate((Ls, Lp)):
                    nrm = 1.0 / math.sqrt(64.0 * L_)
                    act(lambda e, Li=Li, nrm=nrm: e.activation(out=Wcs[:, Li, 0].rearrange("p a b -> p (a b)"), in_=PS[4][:, 0:256], func=AF.Copy, scale=nrm),
                        [("ps", 4)], ["Wcs"])
                    act(lambda e, Li=Li, nrm=nrm: e.activation(out=Wcs[:, Li, 1].rearrange("p a b -> p (a b)"), in_=PS[5][:, 0:256], func=AF.Copy, scale=-nrm),
                        [("ps", 5)], ["Wcs"])
                    act(lambda e, Li=Li, nrm=nrm: e.activation(out=Wcs[:, Li, 2].rearrange("p a b -> p (a b)"), in_=PS[5][:, 0:256], func=AF.Copy, scale=nrm),
                        [("ps", 5)], ["Wcs"])
                dsl = [A(f"dft{i}", [128, 2, 2, T], BF16) for i in range(4)]
                NP_ = 256
                PWBD = A("PWBD", [128, 2, 128], BF16)
                Ein = A("Ein", [128, 2, NP_ + 16], BF16)
                hsv = A("hsv", [128, 2, 8], BF16)
                sA, sB = A("sA", [128, 2, NP_ + 16]), A("sB", [128, 2, NP_ + 16])
                pw = A("pw", [128, 2, NP_])
                pb = A("pb", [128, 2, NP_], BF16)
                et = A("et", [128, 2, 8])
                dve(lambda e: e.memset(PWBD[:], 0.0), [], ["PWBD"])
                for g in range(4):
                    p0 = (g % 2) * 64
                    tr.dma("pool", PWBD[p0:p0 + 64, g // 2, p0:p0 + 64], I["pool_w"][l, g], r=["PWBD"], w=["PWBD"])

                def pool_seg(t0):
                    n = NP_
                    b = t0 // T
                    if t0 < Ls:
                        col0, left, right = 8 + t0, t0 == 0, t0 + n == Ls
                    else:
                        col0, left, right = POFF[1 + (t0 - Ls) // Lp], True, True
                    zk = [("zc", bb) for bb in range(max(0, b - 1), min(NB, b + 2))]
                    if left:
                        pool(lambda e: e.tensor_copy(out=Ein[:, :, :], in_=zc[:, :, col0 - 8:col0 + n + 8]), zk, ["Ein"])
                    else:
                        pool(lambda e: e.tensor_copy(out=Ein[:, :, 0:8], in_=hsv[:, :, :]), ["hsv"], ["Ein"])
                        pool(lambda e: e.tensor_copy(out=Ein[:, :, 8:n + 16], in_=zc[:, :, col0:col0 + n + 8]), zk + ["Ein"], ["Ein"])
                    if not right:
                        pool(lambda e: e.tensor_copy(out=hsv[:, :, :], in_=zc[:, :, col0 + n - 8:col0 + n]), zk, ["hsv"])
                    E = Ein
                    dve(lambda e: e.tensor_tensor(out=sA[:, :, 1:n + 16], in0=E[:, :, 0:n + 15], in1=E[:, :, 1:n + 16], op=ALU.add), ["Ein"], ["sA"])
                    for c in range(2):
                        dve(lambda e, c=c: e.scalar_tensor_tensor(out=pw[:, c, :], in0=sA[:, c, 8:n + 8], scalar=poolm[:, c, 0:1], in1=E[:, c, 8:n + 8],
                                                                 op0=ALU.mult, op1=ALU.subtract), ["sA", "poolm", "Ein"], [("pw", c)])
                    dve(lambda e: e.tensor_tensor(out=sB[:, :, 2:n + 15], in0=sA[:, :, 1:n + 14], in1=sA[:, :, 3:n + 16], op=ALU.add), ["sA"], ["sB"])
                    dve(lambda e: e.scalar_tensor_tensor(out=pw[:, 0, :], in0=sB[:, 0, 8:n + 8], scalar=poolm[:, 0, 1:2], in1=pw[:, 0, :],
                                                         op0=ALU.mult, op1=ALU.add), ["sB", "poolm", ("pw", 0)], [("pw", 0)])
                    dve(lambda e: e.tensor_tensor(out=sA[:, 1, 4:n + 13], in0=sB[:, 1, 2:n + 11], in1=sB[:, 1, 6:n + 15], op=ALU.add), ["sB", "sA"], ["sA"])
                    dve(lambda e: e.scalar_tensor_tensor(out=pw[:, 1, :], in0=sA[:, 1, 8:n + 8], scalar=poolm[:, 1, 2:3], in1=pw[:, 1, :],
                                                         op0=ALU.mult, op1=ALU.add), ["sA", "poolm", ("pw", 1)], [("pw", 1)])
                    dve(lambda e: e.tensor_tensor(out=sB[:, 1, 8:n + 8], in0=sA[:, 1, 4:n + 4], in1=sA[:, 1, 12:n + 12], op=ALU.add), ["sA", "sB"], ["sB"])
                    dve(lambda e: e.scalar_tensor_tensor(out=pw[:, 1, :], in0=sB[:, 1, 8:n + 8], scalar=poolm[:, 1, 3:4], in1=pw[:, 1, :],
                                                         op0=ALU.mult, op1=ALU.add), ["sB", "poolm", ("pw", 1)], [("pw", 1)])
                    for (flag, cs_, corr, ck) in ((left, 0, corrL, "corrL"), (right, n - 8, corrR, "corrR")):
                        if not flag:
                            continue
                        zz = E[:, :, 8 + cs_:16 + cs_]
                        pk2 = [("pw", 0), ("pw", 1)]
                        dve(lambda e, cs_=cs_, zz=zz: e.tensor_tensor(out=et[:], in0=pw[:, :, cs_:cs_ + 8], in1=zz, op=ALU.add), pk2 + ["Ein"], ["et"])
                        dve(lambda e, corr=corr: e.tensor_tensor(out=et[:], in0=et[:], in1=corr[:], op=ALU.mult), ["et", ck], ["et"])
                        dve(lambda e, cs_=cs_, zz=zz: e.tensor_tensor(out=pw[:, :, cs_:cs_ + 8], in0=et[:], in1=zz, op=ALU.subtract), ["et", "Ein"] + pk2, pk2)
                    pool(lambda e: e.tensor_copy(out=pb[:, :, :], in_=pw[:, :, :]), [("pw", 0), ("pw", 1)], ["pb"])
                    for ct in range(2):
                        mm(PS[4 + ct][:, 0:n], PWBD[:, ct, :], pb[:, ct, :], True, True, ["PWBD", "pb"], [("ps", 4 + ct)])
                        pool_evac(ct, col0, n, b)

                def pool_evac(ct, col0, n, b):
                    dve(lambda e: e.tensor_scalar_mul(out=zc[:, ct, col0:col0 + n], in0=PS[4 + ct][:, 0:n], scalar1=psc[:, l, ct:ct + 1]),
                        [("ps", 4 + ct), "psc", "Ein", "hsv"], [("zc", b)])

                PSEGS = list(range(0, Ltot, NP_))
                dP = A("dftP", [128, 2, 2, Lp], BF16)
                Pb, Qb = A("Pb", [128, 2, T], BF16), A("Qb", [128, 2, T], BF16)
                tr.dma("sp", dP[:].rearrange("p j c k -> p j (c k)"), I["dftP"].rearrange("j p c k -> p j (c k)"), w=["dP"])
                nd = 0
                SYM = NKB >= 2 and NKB % 2 == 0
                NKH = NKB // 2 if SYM else NKB
                if SYM:
                    alt = A("alt", [128, 2], BF16)
                    tr.dma("sp", alt[:, 0:1], I["dftS"][NKB // 2, 0, :, 0, 0:1], w=["alt"], slow=True)
                    for i in range(NTs):
                        for ct in range(2):
                            mm(PS[ct][:, 0:1], zbV[:, i, ct * 128:(ct + 1) * 128], alt[:, 0:1], i == 0, i == NTs - 1, [("zb", i), "alt"], [("ps", ct)])
                    for ct in range(2):
                        act(lambda e, ct=ct: e.activation(out=Pb[:, ct, 0:1], in_=PS[ct][:, 0:1], func=AF.Copy), [("ps", ct)], [("Pb", ct)])
                        mm(PS[4 + ct][:, 0:1], Wcs[:, 0, 0, ct, :], Pb[:, ct, 0:1], True, True, ["Wcs", ("Pb", ct)], [("ps", 4 + ct)])
                        act(lambda e, ct=ct: e.activation(out=yb[:, ct, Ls // 2:Ls // 2 + 1], in_=PS[4 + ct][:, 0:1], func=AF.Copy), [("ps", 4 + ct)], [("yb", ct, NKB // 2)])

                for kb in list(range(NKH)) + [NKB, NKB + 1]:
                    if kb < NKB:
                        n, Li = T, 0
                        for ig in range(NTs // 2):
                            sl_, dk = dsl[nd % 4], ("dft", nd % 4)
                            nd += 1
                            tr.dma("sp", sl_[:].rearrange("p i c k -> p i (c k)"),
                                   I["dftS"][kb, ig * 2:(ig + 1) * 2].rearrange("i p c k -> p i (c k)"), w=[dk])
                            if ig % 4 == 3 and PSEGS:
                                pool_seg(PSEGS.pop(0))
                            for i4 in range(2):
                                i = ig * 2 + i4
                                for ct in range(2):
                                    mm(PS[ct][:, :], zbV[:, i, ct * 128:(ct + 1) * 128], sl_[:, i4, 0, :], i == 0, i == NTs - 1, [("zb", i), dk], [("ps", ct)])
                                    mm(PS[2 + ct][:, :], zbV[:, i, ct * 128:(ct + 1) * 128], sl_[:, i4, 1, :], i == 0, i == NTs - 1, [("zb", i), dk], [("ps", 2 + ct)])
                        c0 = kb * T
                        dkeys = lambda ct: [("yb", ct, kb)]
                    else:
                        qi = kb - NKB
                        n, Li = Lp, 1
                        for j in range(2):
                            i = NTs + 2 * qi + j
                            for ct in range(2):
                                mm(PS[ct][:, 0:n], zbV[:, i, ct * 128:(ct + 1) * 128], dP[:, j, 0, :], j == 0, j == 1, [("zb", i), "dP"], [("ps", ct)])
                                mm(PS[2 + ct][:, 0:n], zbV[:, i, ct * 128:(ct + 1) * 128], dP[:, j, 1, :], j == 0, j == 1, [("zb", i), "dP"], [("ps", 2 + ct)])
                        c0 = Ls + qi * Lp
                        dkeys = lambda ct: [("yb", ct, NBS)]
                        while qi == 1 and PSEGS:
                            pool_seg(PSEGS.pop(0))
                    for ct in range(2):
                        act(lambda e, ct=ct, n=n: e.activation(out=Pb[:, ct, 0:n], in_=PS[ct][:, 0:n], func=AF.Copy), [("ps", ct)], [("Pb", ct)])
                        dve(lambda e, ct=ct, n=n: e.tensor_copy(out=Qb[:, ct, 0:n], in_=PS[2 + ct][:, 0:n]), [("ps", 2 + ct)], [("Qb", ct)])
                    for ct in range(2):
                        mm(PS[4 + ct][:, 0:n], Wcs[:, Li, 0, ct, :], Pb[:, ct, 0:n], True, False, ["Wcs", ("Pb", ct)], [("ps", 4 + ct)])
                        mm(PS[4 + ct][:, 0:n], Wcs[:, Li, 1, ct, :], Qb[:, ct, 0:n], False, True, ["Wcs", ("Qb", ct)], [("ps", 4 + ct)])
                        if ct == 0:
                            act(lambda e, ct=ct, n=n, c0=c0: e.activation(out=yb[:, ct, c0:c0 + n], in_=PS[4 + ct][:, 0:n], func=AF.Copy), [("ps", 4 + ct)], dkeys(ct))
                        else:
                            dve(lambda e, ct=ct, n=n, c0=c0: e.tensor_copy(out=yb[:, ct, c0:c0 + n], in_=PS[4 + ct][:, 0:n]), [("ps", 4 + ct)], dkeys(ct))
                        if SYM and kb < NKB:
                            mm(PS[6 + ct][:, 0:n], Wcs[:, Li, 0, ct, :], Pb[:, ct, 0:n], True, False, ["Wcs", ("Pb", ct)], [("ps", 6 + ct)])
                            mm(PS[6 + ct][:, 0:n], Wcs[:, Li, 2, ct, :], Qb[:, ct, 0:n], False, True, ["Wcs", ("Qb", ct)], [("ps", 6 + ct)])
                            j0 = 1 if kb == 0 else 0
                            hi = Ls - T * kb - j0
                            dsl_ = slice(hi, Ls - T * kb - T, -1)
                            mk = [("yb", ct, NKB - 1 - kb)] + ([("yb", ct, NKB - kb)] if kb >= 1 else [])
                            if ct == 0:
                                dve(lambda e, ct=ct, j0=j0, dsl_=dsl_: e.tensor_copy(out=yb[:, ct, dsl_], in_=PS[6 + ct][:, j0:T]), [("ps", 6 + ct)], mk)
                            else:
                                act(lambda e, ct=ct, j0=j0, dsl_=dsl_: e.activation(out=yb[:, ct, dsl_], in_=PS[6 + ct][:, j0:T], func=AF.Copy), [("ps", 6 + ct)], mk)

        def pass3(l, B1, ya, yb, zc):
            TB = 256
            wmi = I["w_mix_in"][l].rearrange("(k p) n -> p k n", p=128)
            wmo = I["w_mix_out"][l].rearrange("(k p) n -> p k n", p=128)
            with ExitStack() as st:
                A = lambda name, shape, dt=F32: sb(name, shape, dt, st)
                cv = [0]

                def carve(nm, n):
                    if cv[0] + n <= 2 * Ltot:
                        ap = B1[:, cv[0]:cv[0] + n]
                        cv[0] += n
                        return ap
                    return A(nm, [128, n], BF16)[:]

                wd = carve("wd", KC * 512).rearrange("p (k n) -> p k n", n=512)
                h = carve("h3", KC * TB).rearrange("p (k n) -> p k n", n=TB)
                yn0 = carve("yn", KC * TB).rearrange("p (k n) -> p k n", n=TB)
                yn1 = A("yn1", [128, KC, TB], BF16)
                YN = [yn0, yn1[:]]
                vn = carve("vn", 512)
                sqb = carve("sqb", 2 * TB).rearrange("p (k n) -> p k n", n=TB)
                ep = make_epi(st, TB)
                glw = A("glw", [128, 2, 256], BF16)
                SWT = A("SWT", [128, 4, 128], BF16)
                biasT = A("biasT", [128, 2, TB])
                wmall = A("wmall", [128, KC, D], BF16)
                ym = [A(f"ym{i}", [128, 2, TB]) for i in range(2)]
                sig = A("sig", [128, 2, TB])
                rs = A("rs", [128, TB])
                ug = A("ug", [128, 2, TB])
                vg, vsq = A("vg", [128, 512]), A("vsq", [128, 512])
                sm1, sm2, mn_, msq = A("sm1", [128, 8]), A("sm2", [128, 8]), A("mn_", [128, 8]), A("msq", [128, 8])
                tr.dma("pool", wd, wmi[:, :, 768:1280], w=["wd"])
                for k in range(KC):
                    tr.dma("pool", wmall[:, k, :], wmo[:, k, :], w=[("wmall", k)])
                tr.dma("pool", glw[:], I["ssm_glu_w"][l].rearrange("(c p) n -> p c n", p=128), w=["glw"])
                for g in range(4):
                    p0 = (g % 2) * 64
                    for rep in range(TB // 128):
                        tr.dma("sp", biasT[p0:p0 + 64, g // 2, rep * 128:(rep + 1) * 128],
                               I["sgu_b"][l, g:g + 1, :].to_broadcast([64, 128]), w=["biasT"], slow=True)
                SWn = vg[:].rearrange("p (g s) -> p g s", g=4)
                tr.dma("sp", SWn, I["sgu_w"][l].rearrange("g t s -> t g s"), w=["vg"])
                for g in range(4):
                    pe(lambda e, g=g: e.transpose(PS[0][:, g * 128:(g + 1) * 128], SWn[:, g, :], ident[:]), ["vg", "ident"], [("ps", 0)])
                act(lambda e: e.activation(out=SWT[:].rearrange("p a b -> p (a b)"), in_=PS[0][:, :], func=AF.Copy), [("ps", 0)], ["SWT"])
                nwm = 0

                def rms(m, srcs, rkeys, slot):
                    yn = YN[slot]
                    for ct in range(2):
                        act(lambda e, ct=ct: e.activation(out=sqb[:, ct, :], in_=srcs[ct], func=AF.Square), rkeys, [("sqb", ct)])
                    mm(PS[3][:, 0:TB], ones256[:], sqb[:, 0, :], True, False, ["ones256", ("sqb", 0)], [("ps", 3)])
                    mm(PS[3][:, 0:TB], ones256[:], sqb[:, 1, :], False, True, ["ones256", ("sqb", 1)], [("ps", 3)])
                    dve(lambda e: e.tensor_scalar_add(out=rs[:], in0=PS[3][:, 0:TB], scalar1=EPS), [("ps", 3)], ["rs"])
                    act(lambda e: e.activation(out=rs[:], in_=rs[:], func=AF.Ln), ["rs"], ["rs"])
                    act(lambda e: e.activation(out=rs[:], in_=rs[:], func=AF.Exp, scale=-0.5), ["rs"], ["rs"])
                    for ct in range(2):
                        dve(lambda e, ct=ct: e.scalar_tensor_tensor(out=yn[:, 2 * m + ct, :], in0=srcs[ct], scalar=mng[:, l, 2 * m + ct:2 * m + ct + 1],
                                                                   in1=rs[:], op0=ALU.mult, op1=ALU.mult), rkeys + ["rs", "mng"], [("yn", slot, 2 * m + ct)])

                def p3_front(hbk):
                    t0 = hbk * TB
                    n = TB
                    b = t0 // T
                    cond = 0 if t0 < Ls else 1
                    slot = hbk % 2
                    tsl = slice(t0, t0 + n)
                    bg_issue(1)
                    make_h(h, t0, n, l, 1)
                    y0 = ym[0]
                    for dt_ in range(2):
                        for ct in range(2):
                            mm(PS[0][:, dt_ * n:(dt_ + 1) * n], glw[:, ct, dt_ * 128:(dt_ + 1) * 128], ya[:, ct, tsl], ct == 0, ct == 1, ["glw", ("ya", ct, b)], [("ps", 0)])
                    for ct in range(2):
                        for k in range(KC):
                            mm(PS[1][:, ct * n:(ct + 1) * n], wd[:, k, ct * 128:(ct + 1) * 128], h[:, k, :], k == 0, k == KC - 1, ["wd", ("h", k)], [("ps", 1)])
                    for qq in range(2):
                        for k in range(KC):
                            mm(PS[2][:, qq * 256:(qq + 1) * 256], h[:, k, qq * 128:(qq + 1) * 128], wd[:, k, 256:512], k == 0, k == KC - 1, ["wd", ("h", k)], [("ps", 2)])
                    for dt_ in range(2):
                        act(lambda e, dt_=dt_: e.activation(out=sig[:, dt_, :], in_=PS[0][:, dt_ * n:(dt_ + 1) * n], func=AF.Sigmoid, bias=glb[:, l, dt_:dt_ + 1], scale=1.0),
                            [("ps", 0), "glb"], [("sig", dt_)])
                    for ct in range(2):
                        act(lambda e, ct=ct: e.activation(out=ug[:, ct, :], in_=PS[1][:, ct * n:(ct + 1) * n], func=AF.Gelu_apprx_tanh), [("ps", 1)], [("ug", ct)])
                    act(lambda e: e.activation(out=vg[:], in_=PS[2][:, :], func=AF.Gelu_apprx_tanh), [("ps", 2)], ["vg"])
                    act(lambda e: e.activation(out=vsq[:], in_=vg[:], func=AF.Square), ["vg"], ["vsq"])
                    yield
                    for dt_ in range(2):
                        dve(lambda e, dt_=dt_: e.tensor_tensor(out=y0[:, dt_, :], in0=ya[:, dt_, tsl], in1=sig[:, dt_, :], op=ALU.mult),
                            [("ya", dt_, b), ("sig", dt_)], [("ym", 0)])
                    vg3, vsq3 = vg[:].rearrange("p (g c) -> p g c", c=64), vsq[:].rearrange("p (g c) -> p g c", c=64)
                    dve(lambda e: e.tensor_reduce(out=sm1[:], in_=vg3, axis=AX.X, op=ALU.add), ["vg"], ["sm1"])
                    dve(lambda e: e.tensor_reduce(out=sm2[:], in_=vsq3, axis=AX.X, op=ALU.add), ["vsq"], ["sm2"])
                    dve(lambda e: e.tensor_scalar_mul(out=mn_[:], in0=sm1[:], scalar1=1.0 / 64), ["sm1"], ["mn_"])
                    dve(lambda e: e.tensor_tensor(out=msq[:], in0=mn_[:], in1=mn_[:], op=ALU.mult), ["mn_"], ["msq"])
                    dve(lambda e: e.scalar_tensor_tensor(out=sm2[:], in0=sm2[:], scalar=1.0 / 64, in1=msq[:], op0=ALU.mult, op1=ALU.subtract), ["sm2", "msq"], ["sm2"])
                    dve(lambda e: e.tensor_scalar_add(out=sm2[:], in0=sm2[:], scalar1=EPS), ["sm2"], ["sm2"])
                    act(lambda e: e.activation(out=sm2[:], in_=sm2[:], func=AF.Ln), ["sm2"], ["sm2"])
                    act(lambda e: e.activation(out=sm2[:], in_=sm2[:], func=AF.Exp, scale=-0.5), ["sm2"], ["sm2"])
                    yield
                    rms(0, [y0[:, 0, :], y0[:, 1, :]], [("ym", 0)], slot)
                    yield
                    dve(lambda e: e.tensor_tensor(out=vg3, in0=vg3, in1=bc(mn_[:], [128, 8, 64], 2), op=ALU.subtract), ["vg", "mn_"], ["vg"])
                    dve(lambda e: e.tensor_tensor(out=vn.rearrange("p (g c) -> p g c", c=64), in0=vg3, in1=bc(sm2[:], [128, 8, 64], 2), op=ALU.mult),
                        ["vg", "sm2"], ["vn"])
                    for qq in range(2):
                        for g in range(4):
                            p0 = (g % 2) * 64
                            mm(PS[0][p0:p0 + 64, (g // 2) * n + qq * 128:(g // 2) * n + (qq + 1) * 128], vn[:, qq * 256 + g * 64:qq * 256 + (g + 1) * 64], SWT[:, g, :],
                               True, True, ["vn", "SWT"], [("ps", 0)])
                    yield
                    rms(1, [yb[:, 0, tsl], yb[:, 1, tsl]], [("yb", 0, b), ("yb", 1, b)], slot)
                    yield
                    if t0 < Ls:
                        col0 = 8 + t0
                    else:
                        col0 = POFF[1 + (t0 - Ls) // Lp]
                    rms(2, [zc[:, 0, col0:col0 + n], zc[:, 1, col0:col0 + n]], [("zc", b)], slot)
                    yield
                    y3 = ym[1]
                    for ct in range(2):
                        dve(lambda e, ct=ct: e.tensor_tensor(out=y3[:, ct, :], in0=PS[0][:, ct * n:(ct + 1) * n], in1=biasT[:, ct, :], op=ALU.add),
                            [("ps", 0), "biasT"], [("ym", 1)])
                        dve(lambda e, ct=ct: e.tensor_tensor(out=y3[:, ct, :], in0=y3[:, ct, :], in1=ug[:, ct, :], op=ALU.mult),
                            [("ym", 1), ("ug", ct)], [("ym", 1)])
                    rms(3, [y3[:, 0, :], y3[:, 1, :]], [("ym", 1)], slot)

                def p3_back(hbk):
                    t0 = hbk * TB
                    n = TB
                    b = t0 // T
                    cond = 0 if t0 < Ls else 1
                    slot = hbk % 2
                    for c in range(KC):
                        po, pk = PS[4 + c % 2], ("ps", 4 + c % 2)
                        for ic in range(KC):
                            mm(po[:, 0:n], wmall[:, ic, c * 128:(c + 1) * 128], YN[slot][:, ic, :], ic == 0, ic == KC - 1, [("wmall", ic), ("yn", slot, ic)], [pk])
                        epi_chunk(ep, c, po, pk, t0, n, l, 1, cond)
                        yield
                    epi_finish(ep, t0, n, l, 1)

                NHB = Ltot // TB
                for _ in p3_front(0):
                    pass
                for hbk in range(NHB):
                    gb = p3_back(hbk)
                    gf = p3_front(hbk + 1) if hbk + 1 < NHB else iter(())
                    fa, ba = True, True
                    while fa or ba:
                        if fa:
                            try:
                                next(gf)
                            except StopIteration:
                                fa = False
                        if ba:
                            try:
                                next(gb)
                            except StopIteration:
                                ba = False

        MIX = cfg.get("mixer", True)
        for l in range(depth):
            ffn(l, 0, False)
            if MIX:
                mixer(l)
            ffn(l, 1, l == depth - 1)
        tr.finish()
    return nc


def _consts(Ls):
    bf = ml_dtypes.bfloat16
    c = {}
    c["ident"] = np.eye(128, dtype=np.float32)
    k = np.arange(64)
    ang = 2 * np.pi * np.outer(k, k) / 64.0
    c64 = np.zeros((128, 128), np.float32)
    s64 = np.zeros((128, 128), np.float32)
    for g in range(2):
        c64[g * 64:(g + 1) * 64, g * 64:(g + 1) * 64] = np.cos(ang)
        s64[g * 64:(g + 1) * 64, g * 64:(g + 1) * 64] = np.sin(ang)
    c["c64"], c["s64"] = c64, s64
    nbk = Ls // 8
    fwd = np.concatenate([np.arange(nbk), np.arange(32), np.arange(32)]).astype(np.float32)
    rev = np.concatenate([np.arange(nbk)[::-1], np.arange(32)[::-1], np.arange(32)[::-1]]).astype(np.float32)
    c["iota"] = np.ascontiguousarray(np.broadcast_to(np.stack([fwd, rev])[None], (128, 2, nbk + 64))).astype(np.float32)
    quarter = D // 4
    omega = (1.0 / (10000.0 ** (np.arange(quarter, dtype=np.float32) / np.float32(quarter)))).astype(np.float32)
    rows = Ls // 64
    ang_r = (np.arange(rows, dtype=np.float32)[:, None] * omega).astype(np.float32)
    ang_c = (np.arange(64, dtype=np.float32)[:, None] * omega).astype(np.float32)
    emb_r = np.concatenate([np.sin(ang_r), np.cos(ang_r)], -1)
    emb_c = np.concatenate([np.sin(ang_c), np.cos(ang_c)], -1)
    pos = np.concatenate([np.broadcast_to(emb_r[:, None], (rows, 64, D // 2)),
                          np.broadcast_to(emb_c[None], (rows, 64, D // 2))], -1)
    c["pos"] = np.ascontiguousarray(pos.reshape(rows * 64, D).astype(np.float32))

    def dft(L, kblk):
        t = np.arange(L, dtype=np.int64)
        m = np.outer(t, t) % L
        a = 2 * np.pi * m.astype(np.float64) / L
        cs = np.stack([np.cos(a), np.sin(a)], 1)
        nk = L // kblk
        out = cs.reshape(L // 128, 128, 2, nk, kblk).transpose(3, 0, 1, 2, 4)
        return np.ascontiguousarray(out.astype(np.float32).astype(bf))

    c["dftS"] = dft(Ls, T)
    c["dftP"] = dft(256, 256)[0]
    return c


_WKEYS = ["w_ada", "b_ada", "ffn_w_in", "ffn_w_out", "w_mix_in", "w_mix_out", "mix_norm_g", "ssm_lam_re",
          "ssm_lam_im", "ssm_log_dt", "ssm_b_re", "ssm_b_im", "ssm_c_re", "ssm_c_im", "ssm_d", "ssm_glu_w",
          "ssm_glu_b", "fnet_w", "pool_w", "pool_scale", "sgu_w", "sgu_b", "ln_g", "ln_b"]


def run(inputs, cfg, ncores):
    Ls, depth = cfg["Ls"], cfg["depth"]
    nc = build(cfg)
    cst = _consts(Ls)
    f = lambda a: np.ascontiguousarray(np.asarray(a, dtype=np.float32))
    W = {k: f(inputs[k]) for k in _WKEYS}
    in_maps = []
    for c in range(ncores):
        m = dict(W)
        m.update(cst)
        m["xs"] = f(inputs["x_sample"][c])
        m["xp"] = f(np.asarray(inputs["x_prompt"])[2 * c:2 * c + 2].reshape(512, D))
        m["cvec"] = f(np.stack([np.asarray(inputs["c"])[c], np.asarray(inputs["c_ctx"])]))
        m["st_re"] = f(np.asarray(inputs["state_s5_re"])[c])
        m["st_im"] = f(np.asarray(inputs["state_s5_im"])[c])
        in_maps.append(m)
    res = run_bass_kernel_spmd(nc, in_maps, core_ids=list(range(ncores)))
    R = res.results
    ys = np.stack([np.asarray(r["ys"], np.float32) for r in R])
    yp = np.concatenate([np.asarray(r["yp"], np.float32).reshape(2, 256, D) for r in R])
    nre = np.concatenate([np.asarray(r["nre"], np.float32) for r in R])
    nim = np.concatenate([np.asarray(r["nim"], np.float32) for r in R])
    return yp, ys, nre, nim


def kernel(**inputs):
    return run(inputs, {"Ls": 4096, "depth": 4}, 8)
```

```python
import math
from contextlib import ExitStack

import numpy as np
import ml_dtypes

import concourse.bass as bass
import concourse.mybir as mybir
from concourse.bass_utils import run_bass_kernel_spmd

F32 = mybir.dt.float32
F32R = mybir.dt.float32r
BF16 = mybir.dt.bfloat16
I32 = mybir.dt.int32
ALU = mybir.AluOpType
AF = mybir.ActivationFunctionType
AX = mybir.AxisListType

D = 1024
KC = 8
DFF = 2816
FC = 22
T = 512
NMOD = 9
ALPHA = 8.0 ** 0.25
EPS = 1e-5
EPS_LN = EPS / (ALPHA * ALPHA)
TWO_PI = 2.0 * math.pi


class TR:
    def __init__(self, nc, es):
        self.nc = nc
        self.eng = {"pe": nc.tensor, "act": nc.scalar, "dve": nc.vector, "pool": nc.gpsimd, "sp": nc.sync}
        self.semh = {}
        for e in ("pe", "act", "dve", "pool"):
            self.semh[e] = es.enter_context(nc.semaphore("c_" + e))
        self.cnt = {e: 0 for e in ("pe", "act", "dve", "pool")}
        self.waited = {e: {} for e in self.eng}
        self.last_w = {}
        self.readers = {}
        self.ndma = {"sp": 0, "pool": 0, "act": 0, "poolbg": 0}
        self.dma_k = {"sp": 8, "pool": 8, "act": 4, "poolbg": 16}
        self.dma_uses = {}
        for q, k in self.dma_k.items():
            for i in range(k):
                key = ("dma", q, i)
                self.semh[key] = es.enter_context(nc.semaphore(f"d_{q}{i}"))
                self.dma_uses[key] = 0
        self.nwaits = 0

    def _wait(self, e, sk, v):
        if e == "pe" and sk == "pe":
            return
        if self.waited[e].get(sk, 0) >= v:
            return
        self.eng[e].wait_ge(self.semh[sk], v)
        self.waited[e][sk] = v
        self.nwaits += 1

    def _deps(self, e, reads, writes):
        for r in reads:
            t = self.last_w.get(r)
            if t is not None:
                self._wait(e, t[0], t[1])
        for w in writes:
            t = self.last_w.get(w)
            if t is not None:
                self._wait(e, t[0], t[1])
            rd = self.readers.get(w)
            if rd:
                for sk, v in rd.items():
                    self._wait(e, sk, v)

    def _commit(self, tok, reads, writes):
        for r in reads:
            d = self.readers.setdefault(r, {})
            if d.get(tok[0], 0) < tok[1]:
                d[tok[0]] = tok[1]
        for w in writes:
            self.last_w[w] = tok
            self.readers[w] = {}

    def op(self, e, fn, reads=(), writes=()):
        self._deps(e, reads, writes)
        inst = fn(self.eng[e])
        self.cnt[e] += 1
        inst.then_inc(self.semh[e], 1)
        self._commit((e, self.cnt[e]), reads, writes)

    def dma(self, q, out, in_, r=(), w=(), slow=False, bg=False):
        reads, writes = r, w
        self._deps(q, reads, writes)
        qs = q + "bg" if bg else q
        n = self.ndma[qs]
        self.ndma[qs] += 1
        key = ("dma", qs, n % self.dma_k[qs])
        if self.dma_uses[key] > 0:
            self._wait(q, key, 16 * self.dma_uses[key])
        kw = {"allow_slow_non_contiguous": True} if slow else {}
        self.eng[q].dma_start(out=out, in_=in_, **kw).then_inc(self.semh[key], 16)
        self.dma_uses[key] += 1
        self._commit((key, 16 * self.dma_uses[key]), reads, writes)

    def barrier(self):
        toks = [(e, c) for e, c in self.cnt.items() if c > 0]
        toks += [(k, 16 * u) for k, u in self.dma_uses.items() if u > 0 and k[1] != "poolbg"]
        for e in self.eng:
            for sk, v in toks:
                if sk != e:
                    self._wait(e, sk, v)
        keep = {r: t for r, t in self.last_w.items() if isinstance(t[0], tuple) and t[0][1] == "poolbg"}
        self.last_w = keep
        self.readers = {}

    def finish(self):
        for k, u in self.dma_uses.items():
            if u > 0:
                self._wait("sp", k, 16 * u)
        for e, c in self.cnt.items():
            if c > 0:
                self._wait("sp", e, c)


def build(cfg):
    Ls, depth = cfg["Ls"], cfg["depth"]
    NPS, Lp = 2, 256
    NBS = Ls // T
    NB = NBS + 1
    Ltot = Ls + NPS * Lp
    NTI = Ltot // 128
    ZP = Ltot + 8 * 4
    nc = bass.Bass("TRN2", target_bir_lowering=False)

    def din(name, shape, dt=F32):
        return nc.dram_tensor(name, list(shape), dt, kind="ExternalInput").ap()

    def dout(name, shape, dt=F32):
        return nc.dram_tensor(name, list(shape), dt, kind="ExternalOutput").ap()

    I = {}
    I["xs"] = din("xs", [Ls, D])
    I["xp"] = din("xp", [NPS * Lp, D])
    I["pos"] = din("pos", [Ls, D])
    I["cvec"] = din("cvec", [2, D])
    I["st_re"] = din("st_re", [depth, 2, 16, 64])
    I["st_im"] = din("st_im", [depth, 2, 16, 64])
    wshapes = {
        "w_ada": [depth, D, NMOD * D], "b_ada": [depth, NMOD * D],
        "ffn_w_in": [depth, 2, D, 2 * DFF], "ffn_w_out": [depth, 2, DFF, D],
        "w_mix_in": [depth, D, 1280], "w_mix_out": [depth, D, D], "mix_norm_g": [depth, D],
        "ssm_lam_re": [depth, 2, 16, 64], "ssm_lam_im": [depth, 2, 16, 64], "ssm_log_dt": [depth, 2, 16],
        "ssm_b_re": [depth, 2, 16, 64, 16], "ssm_b_im": [depth, 2, 16, 64, 16],
        "ssm_c_re": [depth, 2, 16, 16, 64], "ssm_c_im": [depth, 2, 16, 16, 64],
        "ssm_d": [depth, 256], "ssm_glu_w": [depth, 256, 256], "ssm_glu_b": [depth, 256],
        "fnet_w": [depth, 4, 64, 64], "pool_w": [depth, 4, 64, 64], "pool_scale": [depth, 256],
        "sgu_w": [depth, 4, 128, 128], "sgu_b": [depth, 4, 128],
        "ln_g": [depth, 3, D], "ln_b": [depth, 3, D],
    }
    for k, s in wshapes.items():
        I[k] = din(k, s)
    NKB = Ls // T
    I["ident"] = din("ident", [128, 128])
    I["c64"] = din("c64", [128, 128])
    I["s64"] = din("s64", [128, 128])
    NT_ = Ls // 8 + 64
    I["iota"] = din("iota", [128, 2, NT_])
    I["dftS"] = din("dftS", [NKB, Ls // 128, 128, 2, T], BF16)
    I["dftP"] = din("dftP", [Lp // 128, 128, 2, Lp], BF16)
    O = {
        "ys": dout("ys", [Ls, D]), "yp": dout("yp", [NPS * Lp, D]),
        "nre": dout("nre", [NPS, depth, 2, 16, 64]), "nim": dout("nim", [NPS, depth, 2, 16, 64]),
    }

    scr_in = [[nc.dram_tensor(f"scr_in_{l}_{j}", [128, FC, KC * 256], BF16, kind="Internal").ap() for j in range(2)]
              for l in range(depth)]
    scr_out = [[nc.dram_tensor(f"scr_out_{l}_{j}", [128, KC, FC * 128], BF16, kind="Internal").ap() for j in range(2)]
               for l in range(depth)]

    SCRK = {}
    es = ExitStack()
    with es:
        tr = TR(nc, es)

        nsb = [0]

        def sb(name, shape, dt=F32, st=es):
            nsb[0] += 1
            return st.enter_context(nc.sbuf_tensor(f"s{nsb[0]}_{name}", list(shape), dt))

        PS = [es.enter_context(nc.psum_tensor(f"ps{i}", [128, 512], F32)) for i in range(8)]

        def dve(fn, r=(), w=()):
            tr.op("dve", fn, r, w)

        def act(fn, r=(), w=()):
            tr.op("act", fn, r, w)

        def pool(fn, r=(), w=()):
            tr.op("pool", fn, r, w)

        def pe(fn, r=(), w=()):
            tr.op("pe", fn, r, w)

        def mm(out, lhsT, rhs, start, stop, r, w):
            tr.op("pe", lambda e: e.matmul(out, lhsT=lhsT, rhs=rhs, start=start, stop=stop), r, w)

        X = sb("X", [128, KC, Ltot], BF16)
        ident = sb("ident", [128, 128])
        identb = sb("identb", [128, 128], BF16)
        onesD = sb("onesD", [128, 128], BF16)
        ones256 = sb("ones256", [128, 128], BF16)
        modT = sb("modT", [128, depth, NMOD, KC, 2])
        lng = sb("lng", [128, depth * 3, KC])
        lnb = sb("lnb", [128, depth * 3, KC])
        mng = sb("mng", [128, depth, KC])
        ssd = sb("ssd", [128, depth, 2])
        glb = sb("glb", [128, depth, 2])
        psc = sb("psc", [128, depth, 2])
        corrL = sb("corrL", [128, 2, 8])
        corrR = sb("corrR", [128, 2, 8])
        poolm = sb("poolm", [128, 2, 4])
        mskE = sb("mskE", [32, 1])
        mskO = sb("mskO", [32, 1])

        def xkey(c, b):
            return ("X", c, b)

        def xkeys(c, t0, n):
            return [("X", c, hb_) for hb_ in range(t0 // 256, (t0 + n + 255) // 256)]

        tr.dma("sp", ident[:], I["ident"], w=["ident"])
        act(lambda e: e.activation(out=identb[:], in_=ident[:], func=AF.Copy), ["ident"], ["identb"])
        dve(lambda e: e.memset(onesD[:], 1.0 / D), w=["onesD"])
        dve(lambda e: e.memset(ones256[:], 1.0 / 256), w=["ones256"])

        tr.dma("sp", lng[:], I["ln_g"].rearrange("l j (k p) -> p (l j) k", p=128), w=["lng"], slow=True)
        tr.dma("sp", lnb[:], I["ln_b"].rearrange("l j (k p) -> p (l j) k", p=128), w=["lnb"], slow=True)
        tr.dma("sp", mng[:], I["mix_norm_g"].rearrange("l (k p) -> p l k", p=128), w=["mng"], slow=True)
        tr.dma("sp", ssd[:], I["ssm_d"].rearrange("l (k p) -> p l k", p=128), w=["ssd"], slow=True)
        tr.dma("sp", glb[:], I["ssm_glu_b"].rearrange("l (k p) -> p l k", p=128), w=["glb"], slow=True)
        tr.dma("sp", psc[:], I["pool_scale"].rearrange("l (k p) -> p l k", p=128), w=["psc"], slow=True)
        dve(lambda e: e.memset(poolm[:], 0.0), w=["poolm"])
        dve(lambda e: e.memset(corrL[:], 1.0), w=["corrL"])
        dve(lambda e: e.memset(corrR[:], 1.0), w=["corrR"])
        for gi, wv in enumerate((2, 4, 8, 16)):
            ch, p0 = gi // 2, (gi % 2) * 64
            dve(lambda e, ch=ch, p0=p0, gi=gi, wv=wv: e.memset(poolm[p0:p0 + 64, ch, gi:gi + 1], 1.0 / wv),
                ["poolm"], ["poolm"])
            for t in range(8):
                cntL = t + wv // 2 - max(t - wv // 2, 0)
                if cntL != wv:
                    dve(lambda e, ch=ch, p0=p0, t=t, v=wv / cntL: e.memset(corrL[p0:p0 + 64, ch, t:t + 1], v),
                        ["corrL"], ["corrL"])
                dist = 8 - t
                cntR = min(wv // 2, dist) + wv // 2
                if cntR != wv:
                    dve(lambda e, ch=ch, p0=p0, t=t, v=wv / cntR: e.memset(corrR[p0:p0 + 64, ch, t:t + 1], v),
                        ["corrR"], ["corrR"])
        dve(lambda e: e.memset(mskE[:], 0.0), w=["mskE"])
        dve(lambda e: e.memset(mskO[:], 1.0), w=["mskO"])
        dve(lambda e: e.memset(mskE[0:16, :], 1.0), ["mskE"], ["mskE"])
        dve(lambda e: e.memset(mskO[0:16, :], 0.0), ["mskO"], ["mskO"])

        with ExitStack() as p0s:
            cvT = sb("cvT", [128, 2, KC], F32, p0s)
            scv = sb("scv", [128, 2, KC], F32, p0s)
            bada = sb("bada", [128, depth, NMOD * KC], F32, p0s)
            wad = [sb(f"wad{i}", [128, KC, 512], F32, p0s) for i in range(2)]
            xin = [sb(f"xin{i}", [128, D], F32, p0s) for i in range(2)]
            pin = [sb(f"pin{i}", [128, D], F32, p0s) for i in range(2)]
            stg = [sb(f"stg{i}", [128, 2816], F32, p0s) for i in range(2)]
            wall = [sb(f"wall{i}", [128, 11 * KC * 256], BF16, p0s) for i in range(1)]
            for cnd in range(2):
                tr.dma("sp", cvT[:, cnd, :], I["cvec"][cnd].rearrange("(k p) -> p k", p=128), w=["cvT"], slow=True)
            tr.dma("sp", bada[:], I["b_ada"].rearrange("l (m p) -> p l m", p=128), w=["bada"], slow=True)
            act(lambda e: e.activation(out=scv[:], in_=cvT[:], func=AF.Silu), ["cvT"], ["scv"])
            WA, WB, WC = [], [], []

            def mod_item(it, l, m, half):
                wt = wad[it % 2]
                wk = ("wad", it % 2)
                col0 = m * D + half * 512
                tr.dma("sp" if it % 2 == 0 else "act", wt[:],
                       I["w_ada"][l, :, col0:col0 + 512].rearrange("(k p) n -> p k n", p=128), w=[wk])
                ps = PS[it % 2]
                pk = ("ps", it % 2)
                for cc in range(4):
                    for k in range(KC):
                        mm(ps[:, cc * 2:cc * 2 + 2], wt[:, k, cc * 128:(cc + 1) * 128], scv[:, :, k],
                           k == 0, k == KC - 1, [wk, "scv"], [pk])
                for cnd in range(2):
                    dve(lambda e, cnd=cnd: e.tensor_tensor(
                        out=modT[:, l, m, half * 4:half * 4 + 4, cnd], in0=ps[:, cnd:8:2],
                        in1=bada[:, l, m * KC + half * 4:m * KC + half * 4 + 4], op=ALU.add),
                        [pk, "bada"], ["modT"])

            it = 0
            for l in range(depth):
                for m in range(NMOD):
                    for half in range(2):
                        WA.append(lambda it=it, l=l, m=m, half=half: mod_item(it, l, m, half))
                        it += 1

            def in_item(i):
                xt = xin[i % 2]
                xk = ("xin", i % 2)
                if i < Ls // 128:
                    tr.dma("sp", xt[:], I["xs"][i * 128:(i + 1) * 128, :], w=[xk])
                    tr.dma("act", pin[i % 2][:], I["pos"][i * 128:(i + 1) * 128, :], w=[("pin", i % 2)])
                    pool(lambda e, xt=xt, pt=pin[i % 2]: e.tensor_tensor(out=xt[:], in0=xt[:], in1=pt[:], op=ALU.add),
                         [xk, ("pin", i % 2)], [xk])
                else:
                    j = i - Ls // 128
                    tr.dma("sp", xt[:], I["xp"][j * 128:(j + 1) * 128, :], w=[xk])
                for hb in range(2):
                    ps = PS[2 + (2 * i + hb) % 4]
                    pk = ("ps", 2 + (2 * i + hb) % 4)
                    for c4 in range(4):
                        c = hb * 4 + c4
                        pe(lambda e, ps=ps, c4=c4, c=c, xt=xt: e.transpose(ps[:, c4 * 128:(c4 + 1) * 128],
                                                                          xt[:, c * 128:(c + 1) * 128], ident[:]),
                           [xk, "ident"], [pk])
                    wr = [k_ for c4 in range(4) for k_ in xkeys(hb * 4 + c4, i * 128, 128)]
                    if hb == 0:
                        act(lambda e, ps=ps, hb=hb, i=i: e.activation(
                            out=X[:, hb * 4:hb * 4 + 4, i * 128:(i + 1) * 128],
                            in_=ps[:].rearrange("p (c t) -> p c t", c=4), func=AF.Copy), [pk], wr)
                    else:
                        dve(lambda e, ps=ps, hb=hb, i=i: e.tensor_copy(
                            out=X[:, hb * 4:hb * 4 + 4, i * 128:(i + 1) * 128],
                            in_=ps[:].rearrange("p (c t) -> p c t", c=4)), [pk], wr)

            for i in range(NTI):
                WB.append(lambda i=i: in_item(i))

            cst_ = {"nst": 0, "ncast": 0}

            def cast(out, in_, r, w):
                k = cst_["ncast"] % 3
                cst_["ncast"] += 1
                if k == 0:
                    act(lambda e: e.activation(out=out, in_=in_, func=AF.Copy), r, w)
                elif k == 1:
                    dve(lambda e: e.tensor_copy(out=out, in_=in_), r, w)
                else:
                    pool(lambda e: e.tensor_copy(out=out, in_=in_), r, w)

            wi0 = I["ffn_w_in"][0, 0].rearrange("(k p) (h n) -> p k h n", p=128, h=2)
            wo0 = I["ffn_w_out"][0, 0].rearrange("(f p) n -> p f n", p=128)
            wl0, wk0 = wall[0], ("wall", 0)

            def cv_in(fh, k):
                sg_, sk = stg[cst_["nst"] % 2], ("stg", cst_["nst"] % 2)
                cst_["nst"] += 1
                wv = wl0[:].rearrange("p (f k n) -> p f k n", f=11, k=KC)
                tr.dma("sp" if cst_["nst"] % 2 else "act", sg_[:].rearrange("p (h n) -> p h n", h=2), wi0[:, k, :, fh * 1408:(fh + 1) * 1408], w=[sk])
                cast(wv[:, :, k, :].rearrange("p f (h c) -> p h f c", h=2), sg_[:].rearrange("p (h f c) -> p h f c", h=2, c=128), [sk], [wk0])

            def cv_in_store(fh):
                tr.dma("sp", scr_in[0][0][:, fh * 11:(fh + 1) * 11, :].rearrange("p f n -> p (f n)"), wl0[:], r=[wk0], w=[("scr_in", 0, 0, fh)])
                SCRK.setdefault(("in", 0, 0), []).append(("scr_in", 0, 0, fh))

            def cv_out(fg):
                sg_, sk = stg[cst_["nst"] % 2], ("stg", cst_["nst"] % 2)
                cst_["nst"] += 1
                wv = wl0[:].rearrange("p (c f n) -> p c f n", c=KC, f=FC)
                tr.dma("sp" if cst_["nst"] % 2 else "act", sg_[:, 0:2048].rearrange("p (f n) -> p f n", f=2), wo0[:, fg * 2:fg * 2 + 2, :], w=[sk])
                cast(wv[:, :, fg * 2:fg * 2 + 2, :], sg_[:, 0:2048].rearrange("p (f c n) -> p c f n", f=2, n=128), [sk], [wk0])

            def cv_out_store():
                tr.dma("sp", scr_out[0][0][:].rearrange("p c n -> p (c n)"), wl0[:], r=[wk0], w=[("scr_out", 0, 0, 0)])
                SCRK.setdefault(("out", 0, 0), []).append(("scr_out", 0, 0, 0))

            for fh in range(2):
                for k in range(KC):
                    WC.append(lambda fh=fh, k=k: cv_in(fh, k))
                WC.append(lambda fh=fh: cv_in_store(fh))
            for fg in range(11):
                WC.append(lambda fg=fg: cv_out(fg))
            WC.append(cv_out_store)
            nA = len(WA)
            for ia in range(nA):
                WA[ia]()
                tb = (ia + 1) * len(WB) // nA - ia * len(WB) // nA
                for _ in range(tb):
                    WB.pop(0)()
                tcv = (ia + 1) * 29 // nA - ia * 29 // nA
                for _ in range(tcv):
                    if WC:
                        WC.pop(0)()
            while WB:
                WB.pop(0)()
            while WC:
                WC.pop(0)()
            for l in range(depth):
                for j in range(3):
                    dve(lambda e, l=l, j=j: e.tensor_scalar_add(out=modT[:, l, 3 * j + 1], in0=modT[:, l, 3 * j + 1],
                                                                scalar1=1.0), ["modT"], ["modT"])
                    gsc = (1.0 if j == 1 else 0.5) / ALPHA
                    dve(lambda e, l=l, j=j, gsc=gsc: e.tensor_scalar_mul(out=modT[:, l, 3 * j + 2],
                                                                         in0=modT[:, l, 3 * j + 2], scalar1=gsc),
                        ["modT"], ["modT"])
        tr.barrier()

        def conv_bg_list(l, j):
            wi = I["ffn_w_in"][l, j].rearrange("(k p) (h f c) -> p k h f c", p=128, h=2, c=128)
            wo = I["ffn_w_out"][l, j].rearrange("(f p) (c n) -> p f c n", p=128, n=128)
            si = scr_in[l][j].rearrange("p f (k h c) -> p f k h c", k=KC, h=2)
            so = scr_out[l][j].rearrange("p c (f n) -> p c f n", n=128)
            lst = []
            for k in range(KC):
                for hh in range(2):
                    key = ("scr_in", l, j, k * 2 + hh)
                    SCRK.setdefault(("in", l, j), []).append(key)
                    lst.append(lambda k=k, hh=hh, key=key: tr.dma("pool", si[:, :, k, hh, :], wi[:, k, hh, :, :], w=[key], bg=True))
            for c in range(KC):
                key = ("scr_out", l, j, c)
                SCRK.setdefault(("out", l, j), []).append(key)
                lst.append(lambda c=c, key=key: tr.dma("pool", so[:, c, :, :], wo[:, :, c, :], w=[key], bg=True))
            return lst

        BG = []

        def bg_issue(n):
            for _ in range(n):
                if BG:
                    BG.pop(0)()

        def col(t, *idx):
            a = t
            sl = tuple([slice(None)] + list(idx[:-1]) + [slice(idx[-1], idx[-1] + 1)])
            return t[sl]

        class Epi:
            pass

        def make_epi(st, W_=T):
            ep = Epi()
            ep.rp = sb("rp", [128, KC, W_], F32, st)
            ep.rb = [sb(f"rb{i}", [128, W_], BF16, st) for i in range(2)]
            ep.r2b = [sb(f"r2b{i}", [128, W_], BF16, st) for i in range(2)]
            ep.mean = sb("mean_sb", [128, W_], F32, st)
            ep.m2 = sb("m2", [128, W_], F32, st)
            ep.rstd = sb("rstd", [128, W_], F32, st)
            ep.t1 = [sb(f"t1_{i}", [128, W_], F32, st) for i in range(2)]
            ep.pend = None
            return ep

        def epi_chunk(ep, c, psO, pk, t0, n, l, sj, cond):
            blk = slice(t0, t0 + n)
            b = t0 // T
            ep.n = n
            dve(lambda e: e.scalar_tensor_tensor(out=ep.rp[:, c, :], in0=psO[:, 0:n], scalar=modT[:, l, 3 * sj + 2, c, cond:cond + 1],
                                                 in1=X[:, c, blk], op0=ALU.mult, op1=ALU.add),
                [pk, "modT"] + xkeys(c, t0, n), [("rp", c)])
            s = c % 2
            act(lambda e: e.activation(out=ep.rb[s][:], in_=ep.rp[:, c, :], func=AF.Copy), [("rp", c)], [("rb", s)])
            act(lambda e: e.activation(out=ep.r2b[s][:], in_=ep.rp[:, c, :], func=AF.Square), [("rp", c)], [("r2b", s)])
            epi_flush(ep)
            ep.pend = (c, s)

        def epi_flush(ep):
            if ep.pend is None:
                return
            c, s = ep.pend
            mm(PS[6][:, 0:ep.n], onesD[:], ep.rb[s][:], c == 0, c == KC - 1, [("rb", s), "onesD"], [("ps", 6)])
            mm(PS[7][:, 0:ep.n], onesD[:], ep.r2b[s][:], c == 0, c == KC - 1, [("r2b", s), "onesD"], [("ps", 7)])
            ep.pend = None

        def epi_finish(ep, t0, n, l, sj, final_out=None, split=False):
            epi_flush(ep)
            blk = slice(t0, t0 + n)
            b = t0 // T
            act(lambda e: e.activation(out=ep.mean[:], in_=PS[6][:, 0:n], func=AF.Copy), [("ps", 6)], ["mean"])
            dve(lambda e: e.tensor_tensor(out=ep.m2[:], in0=ep.mean[:], in1=ep.mean[:], op=ALU.mult), ["mean"], ["m2"])
            dve(lambda e: e.tensor_tensor(out=ep.m2[:], in0=PS[7][:, 0:n], in1=ep.m2[:], op=ALU.subtract),
                [("ps", 7), "m2"], ["m2"])
            dve(lambda e: e.tensor_scalar(out=ep.m2[:], in0=ep.m2[:], scalar1=EPS_LN, scalar2=None, op0=ALU.add),
                ["m2"], ["m2"])
            act(lambda e: e.activation(out=ep.m2[:], in_=ep.m2[:], func=AF.Ln), ["m2"], ["m2"])
            act(lambda e: e.activation(out=ep.rstd[:], in_=ep.m2[:], func=AF.Exp, scale=-0.5), ["m2"], ["rstd"])
            for c in range(KC):
                s = c % 2
                E_ = dve if (split and s == 1) else pool
                E_(lambda e, c=c, s=s: e.tensor_tensor(out=ep.t1[s][:], in0=ep.rp[:, c, :], in1=ep.mean[:], op=ALU.subtract),
                   [("rp", c), "mean"], [("t1", s)])
                E_(lambda e, c=c, s=s: e.tensor_tensor(out=ep.t1[s][:], in0=ep.t1[s][:], in1=ep.rstd[:], op=ALU.mult),
                   [("t1", s), "rstd"], [("t1", s)])
                if final_out is None:
                    E_(lambda e, c=c, s=s: e.tensor_scalar(out=X[:, c, blk], in0=ep.t1[s][:], scalar1=lng[:, l * 3 + sj, c:c + 1],
                                                           scalar2=lnb[:, l * 3 + sj, c:c + 1], op0=ALU.mult, op1=ALU.add),
                       [("t1", s), "lng", "lnb"], xkeys(c, t0, n))
                else:
                    E_(lambda e, c=c, s=s: e.tensor_scalar(out=ep.rp[:, c, :], in0=ep.t1[s][:], scalar1=lng[:, l * 3 + sj, c:c + 1],
                                                           scalar2=lnb[:, l * 3 + sj, c:c + 1], op0=ALU.mult, op1=ALU.add),
                       [("t1", s), "lng", "lnb", ("rp", c)], [("rp", c)])

        def make_h(hbuf, t0, n, l, sj):
            b = t0 // T
            cond = 0 if t0 < Ls else 1
            blk = slice(t0, t0 + n)
            for c in range(KC):
                dve(lambda e, c=c: e.tensor_scalar(out=hbuf[:, c, :], in0=X[:, c, blk],
                                                   scalar1=modT[:, l, 3 * sj + 1, c, cond:cond + 1],
                                                   scalar2=modT[:, l, 3 * sj, c, cond:cond + 1],
                                                   op0=ALU.mult, op1=ALU.add),
                    ["modT"] + xkeys(c, t0, n), [("h", c)])

        def ffn(l, j, final):
            sj = 0 if j == 0 else 2
            with ExitStack() as st:
                ep = make_epi(st)
                hbufs = [sb(f"h{i}", [128, KC, T], BF16, st) for i in range(2)]
                hid = sb("hid", [128, FC, T], BF16, st)
                NWI, NWO = 3, 2
                win = [sb(f"win{i}", [128, 2, KC, 256], BF16, st) for i in range(NWI)]
                wout = [sb(f"wout{i}", [128, FC, 128], BF16, st) for i in range(NWO)]
                sg = [sb(f"sg{i}", [128, T], F32, st) for i in range(2)]
                xo = ep.rp if final else None
                ot = [sb(f"ot{i}", [128, D], F32, st) for i in range(2)] if final else None
                nwi = 0
                nwo = 0
                npa = 0

                def mk_h(b):
                    hh = hbufs[b % 2]
                    cond = 0 if b < NBS else 1
                    for c in range(KC):
                        dve(lambda e, c=c, hh=hh: e.tensor_scalar(out=hh[:, c, :], in0=X[:, c, b * T:(b + 1) * T],
                                                                scalar1=modT[:, l, 3 * sj + 1, c, cond:cond + 1],
                                                                scalar2=modT[:, l, 3 * sj, c, cond:cond + 1],
                                                                op0=ALU.mult, op1=ALU.add),
                            ["modT"] + xkeys(c, b * T, T), [("h", b % 2, c)])

                mk_h(0)
                for b in range(NB):
                    cond = 0 if b < NBS else 1
                    h = hbufs[b % 2]
                    for f in range(FC):
                        if f % 2 == 0:
                            s = nwi % NWI
                            nwi += 1
                            wk = ("win", s)
                            tr.dma("sp", win[s][:].rearrange("p f k n -> p (f k n)"),
                                   scr_in[l][j][:, f:f + 2, :].rearrange("p f n -> p (f n)"), r=SCRK[("in", l, j)], w=[wk])
                        wt = win[s][:, f % 2]
                        pa, pg = PS[(npa % 2) * 2], PS[(npa % 2) * 2 + 1]
                        ka, kg = ("ps", (npa % 2) * 2), ("ps", (npa % 2) * 2 + 1)
                        npa += 1
                        for k in range(KC):
                            mm(pa[:, :], wt[:, k, 0:128], h[:, k, :], k == 0, k == KC - 1, [wk, ("h", b % 2, k)], [ka])
                        for k in range(KC):
                            mm(pg[:, :], wt[:, k, 128:256], h[:, k, :], k == 0, k == KC - 1, [wk, ("h", b % 2, k)], [kg])
                        q = f % 2
                        act(lambda e, pg=pg, q=q: e.activation(out=sg[q][:], in_=pg[:, :], func=AF.Silu), [kg], [("sg", q)])
                        dve(lambda e, pa=pa, q=q, f=f: e.tensor_tensor(out=hid[:, f, :], in0=pa[:, :], in1=sg[q][:], op=ALU.mult),
                            [ka, ("sg", q)], [("hid", f)])
                    if b + 1 < NB:
                        mk_h(b + 1)
                    for c in range(KC):
                        s = nwo % NWO
                        nwo += 1
                        wk = ("wout", s)
                        tr.dma("sp", wout[s][:].rearrange("p f n -> p (f n)"), scr_out[l][j][:, c, :], r=SCRK[("out", l, j)], w=[wk])
                        po = PS[4 + c % 2]
                        pk = ("ps", 4 + c % 2)
                        for f in range(FC):
                            mm(po[:, :], wout[s][:, f, :], hid[:, f, :], f == 0, f == FC - 1, [wk, ("hid", f)], [pk])
                        epi_chunk(ep, c, po, pk, b * T, T, l, sj, cond)
                    epi_finish(ep, b * T, T, l, sj, xo, split=(b == NB - 1))
                    if final:
                        for q in range(4):
                            o = ot[q % 2]
                            ok = ("ot", q % 2)
                            for hb in range(2):
                                ps = PS[hb]
                                pk = ("ps", hb)
                                for c4 in range(4):
                                    c = hb * 4 + c4
                                    pe(lambda e, ps=ps, c4=c4, c=c, q=q: e.transpose(
                                        ps[:, c4 * 128:(c4 + 1) * 128], xo[:, c, q * 128:(q + 1) * 128], ident[:]),
                                       [("rp", c), "ident"], [pk])
                                if hb == 0:
                                    act(lambda e, ps=ps, o=o: e.activation(out=o[:, 0:512], in_=ps[:, :], func=AF.Copy), [pk], [ok])
                                else:
                                    dve(lambda e, ps=ps, o=o: e.tensor_copy(out=o[:, 512:1024], in_=ps[:, :]), [pk], [ok])
                            t0 = b * T + q * 128
                            if t0 < Ls:
                                tr.dma("sp", O["ys"][t0:t0 + 128, :], o[:], r=[ok])
                            else:
                                tr.dma("sp", O["yp"][t0 - Ls:t0 - Ls + 128, :], o[:], r=[ok])
            tr.barrier()

        NBLK = Ls // 8
        SEQS = [(0, NBLK, True, -1), (Ls, Lp // 8, False, 0), (Ls + Lp, Lp // 8, False, 1)]
        HB = []
        _c = 0
        for (_o, _n, _s, _q) in SEQS:
            HB.append(_c)
            _c += _n + 1
        NCOL = _c
        POFF = [8, Ls + 16, Ls + 16 + Lp + 8]

        def bc(ap, shape, axis):
            return ap.unsqueeze(axis).to_broadcast(list(shape))

        def mixer(l):
            with ExitStack() as ms:
                B1 = sb("B1", [128, 2 * Ltot], BF16, ms)
                ya = sb("ya", [128, 2, Ltot], BF16, ms)
                zaV = B1[:].rearrange("p (c t) -> p c t", c=2)
                zbV = B1[:].rearrange("p (i n) -> p i n", n=256)
                wmi = I["w_mix_in"][l].rearrange("(k p) n -> p k n", p=128)
                with ExitStack() as st:
                    wa = sb("wa", [128, KC, 256], BF16, st)
                    h = sb("hmix", [128, KC, T], BF16, st)
                    tr.dma("pool", wa[:], wmi[:, :, 0:256], w=["wa"])
                    BG.extend(conv_bg_list(l, 1))
                    if l + 1 < depth:
                        BG.extend(conv_bg_list(l + 1, 0))
                    for b in range(NB):
                        bg_issue(2)
                        make_h(h, b * T, T, l, 1)
                        for ct in range(2):
                            ps, pk = PS[ct], ("ps", ct)
                            for k in range(KC):
                                mm(ps[:, :], wa[:, k, ct * 128:(ct + 1) * 128], h[:, k, :], k == 0, k == KC - 1,
                                   ["wa", ("h", k)], [pk])
                            if ct == 0:
                                act(lambda e, ps=ps, b=b, ct=ct: e.activation(out=zaV[:, ct, b * T:(b + 1) * T], in_=ps[:, :], func=AF.Copy),
                                    [pk], [("za", ct, b)])
                            else:
                                dve(lambda e, ps=ps, b=b, ct=ct: e.tensor_copy(out=zaV[:, ct, b * T:(b + 1) * T], in_=ps[:, :]),
                                    [pk], [("za", ct, b)])
                tr.barrier()
                s5(l, zaV, ya)
                tr.barrier()
                zc = sb("zc", [128, 2, ZP], BF16, ms)
                yb = sb("yb", [128, 2, Ltot], BF16, ms)
                fnet_pass(l, zbV, zc, yb)
                tr.barrier()
                pass3(l, B1, ya, yb, zc)
                bg_issue(len(BG))
            tr.barrier()

        def s5(l, zaV, ya):
            with ExitStack() as st:
                A = lambda name, shape, dt=F32: sb(name, shape, dt, st)
                iota = A("iota", [128, 2, NT_])
                tr.dma("sp", iota[:], I["iota"], w=["iota"])
                lre, lim, dtc = A("lre", [128, 16]), A("lim", [128, 16]), A("dtc", [128, 16])
                h0re, h0im = A("h0re", [128, 16]), A("h0im", [128, 16])
                for d in range(2):
                    sl = slice(d * 8, (d + 1) * 8)
                    tr.dma("sp", lre[:, sl], I["ssm_lam_re"][l, d].rearrange("(i q) p -> (q p) i", q=2), w=["lre"], slow=True)
                    tr.dma("sp", lim[:, sl], I["ssm_lam_im"][l, d].rearrange("(i q) p -> (q p) i", q=2), w=["lim"], slow=True)
                    tr.dma("sp", h0re[:, sl], I["st_re"][l, d].rearrange("(i q) p -> (q p) i", q=2), w=["h0re"], slow=True)
                    tr.dma("sp", h0im[:, sl], I["st_im"][l, d].rearrange("(i q) p -> (q p) i", q=2), w=["h0im"], slow=True)
                    for q in range(2):
                        tr.dma("sp", dtc[q * 64:(q + 1) * 64, sl],
                               I["ssm_log_dt"][l, d].rearrange("(i q) -> q i", q=2)[q:q + 1, :].to_broadcast([64, 8]),
                               w=["dtc"], slow=True)
                act(lambda e: e.activation(out=dtc[:], in_=dtc[:], func=AF.Exp), ["dtc"], ["dtc"])
                lrdt, ang1 = A("lrdt", [128, 16]), A("ang1", [128, 16])
                dve(lambda e: e.tensor_tensor(out=lrdt[:], in0=lre[:], in1=dtc[:], op=ALU.mult), ["lre", "dtc"], ["lrdt"])
                dve(lambda e: e.tensor_tensor(out=ang1[:], in0=lim[:], in1=dtc[:], op=ALU.mult), ["lim", "dtc"], ["ang1"])
                mag = A("mag", [128, 16, 9])
                for e_ in range(9):
                    act(lambda e, e_=e_: e.activation(out=mag[:, :, e_], in_=lrdt[:], func=AF.Exp, scale=float(e_)), ["lrdt"], ["mag"])
                ysn, ycs = A("ysn", [128, 16, 9]), A("ycs", [128, 16, 9])
                for e_ in range(9):
                    dve(lambda e, e_=e_: e.tensor_scalar_mul(out=ysn[:, :, e_], in0=ang1[:], scalar1=float(e_) / TWO_PI), ["ang1"], ["ysn"])
                dve(lambda e: e.tensor_scalar_add(out=ycs[:], in0=ysn[:], scalar1=0.25), ["ysn"], ["ycs"])
                ki, kf = A("ki", [128, 16 * 9], I32), A("kf", [128, 16 * 9])

                def frac(y, key):
                    yv = y[:].rearrange("p a b -> p (a b)")
                    dve(lambda e: e.tensor_copy(out=ki[:], in_=yv), [key], ["ki"])
                    dve(lambda e: e.tensor_copy(out=kf[:], in_=ki[:]), ["ki"], ["kf"])
                    dve(lambda e: e.tensor_tensor(out=yv, in0=yv, in1=kf[:], op=ALU.subtract), [key, "kf"], [key])
                    dve(lambda e: e.tensor_scalar(out=yv, in0=yv, scalar1=0.49999, scalar2=-0.49999, op0=ALU.min, op1=ALU.max), [key], [key])

                frac(ysn, "ysn")
                frac(ycs, "ycs")
                sn, cs = A("sn", [128, 16, 9]), A("cs", [128, 16, 9])
                act(lambda e: e.activation(out=sn[:], in_=ysn[:], func=AF.Sin, scale=TWO_PI), ["ysn"], ["sn"])
                act(lambda e: e.activation(out=cs[:], in_=ycs[:], func=AF.Sin, scale=TWO_PI), ["ycs"], ["cs"])
                pwr, pwi = A("pwr", [128, 16, 9]), A("pwi", [128, 16, 9])
                dve(lambda e: e.tensor_tensor(out=pwr[:], in0=mag[:], in1=cs[:], op=ALU.mult), ["mag", "cs"], ["pwr"])
                dve(lambda e: e.tensor_tensor(out=pwi[:], in0=mag[:], in1=sn[:], op=ALU.mult), ["mag", "sn"], ["pwi"])
                am1, den, t_a, t_b = A("am1", [128, 16]), A("den", [128, 16]), A("t_a", [128, 16]), A("t_b", [128, 16])
                cfr, cfi = A("cfr", [128, 16]), A("cfi", [128, 16])
                ar, ai = pwr[:, :, 1], pwi[:, :, 1]
                TT = lambda o, a, b_, op, r, w: dve(lambda e: e.tensor_tensor(out=o, in0=a, in1=b_, op=op), r, w)
                dve(lambda e: e.tensor_scalar_add(out=am1[:], in0=ar, scalar1=-1.0), ["pwr"], ["am1"])
                TT(den[:], lre[:], lre[:], ALU.mult, ["lre"], ["den"])
                TT(t_a[:], lim[:], lim[:], ALU.mult, ["lim"], ["t_a"])
                TT(den[:], den[:], t_a[:], ALU.add, ["den", "t_a"], ["den"])
                dve(lambda e: e.reciprocal(out=den[:], in_=den[:]), ["den"], ["den"])
                TT(t_a[:], am1[:], lre[:], ALU.mult, ["am1", "lre"], ["t_a"])
                TT(t_b[:], ai, lim[:], ALU.mult, ["pwi", "lim"], ["t_b"])
                TT(t_a[:], t_a[:], t_b[:], ALU.add, ["t_a", "t_b"], ["t_a"])
                TT(cfr[:], t_a[:], den[:], ALU.mult, ["t_a", "den"], ["cfr"])
                TT(t_a[:], ai, lre[:], ALU.mult, ["pwi", "lre"], ["t_a"])
                TT(t_b[:], am1[:], lim[:], ALU.mult, ["am1", "lim"], ["t_b"])
                TT(t_a[:], t_a[:], t_b[:], ALU.subtract, ["t_a", "t_b"], ["t_a"])
                TT(cfi[:], t_a[:], den[:], ALU.mult, ["t_a", "den"], ["cfi"])
                cbr, cbi, tmp9 = A("cbr", [128, 16, 8]), A("cbi", [128, 16, 8]), A("tmp9", [128, 16, 8])
                S8 = [128, 16, 8]
                TT(cbr[:], pwr[:, :, 0:8], bc(cfr[:], S8, 2), ALU.mult, ["pwr", "cfr"], ["cbr"])
                TT(tmp9[:], pwi[:, :, 0:8], bc(cfi[:], S8, 2), ALU.mult, ["pwi", "cfi"], ["tmp9"])
                TT(cbr[:], cbr[:], tmp9[:], ALU.subtract, ["cbr", "tmp9"], ["cbr"])
                TT(cbi[:], pwi[:, :, 0:8], bc(cfr[:], S8, 2), ALU.mult, ["pwi", "cfr"], ["cbi"])
                TT(tmp9[:], pwr[:, :, 0:8], bc(cfi[:], S8, 2), ALU.mult, ["pwr", "cfi"], ["tmp9"])
                TT(cbi[:], cbi[:], tmp9[:], ALU.add, ["cbi", "tmp9"], ["cbi"])
                g0r, g0i = A("g0r", [128, 16]), A("g0i", [128, 16])
                TT(g0r[:], h0re[:], cs[:, :, 8], ALU.mult, ["h0re", "cs"], ["g0r"])
                TT(t_a[:], h0im[:], sn[:, :, 8], ALU.mult, ["h0im", "sn"], ["t_a"])
                TT(g0r[:], g0r[:], t_a[:], ALU.subtract, ["g0r", "t_a"], ["g0r"])
                TT(g0i[:], h0re[:], sn[:, :, 8], ALU.mult, ["h0re", "sn"], ["g0i"])
                TT(t_a[:], h0im[:], cs[:, :, 8], ALU.mult, ["h0im", "cs"], ["t_a"])
                TT(g0i[:], g0i[:], t_a[:], ALU.add, ["g0i", "t_a"], ["g0i"])
                fin = A("fin", [128, 2, 16, 2])
                S3 = [128, 8, 64]
                npo = 0
                for ct in range(2):
                  with ExitStack() as cst:
                    Ac = lambda name, shape, dt=F32: sb(name, shape, dt, cst)
                    WinAll = Ac("WinAll", [128, 64, 128], BF16)
                    WoutAll = Ac("WoutAll", [128, 8, 8, 2, 64], BF16)
                    KK = Ac("KK", [128, 16, 128], BF16)
                    dve(lambda e: e.memset(KK[:], 0.0), [], ["KK"])
                    with ExitStack() as pst:
                        Ap = lambda name, shape, dt=F32: sb(name, shape, dt, pst)
                        Bn = [Ap(f"Bn{p}", [128, 8, 16]) for p in range(2)]
                        Cn = [Ap(f"Cn{p}", [32, 8, 64]) for p in range(2)]
                        for d in range(2):
                            sl = slice(d * 4, (d + 1) * 4)
                            gs = slice(ct * 8, ct * 8 + 8)
                            for p, nm in enumerate(("ssm_b_re", "ssm_b_im")):
                                tr.dma("sp", Bn[p][:, sl, :], I[nm][l, d, gs].rearrange("(i q) p c -> (q p) i c", q=2), w=[("Bn", p)], slow=True)
                            for p, nm in enumerate(("ssm_c_re", "ssm_c_im")):
                                tr.dma("sp", Cn[p][:, sl, :], I[nm][l, d, gs].rearrange("(i q) h p -> (q h) i p", q=2), w=[("Cn", p)], slow=True)
                        BBD = [Ap(f"BBD{p}", [128, 8, 64]) for p in range(2)]
                        Cpt = [Ap(f"Cpt{p}", [128, 8, 64]) for p in range(2)]
                        cbd = [Ap(f"cbd{i}", [32, 128]) for i in range(2)]
                        for p in range(2):
                            dve(lambda e, p=p: e.memset(BBD[p][:], 0.0), [], [("BBD", p)])
                            dve(lambda e, p=p: e.memset(Cpt[p][:], 0.0), [], [("Cpt", p)])
                            for ddl in range(8):
                                par = ddl % 2
                                c0 = par * 32
                                dve(lambda e, p=p, ddl=ddl, c0=c0: e.tensor_copy(out=BBD[p][0:64, ddl, c0:c0 + 16], in_=Bn[p][0:64, ddl, :]),
                                    [("Bn", p), ("BBD", p)], [("BBD", p)])
                                dve(lambda e, p=p, ddl=ddl, c0=c0: e.tensor_copy(out=BBD[p][64:128, ddl, c0 + 16:c0 + 32], in_=Bn[p][64:128, ddl, :]),
                                    [("Bn", p), ("BBD", p)], [("BBD", p)])
                            for ddl in range(8):
                                cb_, ck = cbd[ddl % 2], ("cbd", ddl % 2)
                                dve(lambda e, p=p, ddl=ddl, cb_=cb_: e.tensor_scalar_mul(out=cb_[:, 0:64], in0=Cn[p][:, ddl, :], scalar1=mskE[:, 0:1]),
                                    [("Cn", p), "mskE"], [ck])
                                dve(lambda e, p=p, ddl=ddl, cb_=cb_: e.tensor_scalar_mul(out=cb_[:, 64:128], in0=Cn[p][:, ddl, :], scalar1=mskO[:, 0:1]),
                                    [("Cn", p), "mskO", ck], [ck])
                                mm(PS[p][:, ddl * 32:(ddl + 1) * 32], cb_[:, :], ident[0:32, 0:32], True, True, [ck, "ident"], [("ps", p)])
                            for ddl in range(8):
                                c0 = (ddl % 2) * 32
                                fn = lambda e, p=p, ddl=ddl, c0=c0: e.activation(out=Cpt[p][:, ddl, c0:c0 + 32], in_=PS[p][:, ddl * 32:(ddl + 1) * 32],
                                                                               func=AF.Copy, scale=(1.0 if p == 0 else -1.0))
                                act(fn, [("ps", p), ("Cpt", p)], [("Cpt", p)])
                        VB = [Ap(f"VB{p}", [128, 4, 8, 64]) for p in range(2)]
                        vt = Ap("vt", [128, 8, 64])
                        vt2 = Ap("vt2", [128, 8, 64])
                        wo0, wo1 = Ap("wo0", [128, 8, 64]), Ap("wo1", [128, 8, 64])
                        TPp = lambda o, a, b_, op, r, w: pool(lambda e: e.tensor_tensor(out=o, in0=a, in1=b_, op=op), r, w)
                        nb_ = 0
                        for d in range(2):
                            for ip in range(4):
                                dd = d * 8 + ct * 4 + ip
                                ddl = d * 4 + ip
                                br, bi = bc(BBD[0][:, ddl, :], S3, 1), bc(BBD[1][:, ddl, :], S3, 1)
                                cr, ci = bc(cbr[:, dd, :], S3, 2), bc(cbi[:, dd, :], S3, 2)
                                rk = [("BBD", 0), ("BBD", 1), "cbr", "cbi"]
                                TT(VB[0][:, ip], br, cr, ALU.mult, rk, [("VB", 0, ip)])
                                TT(vt[:], bi, ci, ALU.mult, rk, ["vt"])
                                TT(VB[0][:, ip], VB[0][:, ip], vt[:], ALU.subtract, [("VB", 0, ip), "vt"], [("VB", 0, ip)])
                                TT(VB[1][:, ip], br, ci, ALU.mult, rk, [("VB", 1, ip)])
                                TT(vt[:], bi, cr, ALU.mult, rk, ["vt"])
                                TT(VB[1][:, ip], VB[1][:, ip], vt[:], ALU.add, [("VB", 1, ip), "vt"], [("VB", 1, ip)])
                                Cr, nCi = bc(Cpt[0][:, ddl, :], S3, 1), bc(Cpt[1][:, ddl, :], S3, 1)
                                pr, pi_ = bc(pwr[:, dd, 1:9], S3, 2), bc(pwi[:, dd, 1:9], S3, 2)
                                rk2 = [("Cpt", 0), ("Cpt", 1), "pwr", "pwi"]
                                TPp(wo0[:], Cr, pr, ALU.mult, rk2, ["wo0"])
                                TPp(vt2[:], nCi, pi_, ALU.mult, rk2, ["vt2"])
                                TPp(WoutAll[:, ddl, :, 0, :], wo0[:], vt2[:], ALU.add, ["wo0", "vt2"], [("Wout", ddl)])
                                TPp(wo1[:], nCi, pr, ALU.mult, rk2, ["wo1"])
                                TPp(vt2[:], Cr, pi_, ALU.mult, rk2, ["vt2"])
                                TPp(WoutAll[:, ddl, :, 1, :], wo1[:], vt2[:], ALU.subtract, ["wo1", "vt2"], [("Wout", ddl)])
                            for e_ in range(8):
                                ps, pk = PS[2 + nb_ % 4], ("ps", 2 + nb_ % 4)
                                nb_ += 1
                                for part in range(2):
                                    for ip in range(4):
                                        u, par = ip // 2, ip % 2
                                        rg = part * 2 + par
                                        pe(lambda e, ps=ps, rg=rg, ip=ip, u=u, part=part, e_=e_: e.matmul(
                                            ps[u * 64:(u + 1) * 64, rg * 128:(rg + 1) * 128], lhsT=VB[part][:, ip, e_, :],
                                            rhs=ident[:, :], start=True, stop=True), [("VB", part, ip), "ident"], [pk])
                                r0 = (d * 8 + e_) * 4
                                act(lambda e, ps=ps, r0=r0: e.activation(out=WinAll[:, r0:r0 + 4, :].rearrange("p a b -> p (a b)"), in_=ps[:, :], func=AF.Copy),
                                    [pk], ["WinAll"])
                            for i4 in range(2):
                                ps, pk = PS[2 + nb_ % 4], ("ps", 2 + nb_ % 4)
                                nb_ += 1
                                for ii in range(4):
                                    e_ = i4 * 4 + ii
                                    for u in range(2):
                                        n_t = 0
                                        for par in range(2):
                                            ip = u * 2 + par
                                            ddl = d * 4 + ip
                                            for part in range(2):
                                                n_t += 1
                                                pe(lambda e, ps=ps, ii=ii, u=u, ip=ip, part=part, ddl=ddl, e_=e_, n_t=n_t: e.matmul(
                                                    ps[u * 64:(u + 1) * 64, ii * 128 + u * 64: ii * 128 + (u + 1) * 64],
                                                    lhsT=VB[part][:, ip, e_, :], rhs=Cpt[part][:, ddl, :], start=(n_t == 1), stop=(n_t == 4)),
                                                   [("VB", part, ip), ("Cpt", part)], [pk])
                                for u in range(2):
                                    src = ps[u * 64:(u + 1) * 64, :].rearrange("p (a b) -> p a b", b=128)[:, :, u * 64:(u + 1) * 64]
                                    s0 = d * 8 + i4 * 4
                                    dve(lambda e, src=src, u=u, s0=s0: e.tensor_copy(
                                        out=KK[u * 64:(u + 1) * 64, s0:s0 + 4, u * 64:(u + 1) * 64], in_=src), [pk, "KK"], ["KK"])
                        dve(lambda e: e.tensor_tensor(out=KK[:, 0, :], in0=KK[:, 0, :], in1=KK[:, 8, :], op=ALU.add), ["KK"], ["KK"])
                        dve(lambda e, ct=ct: e.scalar_tensor_tensor(out=KK[:, 0, :], in0=identb[:], scalar=ssd[:, l, ct:ct + 1],
                                                                   in1=KK[:, 0, :], op0=ALU.mult, op1=ALU.add), ["KK", "identb", "ssd"], ["KK"])
                    tr.barrier()
                    with ExitStack() as mst:
                        Am = lambda name, shape, dt=F32: sb(name, shape, dt, mst)
                        Hs = Am("Hs", [128, 4, 2, 2, NCOL], BF16)
                        cosT, sinT = Am("cosT", [128, NT_]), Am("sinT", [128, NT_])
                        kiw = Am("kiw", [128, NT_], I32)
                        Sre, Sim = Am("Sre", [128, NT_]), Am("Sim", [128, NT_])
                        Gr, Gi = Am("Gr", [128, NT_]), Am("Gi", [128, NT_])
                        u1, u2 = Am("u1", [128, NT_]), Am("u2", [128, NT_])
                        u3, u4 = Am("u3", [128, NT_]), Am("u4", [128, NT_])
                        kiw2 = Am("kiw2", [128, NT_], I32)
                        TP = lambda o, a, b_, op, r, w: pool(lambda e: e.tensor_tensor(out=o, in0=a, in1=b_, op=op), r, w)
                        CR = [(0, NBLK), (NBLK, NBLK + 32), (NBLK + 32, NBLK + 64)]
                        for ip in range(4):
                            u, par = ip // 2, ip % 2
                            for d in range(2):
                                dd = d * 8 + ct * 4 + ip
                                for (tab, off, key, E_, yw_, ki_, kf_, sfx) in ((sinT, 0.0, "sinT", dve, u1, kiw, u2, ("u1", "kiw", "u2")),
                                                                               (cosT, 0.25, "cosT", pool, u3, kiw2, u4, ("u3", "kiw2", "u4"))):
                                    E_(lambda e, off=off, dd=dd, yw_=yw_, d=d: e.tensor_scalar(out=yw_[:], in0=iota[:, d, :], scalar1=ysn[:, dd, 8:9], scalar2=off,
                                                                                             op0=ALU.mult, op1=ALU.add), ["iota", "ysn"], [sfx[0]])
                                    E_(lambda e, yw_=yw_, ki_=ki_: e.tensor_copy(out=ki_[:], in_=yw_[:]), [sfx[0]], [sfx[1]])
                                    E_(lambda e, kf_=kf_, ki_=ki_: e.tensor_copy(out=kf_[:], in_=ki_[:]), [sfx[1]], [sfx[2]])
                                    E_(lambda e, yw_=yw_, kf_=kf_: e.tensor_tensor(out=yw_[:], in0=yw_[:], in1=kf_[:], op=ALU.subtract), [sfx[0], sfx[2]], [sfx[0]])
                                    E_(lambda e, yw_=yw_: e.tensor_scalar(out=yw_[:], in0=yw_[:], scalar1=0.49999, scalar2=-0.49999, op0=ALU.min, op1=ALU.max), [sfx[0]], [sfx[0]])
                                    act(lambda e, tab=tab, yw_=yw_: e.activation(out=tab[:], in_=yw_[:], func=AF.Sin, scale=TWO_PI), [sfx[0]], [key])
                                for si, (off, nb, is_s, qi) in enumerate(SEQS):
                                    zk = [("za", ct, bb) for bb in range(off // T, (off + 8 * nb - 1) // T + 1)]
                                    for part in range(2):
                                        if is_s:
                                            dst, dkey = PS[part][:, 0:nb], ("ps", part)
                                        else:
                                            dst, dkey = PS[2 + part][:, qi * 32:qi * 32 + 32], ("ps", 2 + part)
                                        for s in range(8):
                                            e_ = 7 - s if d == 0 else s
                                            rg = ((d * 8 + e_) * 2 + part) * 2 + par
                                            mm(dst, WinAll[u * 64:(u + 1) * 64, rg, :],
                                               zaV[u * 64:(u + 1) * 64, ct, off + s:off + 8 * nb:8], s == 0, s == 7, ["WinAll"] + zk, [dkey])
                                act(lambda e: e.activation(out=Sre[:, 0:NBLK], in_=PS[0][:, 0:NBLK], func=AF.Copy), [("ps", 0)], ["Sre"])
                                act(lambda e: e.activation(out=Sim[:, 0:NBLK], in_=PS[1][:, 0:NBLK], func=AF.Copy), [("ps", 1)], ["Sim"])
                                act(lambda e: e.activation(out=Sre[:, NBLK:NBLK + 64], in_=PS[2][:, 0:64], func=AF.Copy), [("ps", 2), "Sre"], ["Sre"])
                                act(lambda e: e.activation(out=Sim[:, NBLK:NBLK + 64], in_=PS[3][:, 0:64], func=AF.Copy), [("ps", 3), "Sim"], ["Sim"])
                                TT(u1[:], Sre[:], cosT[:], ALU.mult, ["Sre", "cosT"], ["u1"])
                                TT(u2[:], Sim[:], sinT[:], ALU.mult, ["Sim", "sinT"], ["u2"])
                                TT(Gr[:], u1[:], u2[:], ALU.add, ["u1", "u2"], ["Gr"])
                                TP(u3[:], Sim[:], cosT[:], ALU.mult, ["Sim", "cosT"], ["u3"])
                                TP(u4[:], Sre[:], sinT[:], ALU.mult, ["Sre", "sinT"], ["u4"])
                                TP(Gi[:], u3[:], u4[:], ALU.subtract, ["u3", "u4"], ["Gi"])
                                for si, (off, nb, is_s, qi) in enumerate(SEQS):
                                    lo, hi_ = CR[si]
                                    rv = slice(lo, hi_) if d == 0 else slice(hi_ - 1, (lo - 1) if lo > 0 else None, -1)
                                    r8 = mag[:, dd, 8:9].to_broadcast([128, nb])
                                    for (G, g0, gk) in ((Gr, g0r, "Gr"), (Gi, g0i, "Gi")):
                                        init = g0[:, dd:dd + 1] if is_s else 0.0
                                        dve(lambda e, G=G, init=init, r8=r8, rv=rv: e.tensor_tensor_scan(
                                            out=G[:, rv], data0=r8, data1=G[:, rv], initial=init, op0=ALU.mult, op1=ALU.add),
                                            [gk, "mag", "g0r", "g0i"], [gk])
                                hk = ("Hs", ip, d)
                                TT(u1[:], Gr[:], cosT[:], ALU.mult, ["Gr", "cosT"], ["u1"])
                                TT(u2[:], Gi[:], sinT[:], ALU.mult, ["Gi", "sinT"], ["u2"])
                                TP(u3[:], Gr[:], sinT[:], ALU.mult, ["Gr", "sinT"], ["u3"])
                                TP(u4[:], Gi[:], cosT[:], ALU.mult, ["Gi", "cosT"], ["u4"])
                                for si, (off, nb, is_s, qi) in enumerate(SEQS):
                                    base = HB[si]
                                    lo, hi_ = CR[si]
                                    c0 = base + 1 if d == 0 else base
                                    TT(Hs[:, ip, d, 0, c0:c0 + nb], u1[:, lo:hi_], u2[:, lo:hi_], ALU.subtract, ["u1", "u2"], [hk])
                                    TP(Hs[:, ip, d, 1, c0:c0 + nb], u3[:, lo:hi_], u4[:, lo:hi_], ALU.add, ["u3", "u4"], [hk])
                                    if not is_s:
                                        fc = hi_ - 1 if d == 0 else lo
                                        TT(fin[:, qi, dd, 0:1], u1[:, fc:fc + 1], u2[:, fc:fc + 1], ALU.subtract, ["u1", "u2"], ["fin"])
                                        TP(fin[:, qi, dd, 1:2], u3[:, fc:fc + 1], u4[:, fc:fc + 1], ALU.add, ["u3", "u4"], ["fin"])
                                    cb0 = base if d == 0 else base + nb
                                    if is_s:
                                        dve(lambda e, cb0=cb0, ip=ip, d=d, dd=dd: e.tensor_copy(out=Hs[:, ip, d, 0, cb0:cb0 + 1], in_=h0re[:, dd:dd + 1]), ["h0re"], [hk])
                                        dve(lambda e, cb0=cb0, ip=ip, d=d, dd=dd: e.tensor_copy(out=Hs[:, ip, d, 1, cb0:cb0 + 1], in_=h0im[:, dd:dd + 1]), ["h0im"], [hk])
                                    else:
                                        dve(lambda e, cb0=cb0, ip=ip, d=d: e.memset(Hs[:, ip, d, :, cb0:cb0 + 1], 0.0), [], [hk])
                        for si, (off, nb, is_s, qi) in enumerate(SEQS):
                            base = HB[si]
                            zk = [("za", ct, bb) for bb in range(off // T, (off + 8 * nb - 1) // T + 1)]
                            for s in range(8):
                                ps, pk = PS[2 + npo % 4], ("ps", 2 + npo % 4)
                                npo += 1
                                for s2 in range(8):
                                    mm(ps[:, 0:nb], KK[:, (s - s2) if s >= s2 else 8 + (s2 - s), :], zaV[:, ct, off + s2:off + 8 * nb:8], s2 == 0, False, ["KK"] + zk, [pk])
                                n_t = 0
                                for ip in range(4):
                                    u = ip // 2
                                    for d in range(2):
                                        ddl = d * 4 + ip
                                        e_ = s + 1 if d == 0 else 8 - s
                                        c0 = base if d == 0 else base + 1
                                        for part in range(2):
                                            n_t += 1
                                            mm(ps[u * 64:(u + 1) * 64, 0:nb], WoutAll[:, ddl, e_ - 1, part, :], Hs[:, ip, d, part, c0:c0 + nb],
                                               False, n_t in (8, 16), [("Wout", ddl), ("Hs", ip, d)], [pk])
                                act(lambda e, ps=ps, nb=nb, off=off, s=s, ct=ct: e.activation(
                                    out=ya[:, ct, off + s:off + 8 * nb:8], in_=ps[:, 0:nb], func=AF.Gelu_apprx_tanh),
                                    [pk], [("ya", ct, bb) for bb in range(off // T, (off + 8 * nb - 1) // T + 1)])
                    tr.barrier()
                for qi in range(2):
                    for d in range(2):
                        for part, nm in enumerate(("nre", "nim")):
                            tr.dma("sp", O[nm][qi, l, d].rearrange("(i q) p -> (q p) i", q=2), fin[:, qi, d * 8:(d + 1) * 8, part],
                                   r=["fin"], slow=True)

        def fnet_pass(l, zbV, zc, yb):
            wmi = I["w_mix_in"][l].rearrange("(k p) n -> p k n", p=128)
            NTs = Ls // 128
            with ExitStack() as st:
                A = lambda name, shape, dt=F32: sb(name, shape, dt, st)
                h = A("hmix2", [128, KC, T], BF16)
                wbc = A("wbc", [128, KC, 512], BF16)
                tr.dma("pool", wbc[:], wmi[:, :, 256:768], w=["wbc"])
                zck = [("zc", b) for b in range(NB)]
                dve(lambda e: e.memset(zc[:], 0.0), [], zck)
                for b in range(NB):
                    bg_issue(2)
                    make_h(h, b * T, T, l, 1)
                    hk = [("h", k) for k in range(KC)]
                    for q in range(4):
                        i = b * 4 + q
                        ps, pk = PS[q % 2], ("ps", q % 2)
                        for k in range(KC):
                            mm(ps[:, 0:256], h[:, k, q * 128:(q + 1) * 128], wbc[:, k, 0:256], k == 0, k == KC - 1, ["wbc", ("h", k)], [pk])
                        if q % 2 == 0:
                            act(lambda e, ps=ps, i=i: e.activation(out=zbV[:, i, :], in_=ps[:, 0:256], func=AF.Copy), [pk], [("zb", i)])
                        else:
                            dve(lambda e, ps=ps, i=i: e.tensor_copy(out=zbV[:, i, :], in_=ps[:, 0:256]), [pk], [("zb", i)])
                    for ct in range(2):
                        ps, pk = PS[2 + ct], ("ps", 2 + ct)
                        for k in range(KC):
                            mm(ps[:, :], wbc[:, k, 256 + ct * 128:256 + (ct + 1) * 128], h[:, k, :], k == 0, k == KC - 1, ["wbc", ("h", k)], [pk])
                        if b < NBS:
                            act(lambda e, ps=ps, b=b, ct=ct: e.activation(out=zc[:, ct, 8 + b * T:8 + (b + 1) * T], in_=ps[:, :], func=AF.Copy),
                                [pk], [("zc", b)])
                        else:
                            act(lambda e, ps=ps, ct=ct: e.activation(out=zc[:, ct, POFF[1]:POFF[1] + Lp], in_=ps[:, 0:Lp], func=AF.Copy), [pk], [("zc", b)])
                            dve(lambda e, ps=ps, ct=ct: e.tensor_copy(out=zc[:, ct, POFF[2]:POFF[2] + Lp], in_=ps[:, Lp:2 * Lp]), [pk], [("zc", b)])
                FW = A("FW", [128, 2, 128])
                c64t, s64t = A("c64t", [128, 128]), A("s64t", [128, 128])
                Wcs = A("Wcs", [128, 2, 3, 2, 128], BF16)
                dve(lambda e: e.memset(FW[:], 0.0), [], ["FW"])
                tr.dma("sp", c64t[:], I["c64"], w=["c64t"])
                tr.dma("sp", s64t[:], I["s64"], w=["s64t"])
                for g in range(4):
                    p0 = (g % 2) * 64
                    tr.dma("sp", FW[p0:p0 + 64, g // 2, p0:p0 + 64], I["fnet_w"][l, g], r=["FW"], w=["FW"])
                for ct in range(2):
                    mm(PS[4][:, ct * 128:(ct + 1) * 128], c64t[:, :], FW[:, ct, :], True, True, ["c64t", "FW"], [("ps", 4)])
                    mm(PS[5][:, ct * 128:(ct + 1) * 128], s64t[:, :], FW[:, ct, :], True, True, ["s64t", "FW"], [("ps", 5)])
                for Li, L_ in enumerate((Ls, Lp)):
                    nrm = 1.0 / math.sqrt(64.0 * L_)
                    act(lambda e, Li=Li, nrm=nrm: e.activation(out=Wcs[:, Li, 0].rearrange("p a b -> p (a b)"), in_=PS[4][:, 0:256], func=AF.Copy, scale=nrm),
                        [("ps", 4)], ["Wcs"])
                    act(lambda e, Li=Li, nrm=nrm: e.activation(out=Wcs[:, Li, 1].rearrange("p a b -> p (a b)"), in_=PS[5][:, 0:256], func=AF.Copy, scale=-nrm),
                        [("ps", 5)], ["Wcs"])
                    act(lambda e, Li=Li, nrm=nrm: e.activation(out=Wcs[:, Li, 2].rearrange("p a b -> p (a b)"), in_=PS[5][:, 0:256], func=AF.Copy, scale=nrm),
                        [("ps", 5)], ["Wcs"])
                dsl = [A(f"dft{i}", [128, 2, 2, T], BF16) for i in range(4)]
                NP_ = 256
                PWBD = A("PWBD", [128, 2, 128], BF16)
                Ein = A("Ein", [128, 2, NP_ + 16], BF16)
                hsv = A("hsv", [128, 2, 8], BF16)
                sA, sB = A("sA", [128, 2, NP_ + 16]), A("sB", [128, 2, NP_ + 16])
                pw = A("pw", [128, 2, NP_])
                pb = A("pb", [128, 2, NP_], BF16)
                et = A("et", [128, 2, 8])
                dve(lambda e: e.memset(PWBD[:], 0.0), [], ["PWBD"])
                for g in range(4):
                    p0 = (g % 2) * 64
                    tr.dma("pool", PWBD[p0:p0 + 64, g // 2, p0:p0 + 64], I["pool_w"][l, g], r=["PWBD"], w=["PWBD"])

                def pool_seg(t0):
                    n = NP_
                    b = t0 // T
                    if t0 < Ls:
                        col0, left, right = 8 + t0, t0 == 0, t0 + n == Ls
                    else:
                        col0, left, right = POFF[1 + (t0 - Ls) // Lp], True, True
                    zk = [("zc", bb) for bb in range(max(0, b - 1), min(NB, b + 2))]
                    if left:
                        pool(lambda e: e.tensor_copy(out=Ein[:, :, :], in_=zc[:, :, col0 - 8:col0 + n + 8]), zk, ["Ein"])
                    else:
                        pool(lambda e: e.tensor_copy(out=Ein[:, :, 0:8], in_=hsv[:, :, :]), ["hsv"], ["Ein"])
                        pool(lambda e: e.tensor_copy(out=Ein[:, :, 8:n + 16], in_=zc[:, :, col0:col0 + n + 8]), zk + ["Ein"], ["Ein"])
                    if not right:
                        pool(lambda e: e.tensor_copy(out=hsv[:, :, :], in_=zc[:, :, col0 + n - 8:col0 + n]), zk, ["hsv"])
                    E = Ein
                    dve(lambda e: e.tensor_tensor(out=sA[:, :, 1:n + 16], in0=E[:, :, 0:n + 15], in1=E[:, :, 1:n + 16], op=ALU.add), ["Ein"], ["sA"])
                    for c in range(2):
                        dve(lambda e, c=c: e.scalar_tensor_tensor(out=pw[:, c, :], in0=sA[:, c, 8:n + 8], scalar=poolm[:, c, 0:1], in1=E[:, c, 8:n + 8],
                                                                 op0=ALU.mult, op1=ALU.subtract), ["sA", "poolm", "Ein"], [("pw", c)])
                    dve(lambda e: e.tensor_tensor(out=sB[:, :, 2:n + 15], in0=sA[:, :, 1:n + 14], in1=sA[:, :, 3:n + 16], op=ALU.add), ["sA"], ["sB"])
                    dve(lambda e: e.scalar_tensor_tensor(out=pw[:, 0, :], in0=sB[:, 0, 8:n + 8], scalar=poolm[:, 0, 1:2], in1=pw[:, 0, :],
                                                         op0=ALU.mult, op1=ALU.add), ["sB", "poolm", ("pw", 0)], [("pw", 0)])
                    dve(lambda e: e.tensor_tensor(out=sA[:, 1, 4:n + 13], in0=sB[:, 1, 2:n + 11], in1=sB[:, 1, 6:n + 15], op=ALU.add), ["sB", "sA"], ["sA"])
                    dve(lambda e: e.scalar_tensor_tensor(out=pw[:, 1, :], in0=sA[:, 1, 8:n + 8], scalar=poolm[:, 1, 2:3], in1=pw[:, 1, :],
                                                         op0=ALU.mult, op1=ALU.add), ["sA", "poolm", ("pw", 1)], [("pw", 1)])
                    dve(lambda e: e.tensor_tensor(out=sB[:, 1, 8:n + 8], in0=sA[:, 1, 4:n + 4], in1=sA[:, 1, 12:n + 12], op=ALU.add), ["sA", "sB"], ["sB"])
                    dve(lambda e: e.scalar_tensor_tensor(out=pw[:, 1, :], in0=sB[:, 1, 8:n + 8], scalar=poolm[:, 1, 3:4], in1=pw[:, 1, :],
                                                         op0=ALU.mult, op1=ALU.add), ["sB", "poolm", ("pw", 1)], [("pw", 1)])
                    for (flag, cs_, corr, ck) in ((left, 0, corrL, "corrL"), (right, n - 8, corrR, "corrR")):
                        if not flag:
                            continue
                        zz = E[:, :, 8 + cs_:16 + cs_]
                        pk2 = [("pw", 0), ("pw", 1)]
                        dve(lambda e, cs_=cs_, zz=zz: e.tensor_tensor(out=et[:], in0=pw[:, :, cs_:cs_ + 8], in1=zz, op=ALU.add), pk2 + ["Ein"], ["et"])
                        dve(lambda e, corr=corr: e.tensor_tensor(out=et[:], in0=et[:], in1=corr[:], op=ALU.mult), ["et", ck], ["et"])
                        dve(lambda e, cs_=cs_, zz=zz: e.tensor_tensor(out=pw[:, :, cs_:cs_ + 8], in0=et[:], in1=zz, op=ALU.subtract), ["et", "Ein"] + pk2, pk2)
                    pool(lambda e: e.tensor_copy(out=pb[:, :, :], in_=pw[:, :, :]), [("pw", 0), ("pw", 1)], ["pb"])
                    for ct in range(2):
                        mm(PS[4 + ct][:, 0:n], PWBD[:, ct, :], pb[:, ct, :], True, True, ["PWBD", "pb"], [("ps", 4 + ct)])
                        pool_evac(ct, col0, n, b)

                def pool_evac(ct, col0, n, b):
                    dve(lambda e: e.tensor_scalar_mul(out=zc[:, ct, col0:col0 + n], in0=PS[4 + ct][:, 0:n], scalar1=psc[:, l, ct:ct + 1]),
                        [("ps", 4 + ct), "psc", "Ein", "hsv"], [("zc", b)])

                PSEGS = list(range(0, Ltot, NP_))
                dP = A("dftP", [128, 2, 2, Lp], BF16)
                Pb, Qb = A("Pb", [128, 2, T], BF16), A("Qb", [128, 2, T], BF16)
                tr.dma("sp", dP[:].rearrange("p j c k -> p j (c k)"), I["dftP"].rearrange("j p c k -> p j (c k)"), w=["dP"])
                nd = 0
                SYM = NKB >= 2 and NKB % 2 == 0
                NKH = NKB // 2 if SYM else NKB
                if SYM:
                    alt = A("alt", [128, 2], BF16)
                    tr.dma("sp", alt[:, 0:1], I["dftS"][NKB // 2, 0, :, 0, 0:1], w=["alt"], slow=True)
                    for i in range(NTs):
                        for ct in range(2):
                            mm(PS[ct][:, 0:1], zbV[:, i, ct * 128:(ct + 1) * 128], alt[:, 0:1], i == 0, i == NTs - 1, [("zb", i), "alt"], [("ps", ct)])
                    for ct in range(2):
                        act(lambda e, ct=ct: e.activation(out=Pb[:, ct, 0:1], in_=PS[ct][:, 0:1], func=AF.Copy), [("ps", ct)], [("Pb", ct)])
                        mm(PS[4 + ct][:, 0:1], Wcs[:, 0, 0, ct, :], Pb[:, ct, 0:1], True, True, ["Wcs", ("Pb", ct)], [("ps", 4 + ct)])
                        act(lambda e, ct=ct: e.activation(out=yb[:, ct, Ls // 2:Ls // 2 + 1], in_=PS[4 + ct][:, 0:1], func=AF.Copy), [("ps", 4 + ct)], [("yb", ct, NKB // 2)])

                for kb in list(range(NKH)) + [NKB, NKB + 1]:
                    if kb < NKB:
                        n, Li = T, 0
                        for ig in range(NTs // 2):
                            sl_, dk = dsl[nd % 4], ("dft", nd % 4)
                            nd += 1
                            tr.dma("sp", sl_[:].rearrange("p i c k -> p i (c k)"),
                                   I["dftS"][kb, ig * 2:(ig + 1) * 2].rearrange("i p c k -> p i (c k)"), w=[dk])
                            if ig % 4 == 3 and PSEGS:
                                pool_seg(PSEGS.pop(0))
                            for i4 in range(2):
                                i = ig * 2 + i4
                                for ct in range(2):
                                    mm(PS[ct][:, :], zbV[:, i, ct * 128:(ct + 1) * 128], sl_[:, i4, 0, :], i == 0, i == NTs - 1, [("zb", i), dk], [("ps", ct)])
                                    mm(PS[2 + ct][:, :], zbV[:, i, ct * 128:(ct + 1) * 128], sl_[:, i4, 1, :], i == 0, i == NTs - 1, [("zb", i), dk], [("ps", 2 + ct)])
                        c0 = kb * T
                        dkeys = lambda ct: [("yb", ct, kb)]
                    else:
                        qi = kb - NKB
                        n, Li = Lp, 1
                        for j in range(2):
                            i = NTs + 2 * qi + j
                            for ct in range(2):
                                mm(PS[ct][:, 0:n], zbV[:, i, ct * 128:(ct + 1) * 128], dP[:, j, 0, :], j == 0, j == 1, [("zb", i), "dP"], [("ps", ct)])
                                mm(PS[2 + ct][:, 0:n], zbV[:, i, ct * 128:(ct + 1) * 128], dP[:, j, 1, :], j == 0, j == 1, [("zb", i), "dP"], [("ps", 2 + ct)])
                        c0 = Ls + qi * Lp
                        dkeys = lambda ct: [("yb", ct, NBS)]
                        while qi == 1 and PSEGS:
                            pool_seg(PSEGS.pop(0))
                    for ct in range(2):
                        act(lambda e, ct=ct, n=n: e.activation(out=Pb[:, ct, 0:n], in_=PS[ct][:, 0:n], func=AF.Copy), [("ps", ct)], [("Pb", ct)])
                        dve(lambda e, ct=ct, n=n: e.tensor_copy(out=Qb[:, ct, 0:n], in_=PS[2 + ct][:, 0:n]), [("ps", 2 + ct)], [("Qb", ct)])
                    for ct in range(2):
                        mm(PS[4 + ct][:, 0:n], Wcs[:, Li, 0, ct, :], Pb[:, ct, 0:n], True, False, ["Wcs", ("Pb", ct)], [("ps", 4 + ct)])
                        mm(PS[4 + ct][:, 0:n], Wcs[:, Li, 1, ct, :], Qb[:, ct, 0:n], False, True, ["Wcs", ("Qb", ct)], [("ps", 4 + ct)])
                        if ct == 0:
                            act(lambda e, ct=ct, n=n, c0=c0: e.activation(out=yb[:, ct, c0:c0 + n], in_=PS[4 + ct][:, 0:n], func=AF.Copy), [("ps", 4 + ct)], dkeys(ct))
                        else:
                            dve(lambda e, ct=ct, n=n, c0=c0: e.tensor_copy(out=yb[:, ct, c0:c0 + n], in_=PS[4 + ct][:, 0:n]), [("ps", 4 + ct)], dkeys(ct))
                        if SYM and kb < NKB:
                            mm(PS[6 + ct][:, 0:n], Wcs[:, Li, 0, ct, :], Pb[:, ct, 0:n], True, False, ["Wcs", ("Pb", ct)], [("ps", 6 + ct)])
                            mm(PS[6 + ct][:, 0:n], Wcs[:, Li, 2, ct, :], Qb[:, ct, 0:n], False, True, ["Wcs", ("Qb", ct)], [("ps", 6 + ct)])
                            j0 = 1 if kb == 0 else 0
                            hi = Ls - T * kb - j0
                            dsl_ = slice(hi, Ls - T * kb - T, -1)
                            mk = [("yb", ct, NKB - 1 - kb)] + ([("yb", ct, NKB - kb)] if kb >= 1 else [])
                            if ct == 0:
                                dve(lambda e, ct=ct, j0=j0, dsl_=dsl_: e.tensor_copy(out=yb[:, ct, dsl_], in_=PS[6 + ct][:, j0:T]), [("ps", 6 + ct)], mk)
                            else:
                                act(lambda e, ct=ct, j0=j0, dsl_=dsl_: e.activation(out=yb[:, ct, dsl_], in_=PS[6 + ct][:, j0:T], func=AF.Copy), [("ps", 6 + ct)], mk)

        def pass3(l, B1, ya, yb, zc):
            TB = 256
            wmi = I["w_mix_in"][l].rearrange("(k p) n -> p k n", p=128)
            wmo = I["w_mix_out"][l].rearrange("(k p) n -> p k n", p=128)
            with ExitStack() as st:
                A = lambda name, shape, dt=F32: sb(name, shape, dt, st)
                cv = [0]

                def carve(nm, n):
                    if cv[0] + n <= 2 * Ltot:
                        ap = B1[:, cv[0]:cv[0] + n]
                        cv[0] += n
                        return ap
                    return A(nm, [128, n], BF16)[:]

                wd = carve("wd", KC * 512).rearrange("p (k n) -> p k n", n=512)
                h = carve("h3", KC * TB).rearrange("p (k n) -> p k n", n=TB)
                yn0 = carve("yn", KC * TB).rearrange("p (k n) -> p k n", n=TB)
                yn1 = A("yn1", [128, KC, TB], BF16)
                YN = [yn0, yn1[:]]
                vn = carve("vn", 512)
                sqb = carve("sqb", 2 * TB).rearrange("p (k n) -> p k n", n=TB)
                ep = make_epi(st, TB)
                glw = A("glw", [128, 2, 256], BF16)
                SWT = A("SWT", [128, 4, 128], BF16)
                biasT = A("biasT", [128, 2, TB])
                wmall = A("wmall", [128, KC, D], BF16)
                ym = [A(f"ym{i}", [128, 2, TB]) for i in range(2)]
                sig = A("sig", [128, 2, TB])
                rs = A("rs", [128, TB])
                ug = A("ug", [128, 2, TB])
                vg, vsq = A("vg", [128, 512]), A("vsq", [128, 512])
                sm1, sm2, mn_, msq = A("sm1", [128, 8]), A("sm2", [128, 8]), A("mn_", [128, 8]), A("msq", [128, 8])
                tr.dma("pool", wd, wmi[:, :, 768:1280], w=["wd"])
                for k in range(KC):
                    tr.dma("pool", wmall[:, k, :], wmo[:, k, :], w=[("wmall", k)])
                tr.dma("pool", glw[:], I["ssm_glu_w"][l].rearrange("(c p) n -> p c n", p=128), w=["glw"])
                for g in range(4):
                    p0 = (g % 2) * 64
                    for rep in range(TB // 128):
                        tr.dma("sp", biasT[p0:p0 + 64, g // 2, rep * 128:(rep + 1) * 128],
                               I["sgu_b"][l, g:g + 1, :].to_broadcast([64, 128]), w=["biasT"], slow=True)
                SWn = vg[:].rearrange("p (g s) -> p g s", g=4)
                tr.dma("sp", SWn, I["sgu_w"][l].rearrange("g t s -> t g s"), w=["vg"])
                for g in range(4):
                    pe(lambda e, g=g: e.transpose(PS[0][:, g * 128:(g + 1) * 128], SWn[:, g, :], ident[:]), ["vg", "ident"], [("ps", 0)])
                act(lambda e: e.activation(out=SWT[:].rearrange("p a b -> p (a b)"), in_=PS[0][:, :], func=AF.Copy), [("ps", 0)], ["SWT"])
                nwm = 0

                def rms(m, srcs, rkeys, slot):
                    yn = YN[slot]
                    for ct in range(2):
                        act(lambda e, ct=ct: e.activation(out=sqb[:, ct, :], in_=srcs[ct], func=AF.Square), rkeys, [("sqb", ct)])
                    mm(PS[3][:, 0:TB], ones256[:], sqb[:, 0, :], True, False, ["ones256", ("sqb", 0)], [("ps", 3)])
                    mm(PS[3][:, 0:TB], ones256[:], sqb[:, 1, :], False, True, ["ones256", ("sqb", 1)], [("ps", 3)])
                    dve(lambda e: e.tensor_scalar_add(out=rs[:], in0=PS[3][:, 0:TB], scalar1=EPS), [("ps", 3)], ["rs"])
                    act(lambda e: e.activation(out=rs[:], in_=rs[:], func=AF.Ln), ["rs"], ["rs"])
                    act(lambda e: e.activation(out=rs[:], in_=rs[:], func=AF.Exp, scale=-0.5), ["rs"], ["rs"])
                    for ct in range(2):
                        dve(lambda e, ct=ct: e.scalar_tensor_tensor(out=yn[:, 2 * m + ct, :], in0=srcs[ct], scalar=mng[:, l, 2 * m + ct:2 * m + ct + 1],
                                                                   in1=rs[:], op0=ALU.mult, op1=ALU.mult), rkeys + ["rs", "mng"], [("yn", slot, 2 * m + ct)])

                def p3_front(hbk):
                    t0 = hbk * TB
                    n = TB
                    b = t0 // T
                    cond = 0 if t0 < Ls else 1
                    slot = hbk % 2
                    tsl = slice(t0, t0 + n)
                    bg_issue(1)
                    make_h(h, t0, n, l, 1)
                    y0 = ym[0]
                    for dt_ in range(2):
                        for ct in range(2):
                            mm(PS[0][:, dt_ * n:(dt_ + 1) * n], glw[:, ct, dt_ * 128:(dt_ + 1) * 128], ya[:, ct, tsl], ct == 0, ct == 1, ["glw", ("ya", ct, b)], [("ps", 0)])
                    for ct in range(2):
                        for k in range(KC):
                            mm(PS[1][:, ct * n:(ct + 1) * n], wd[:, k, ct * 128:(ct + 1) * 128], h[:, k, :], k == 0, k == KC - 1, ["wd", ("h", k)], [("ps", 1)])
                    for qq in range(2):
                        for k in range(KC):
                            mm(PS[2][:, qq * 256:(qq + 1) * 256], h[:, k, qq * 128:(qq + 1) * 128], wd[:, k, 256:512], k == 0, k == KC - 1, ["wd", ("h", k)], [("ps", 2)])
                    for dt_ in range(2):
                        act(lambda e, dt_=dt_: e.activation(out=sig[:, dt_, :], in_=PS[0][:, dt_ * n:(dt_ + 1) * n], func=AF.Sigmoid, bias=glb[:, l, dt_:dt_ + 1], scale=1.0),
                            [("ps", 0), "glb"], [("sig", dt_)])
                    for ct in range(2):
                        act(lambda e, ct=ct: e.activation(out=ug[:, ct, :], in_=PS[1][:, ct * n:(ct + 1) * n], func=AF.Gelu_apprx_tanh), [("ps", 1)], [("ug", ct)])
                    act(lambda e: e.activation(out=vg[:], in_=PS[2][:, :], func=AF.Gelu_apprx_tanh), [("ps", 2)], ["vg"])
                    act(lambda e: e.activation(out=vsq[:], in_=vg[:], func=AF.Square), ["vg"], ["vsq"])
                    yield
                    for dt_ in range(2):
                        dve(lambda e, dt_=dt_: e.tensor_tensor(out=y0[:, dt_, :], in0=ya[:, dt_, tsl], in1=sig[:, dt_, :], op=ALU.mult),
                            [("ya", dt_, b), ("sig", dt_)], [("ym", 0)])
                    vg3, vsq3 = vg[:].rearrange("p (g c) -> p g c", c=64), vsq[:].rearrange("p (g c) -> p g c", c=64)
                    dve(lambda e: e.tensor_reduce(out=sm1[:], in_=vg3, axis=AX.X, op=ALU.add), ["vg"], ["sm1"])
                    dve(lambda e: e.tensor_reduce(out=sm2[:], in_=vsq3, axis=AX.X, op=ALU.add), ["vsq"], ["sm2"])
                    dve(lambda e: e.tensor_scalar_mul(out=mn_[:], in0=sm1[:], scalar1=1.0 / 64), ["sm1"], ["mn_"])
                    dve(lambda e: e.tensor_tensor(out=msq[:], in0=mn_[:], in1=mn_[:], op=ALU.mult), ["mn_"], ["msq"])
                    dve(lambda e: e.scalar_tensor_tensor(out=sm2[:], in0=sm2[:], scalar=1.0 / 64, in1=msq[:], op0=ALU.mult, op1=ALU.subtract), ["sm2", "msq"], ["sm2"])
                    dve(lambda e: e.tensor_scalar_add(out=sm2[:], in0=sm2[:], scalar1=EPS), ["sm2"], ["sm2"])
                    act(lambda e: e.activation(out=sm2[:], in_=sm2[:], func=AF.Ln), ["sm2"], ["sm2"])
                    act(lambda e: e.activation(out=sm2[:], in_=sm2[:], func=AF.Exp, scale=-0.5), ["sm2"], ["sm2"])
                    yield
                    rms(0, [y0[:, 0, :], y0[:, 1, :]], [("ym", 0)], slot)
                    yield
                    dve(lambda e: e.tensor_tensor(out=vg3, in0=vg3, in1=bc(mn_[:], [128, 8, 64], 2), op=ALU.subtract), ["vg", "mn_"], ["vg"])
                    dve(lambda e: e.tensor_tensor(out=vn.rearrange("p (g c) -> p g c", c=64), in0=vg3, in1=bc(sm2[:], [128, 8, 64], 2), op=ALU.mult),
                        ["vg", "sm2"], ["vn"])
                    for qq in range(2):
                        for g in range(4):
                            p0 = (g % 2) * 64
                            mm(PS[0][p0:p0 + 64, (g // 2) * n + qq * 128:(g // 2) * n + (qq + 1) * 128], vn[:, qq * 256 + g * 64:qq * 256 + (g + 1) * 64], SWT[:, g, :],
                               True, True, ["vn", "SWT"], [("ps", 0)])
                    yield
                    rms(1, [yb[:, 0, tsl], yb[:, 1, tsl]], [("yb", 0, b), ("yb", 1, b)], slot)
                    yield
                    if t0 < Ls:
                        col0 = 8 + t0
                    else:
                        col0 = POFF[1 + (t0 - Ls) // Lp]
                    rms(2, [zc[:, 0, col0:col0 + n], zc[:, 1, col0:col0 + n]], [("zc", b)], slot)
                    yield
                    y3 = ym[1]
                    for ct in range(2):
                        dve(lambda e, ct=ct: e.tensor_tensor(out=y3[:, ct, :], in0=PS[0][:, ct * n:(ct + 1) * n], in1=biasT[:, ct, :], op=ALU.add),
                            [("ps", 0), "biasT"], [("ym", 1)])
                        dve(lambda e, ct=ct: e.tensor_tensor(out=y3[:, ct, :], in0=y3[:, ct, :], in1=ug[:, ct, :], op=ALU.mult),
                            [("ym", 1), ("ug", ct)], [("ym", 1)])
                    rms(3, [y3[:, 0, :], y3[:, 1, :]], [("ym", 1)], slot)

                def p3_back(hbk):
                    t0 = hbk * TB
                    n = TB
                    b = t0 // T
                    cond = 0 if t0 < Ls else 1
                    slot = hbk % 2
                    for c in range(KC):
                        po, pk = PS[4 + c % 2], ("ps", 4 + c % 2)
                        for ic in range(KC):
                            mm(po[:, 0:n], wmall[:, ic, c * 128:(c + 1) * 128], YN[slot][:, ic, :], ic == 0, ic == KC - 1, [("wmall", ic), ("yn", slot, ic)], [pk])
                        epi_chunk(ep, c, po, pk, t0, n, l, 1, cond)
                        yield
                    epi_finish(ep, t0, n, l, 1)

                NHB = Ltot // TB
                for _ in p3_front(0):
                    pass
                for hbk in range(NHB):
                    gb = p3_back(hbk)
                    gf = p3_front(hbk + 1) if hbk + 1 < NHB else iter(())
                    fa, ba = True, True
                    while fa or ba:
                        if fa:
                            try:
                                next(gf)
                            except StopIteration:
                                fa = False
                        if ba:
                            try:
                                next(gb)
                            except StopIteration:
                                ba = False

        MIX = cfg.get("mixer", True)
        for l in range(depth):
            ffn(l, 0, False)
            if MIX:
                mixer(l)
            ffn(l, 1, l == depth - 1)
        tr.finish()
    return nc


def _consts(Ls):
    bf = ml_dtypes.bfloat16
    c = {}
    c["ident"] = np.eye(128, dtype=np.float32)
    k = np.arange(64)
    ang = 2 * np.pi * np.outer(k, k) / 64.0
    c64 = np.zeros((128, 128), np.float32)
    s64 = np.zeros((128, 128), np.float32)
    for g in range(2):
        c64[g * 64:(g + 1) * 64, g * 64:(g + 1) * 64] = np.cos(ang)
        s64[g * 64:(g + 1) * 64, g * 64:(g + 1) * 64] = np.sin(ang)
    c["c64"], c["s64"] = c64, s64
    nbk = Ls // 8
    fwd = np.concatenate([np.arange(nbk), np.arange(32), np.arange(32)]).astype(np.float32)
    rev = np.concatenate([np.arange(nbk)[::-1], np.arange(32)[::-1], np.arange(32)[::-1]]).astype(np.float32)
    c["iota"] = np.ascontiguousarray(np.broadcast_to(np.stack([fwd, rev])[None], (128, 2, nbk + 64))).astype(np.float32)
    quarter = D // 4
    omega = (1.0 / (10000.0 ** (np.arange(quarter, dtype=np.float32) / np.float32(quarter)))).astype(np.float32)
    rows = Ls // 64
    ang_r = (np.arange(rows, dtype=np.float32)[:, None] * omega).astype(np.float32)
    ang_c = (np.arange(64, dtype=np.float32)[:, None] * omega).astype(np.float32)
    emb_r = np.concatenate([np.sin(ang_r), np.cos(ang_r)], -1)
    emb_c = np.concatenate([np.sin(ang_c), np.cos(ang_c)], -1)
    pos = np.concatenate([np.broadcast_to(emb_r[:, None], (rows, 64, D // 2)),
                          np.broadcast_to(emb_c[None], (rows, 64, D // 2))], -1)
    c["pos"] = np.ascontiguousarray(pos.reshape(rows * 64, D).astype(np.float32))

    def dft(L, kblk):
        t = np.arange(L, dtype=np.int64)
        m = np.outer(t, t) % L
        a = 2 * np.pi * m.astype(np.float64) / L
        cs = np.stack([np.cos(a), np.sin(a)], 1)
        nk = L // kblk
        out = cs.reshape(L // 128, 128, 2, nk, kblk).transpose(3, 0, 1, 2, 4)
        return np.ascontiguousarray(out.astype(np.float32).astype(bf))

    c["dftS"] = dft(Ls, T)
    c["dftP"] = dft(256, 256)[0]
    return c


_WKEYS = ["w_ada", "b_ada", "ffn_w_in", "ffn_w_out", "w_mix_in", "w_mix_out", "mix_norm_g", "ssm_lam_re",
          "ssm_lam_im", "ssm_log_dt", "ssm_b_re", "ssm_b_im", "ssm_c_re", "ssm_c_im", "ssm_d", "ssm_glu_w",
          "ssm_glu_b", "fnet_w", "pool_w", "pool_scale", "sgu_w", "sgu_b", "ln_g", "ln_b"]


def run(inputs, cfg, ncores):
    Ls, depth = cfg["Ls"], cfg["depth"]
    nc = build(cfg)
    cst = _consts(Ls)
    f = lambda a: np.ascontiguousarray(np.asarray(a, dtype=np.float32))
    W = {k: f(inputs[k]) for k in _WKEYS}
    in_maps = []
    for c in range(ncores):
        m = dict(W)
        m.update(cst)
        m["xs"] = f(inputs["x_sample"][c])
        m["xp"] = f(np.asarray(inputs["x_prompt"])[2 * c:2 * c + 2].reshape(512, D))
        m["cvec"] = f(np.stack([np.asarray(inputs["c"])[c], np.asarray(inputs["c_ctx"])]))
        m["st_re"] = f(np.asarray(inputs["state_s5_re"])[c])
        m["st_im"] = f(np.asarray(inputs["state_s5_im"])[c])
        in_maps.append(m)
    res = run_bass_kernel_spmd(nc, in_maps, core_ids=list(range(ncores)))
    R = res.results
    ys = np.stack([np.asarray(r["ys"], np.float32) for r in R])
    yp = np.concatenate([np.asarray(r["yp"], np.float32).reshape(2, 256, D) for r in R])
    nre = np.concatenate([np.asarray(r["nre"], np.float32) for r in R])
    nim = np.concatenate([np.asarray(r["nim"], np.float32) for r in R])
    return yp, ys, nre, nim


def kernel(**inputs):
    return run(inputs, {"Ls": 4096, "depth": 4}, 8)
```
